# Optimizing a Trainium2 kernel written in Bass

```python
import math
import jax, jax.numpy as jnp
from jax import lax
import numpy as np

D_MODEL = 1024
BATCH = 8
SEQ = 2048
DEPTH = 4

HEAD_DIM = 64
N_HEADS_FOX = 8
N_HEADS_DIFF = 4
DIFF_V_DIM = 2 * HEAD_DIM
N_HEADS_DIL = 16
DILATED_CONFIGS = ((128, 1), (512, 4), (2048, 16))
ROT_DIM = HEAD_DIM // 4
ROPE_THETA = 500000.0
Q_BLOCK = 128
W_BLOCK = 128
NORM_EPS = 1e-6
WIDTH_FOX = N_HEADS_FOX * HEAD_DIM
DIFF_QK_WIDTH = N_HEADS_DIFF * 2 * HEAD_DIM
WIDTH_DIFF = N_HEADS_DIFF * DIFF_V_DIM
WIDTH_DIL = N_HEADS_DIL * HEAD_DIM
EVEN_IN = 4 * WIDTH_FOX + N_HEADS_FOX + 2 * DIFF_QK_WIDTH + 2 * WIDTH_DIFF
ODD_IN = 4 * WIDTH_DIL

kernel_name = 'hybrid_fox_diff_dilated_block'


def rms_norm(x, w):
    xf = x.astype(jnp.float32)
    var = jnp.mean(xf * xf, axis=-1, keepdims=True)
    return (xf * lax.rsqrt(var + NORM_EPS) * w.astype(jnp.float32)).astype(x.dtype)


def split_heads(t, n_heads, dh):
    b, s, _ = t.shape
    return t.reshape(b, s, n_heads, dh).transpose(0, 2, 1, 3)


def merge_heads(t):
    b, h, s, dh = t.shape
    return t.transpose(0, 2, 1, 3).reshape(b, s, h * dh)


def partial_rope(t, positions):
    inv_freq = 1.0 / (ROPE_THETA ** (jnp.arange(0, ROT_DIM, 2, dtype=jnp.float32) / ROT_DIM))
    ang = positions.astype(jnp.float32)[:, None, :, None] * inv_freq
    cos, sin = jnp.cos(ang), jnp.sin(ang)
    tr = t[..., :ROT_DIM].astype(jnp.float32)
    t1, t2 = tr[..., :ROT_DIM // 2], tr[..., ROT_DIM // 2:]
    rot = jnp.concatenate([t1 * cos - t2 * sin, t2 * cos + t1 * sin], axis=-1).astype(t.dtype)
    return jnp.concatenate([rot, t[..., ROT_DIM:]], axis=-1)


def fox_attention(q, k, v, log_f):
    b, h, s, dh = q.shape
    F = jnp.cumsum(log_f.astype(jnp.float32), axis=-1)
    scale = dh ** -0.5
    kpos = jnp.arange(s)

    def block(i):
        start = i * Q_BLOCK
        qb = lax.dynamic_slice_in_dim(q, start, Q_BLOCK, axis=2)
        Fq = lax.dynamic_slice_in_dim(F, start, Q_BLOCK, axis=2)
        sc = jnp.einsum('bhqd,bhkd->bhqk', qb, k, preferred_element_type=jnp.float32) * scale
        sc = sc + (Fq[..., :, None] - F[..., None, :])
        causal = kpos[None, :] <= (start + jnp.arange(Q_BLOCK))[:, None]
        p = jax.nn.softmax(jnp.where(causal, sc, -jnp.inf), axis=-1)
        return jnp.einsum('bhqk,bhkd->bhqd', p.astype(v.dtype), v)

    o = lax.map(block, jnp.arange(s // Q_BLOCK))
    return o.transpose(1, 2, 0, 3, 4).reshape(b, h, s, v.shape[-1])


def diff_attention(q1, q2, k1, k2, v, lam):
    b, h, s, dh = q1.shape
    scale = dh ** -0.5
    kpos = jnp.arange(s)

    def block(i):
        start = i * Q_BLOCK
        qb1 = lax.dynamic_slice_in_dim(q1, start, Q_BLOCK, axis=2)
        qb2 = lax.dynamic_slice_in_dim(q2, start, Q_BLOCK, axis=2)
        causal = kpos[None, :] <= (start + jnp.arange(Q_BLOCK))[:, None]
        s1 = jnp.einsum('bhqd,bhkd->bhqk', qb1, k1, preferred_element_type=jnp.float32) * scale
        s2 = jnp.einsum('bhqd,bhkd->bhqk', qb2, k2, preferred_element_type=jnp.float32) * scale
        p1 = jax.nn.softmax(jnp.where(causal, s1, -jnp.inf), axis=-1)
        p2 = jax.nn.softmax(jnp.where(causal, s2, -jnp.inf), axis=-1)
        p = p1 - lam * p2
        return jnp.einsum('bhqk,bhkd->bhqd', p.astype(v.dtype), v)

    o = lax.map(block, jnp.arange(s // Q_BLOCK))
    return o.transpose(1, 2, 0, 3, 4).reshape(b, h, s, v.shape[-1])


def dilated_window_branch(q, k, v, dilation, n_back):
    b, h, s, dh = q.shape
    L = s // dilation
    Lp = -(-L // W_BLOCK) * W_BLOCK
    nb = Lp // W_BLOCK

    def to_blocks(t):
        t = t.reshape(b, h, L, dilation, t.shape[-1]).transpose(0, 1, 3, 2, 4)
        t = jnp.pad(t, ((0, 0), (0, 0), (0, 0), (0, Lp - L), (0, 0)))
        return t.reshape(b, h, dilation, nb, W_BLOCK, t.shape[-1])

    def band(t):
        prev = jnp.pad(t[:, :, :, :-1], ((0, 0), (0, 0), (0, 0), (1, 0), (0, 0), (0, 0)))
        return jnp.concatenate([prev, t], axis=4)

    qb = to_blocks(q)
    kband = band(to_blocks(k))
    vband = band(to_blocks(v))
    sc = jnp.einsum('bhrnqd,bhrnkd->bhrnqk', qb, kband, preferred_element_type=jnp.float32) * dh ** -0.5
    qi = jnp.arange(W_BLOCK)
    ki = jnp.arange(2 * W_BLOCK) - W_BLOCK
    dist = qi[:, None] - ki[None, :]
    in_window = (dist >= 0) & (dist <= n_back)
    key_exists = (jnp.arange(nb)[:, None, None] * W_BLOCK + ki[None, None, :]) >= 0
    mask = in_window[None] & key_exists
    sc = jnp.where(mask, sc, -jnp.inf)
    m = jnp.max(sc, axis=-1, keepdims=True)
    e = jnp.exp(sc - m)
    l = jnp.sum(e, axis=-1, keepdims=True)
    o = jnp.einsum('bhrnqk,bhrnkd->bhrnqd', (e / l).astype(v.dtype), vband)

    def from_blocks(t):
        t = t.reshape(b, h, dilation, Lp, t.shape[-1])[:, :, :, :L]
        return t.transpose(0, 1, 3, 2, 4).reshape(b, h, s, t.shape[-1])

    return from_blocks(o), from_blocks(m), from_blocks(l)


def dilated_mixture(q, k, v):
    branches = [dilated_window_branch(q, k, v, d, w // d) for (w, d) in DILATED_CONFIGS]
    big_m = branches[0][1]
    for _, m_i, _ in branches[1:]:
        big_m = jnp.maximum(big_m, m_i)
    num = 0.0
    den = 0.0
    for o_i, m_i, l_i in branches:
        w_i = l_i * jnp.exp(m_i - big_m)
        num = num + w_i * o_i.astype(jnp.float32)
        den = den + w_i
    return (num / den).astype(v.dtype)


def fox_diff_mixer(h, positions, w_in, b_forget, lam_q1, lam_k1, lam_q2, lam_k2, subln, w_out, layer_idx):
    b, s, _ = h.shape
    proj = jnp.einsum('bsd,de->bse', h, w_in)
    sizes = (WIDTH_FOX, WIDTH_FOX, WIDTH_FOX, N_HEADS_FOX, WIDTH_FOX,
             DIFF_QK_WIDTH, DIFF_QK_WIDTH, WIDTH_DIFF, WIDTH_DIFF)
    cuts = [int(v) for v in np.cumsum(sizes)[:-1]]
    qa, ka, va, fa, ga, qd, kd, vd, gd = jnp.split(proj, cuts, axis=-1)
    log_f = jax.nn.log_sigmoid((fa + b_forget).astype(jnp.float32)).transpose(0, 2, 1)
    oa = fox_attention(split_heads(qa, N_HEADS_FOX, HEAD_DIM), split_heads(ka, N_HEADS_FOX, HEAD_DIM),
                       split_heads(va, N_HEADS_FOX, HEAD_DIM), log_f)
    out_a = merge_heads(oa) * jax.nn.silu(ga)
    def sub_heads(t):
        return t.reshape(b, s, N_HEADS_DIFF, 2, HEAD_DIM).transpose(0, 2, 3, 1, 4)
    qd2, kd2 = sub_heads(qd), sub_heads(kd)
    q1, q2 = partial_rope(qd2[:, :, 0], positions), partial_rope(qd2[:, :, 1], positions)
    k1, k2 = partial_rope(kd2[:, :, 0], positions), partial_rope(kd2[:, :, 1], positions)
    lam_init = 0.8 - 0.6 * math.exp(-0.3 * layer_idx)
    lam = (jnp.exp(jnp.sum(lam_q1.astype(jnp.float32) * lam_k1.astype(jnp.float32)))
           - jnp.exp(jnp.sum(lam_q2.astype(jnp.float32) * lam_k2.astype(jnp.float32))) + lam_init)
    od = diff_attention(q1, q2, k1, k2, split_heads(vd, N_HEADS_DIFF, DIFF_V_DIM), lam)
    od = rms_norm(od, subln) * (1.0 - lam_init)
    out_b = merge_heads(od) * jax.nn.silu(gd)
    return jnp.einsum('bse,ed->bsd', jnp.concatenate([out_a, out_b], axis=-1), w_out)


def dilated_mixer(h, positions, w_in, w_out):
    proj = jnp.einsum('bsd,de->bse', h, w_in)
    qc, kc, vc, gc = jnp.split(proj, 4, axis=-1)
    q = partial_rope(split_heads(qc, N_HEADS_DIL, HEAD_DIM), positions)
    k = partial_rope(split_heads(kc, N_HEADS_DIL, HEAD_DIM), positions)
    v = split_heads(vc, N_HEADS_DIL, HEAD_DIM)
    oc = merge_heads(dilated_mixture(q, k, v)) * jax.nn.silu(gc)
    return jnp.einsum('bse,ed->bsd', oc, w_out)


def setup_inputs(seed: int = 0) -> dict:
    key = jax.random.key(seed)
    ks = jax.random.split(key, 18)
    n_even = (DEPTH + 1) // 2
    n_odd = DEPTH // 2

    def nrm(k, shape, scale):
        return jax.random.normal(k, shape, jnp.float32) * scale

    x = nrm(ks[0], (BATCH, SEQ, D_MODEL), 1.0)
    c = nrm(ks[1], (BATCH, D_MODEL), 1.0)
    offsets = jax.random.randint(ks[2], (BATCH, 1), 0, 4096, dtype=jnp.int32)
    positions = (offsets + jnp.arange(SEQ, dtype=jnp.int32)[None, :]).astype(jnp.int32)
    norm_pre = 1.0 + nrm(ks[3], (DEPTH, D_MODEL), 0.1)
    norm_post = 1.0 + nrm(ks[4], (DEPTH, D_MODEL), 0.1)
    ada_w = nrm(ks[5], (DEPTH, D_MODEL, 3 * D_MODEL), D_MODEL ** -0.5)
    ada_b = nrm(ks[6], (DEPTH, 3 * D_MODEL), 0.02)
    ev_w_in = nrm(ks[7], (n_even, D_MODEL, EVEN_IN), D_MODEL ** -0.5)
    ev_b_forget = 2.0 + nrm(ks[8], (n_even, N_HEADS_FOX), 0.5)
    ev_lambda_q1 = nrm(ks[9], (n_even, HEAD_DIM), 0.1)
    ev_lambda_k1 = nrm(ks[10], (n_even, HEAD_DIM), 0.1)
    ev_lambda_q2 = nrm(ks[11], (n_even, HEAD_DIM), 0.1)
    ev_lambda_k2 = nrm(ks[12], (n_even, HEAD_DIM), 0.1)
    ev_subln = 1.0 + nrm(ks[13], (n_even, DIFF_V_DIM), 0.1)
    ev_w_out = nrm(ks[14], (n_even, WIDTH_FOX + WIDTH_DIFF, D_MODEL), (WIDTH_FOX + WIDTH_DIFF) ** -0.5)
    od_w_in = nrm(ks[15], (n_odd, D_MODEL, ODD_IN), D_MODEL ** -0.5)
    od_w_out = nrm(ks[16], (n_odd, WIDTH_DIL, D_MODEL), WIDTH_DIL ** -0.5)
    return {'x': x, 'c': c, 'positions': positions, 'norm_pre': norm_pre, 'norm_post': norm_post,
            'ada_w': ada_w, 'ada_b': ada_b, 'ev_w_in': ev_w_in, 'ev_b_forget': ev_b_forget,
            'ev_lambda_q1': ev_lambda_q1, 'ev_lambda_k1': ev_lambda_k1,
            'ev_lambda_q2': ev_lambda_q2, 'ev_lambda_k2': ev_lambda_k2, 'ev_subln': ev_subln,
            'ev_w_out': ev_w_out, 'od_w_in': od_w_in, 'od_w_out': od_w_out}


def reference(x, c, positions, norm_pre, norm_post, ada_w, ada_b, ev_w_in, ev_b_forget,
              ev_lambda_q1, ev_lambda_k1, ev_lambda_q2, ev_lambda_k2, ev_subln, ev_w_out,
              od_w_in, od_w_out):
    cond = jax.nn.silu(c)
    for layer in range(DEPTH):
        mod = jnp.einsum('bd,de->be', cond, ada_w[layer]) + ada_b[layer]
        shift, scale, gate = jnp.split(mod, 3, axis=-1)
        h = rms_norm(x, norm_pre[layer]) * (1.0 + scale[:, None, :]) + shift[:, None, :]
        if layer % 2 == 0:
            i = layer // 2
            y = fox_diff_mixer(h, positions, ev_w_in[i], ev_b_forget[i], ev_lambda_q1[i], ev_lambda_k1[i],
                               ev_lambda_q2[i], ev_lambda_k2[i], ev_subln[i], ev_w_out[i], layer)
        else:
            j = layer // 2
            y = dilated_mixer(h, positions, od_w_in[j], od_w_out[j])
        x = x + gate[:, None, :] * rms_norm(y, norm_post[layer])
    return x
```

```python
import math
import os
from contextlib import ExitStack

import numpy as np
import ml_dtypes
import concourse.bass as bass
import concourse.mybir as mybir
from concourse.bass_utils import run_bass_kernel_spmd

F32 = mybir.dt.float32
BF16 = mybir.dt.bfloat16
I32 = mybir.dt.int32
AF = mybir.ActivationFunctionType
ALU = mybir.AluOpType
AX = mybir.AxisListType

S_LEN = 2048
D = 1024
NB = 16
KC = 8
EPS = 1e-6
N_CORES = 8
THETA = 500000.0


ALL_BUFS = []


class Buf:
    __slots__ = ("w", "r")

    def __init__(self):
        self.w = None
        self.r = {}
        ALL_BUFS.append(self)


class Sched:
    def __init__(self, nc, stack, n_dma_sems=16):
        self.nc = nc
        self.eng = {"pe": nc.tensor, "dve": nc.vector, "act": nc.scalar, "pool": nc.gpsimd, "sp": nc.sync}
        self.sem = {k: stack.enter_context(nc.semaphore("s_" + k)) for k in self.eng}
        self.dsem = [stack.enter_context(nc.semaphore("d%d" % i)) for i in range(n_dma_sems)]
        self.semobj = {}
        for k in self.eng:
            self.semobj[("e", k)] = self.sem[k]
        for i in range(n_dma_sems):
            self.semobj[("d", i)] = self.dsem[i]
        self.plan = True
        self.targets = {("e", k): set() for k in self.eng}
        self.rank = {}
        self.reset()

    def reset(self):
        self.cnt = {k: 0 for k in self.eng}
        self.seen = {k: {} for k in self.eng}
        self.dcnt = [0] * len(self.dsem)
        self.dnext = 0
        for b in ALL_BUFS:
            b.w = None
            b.r = {}

    def finish_plan(self):
        self.plan = False
        for k, tg in self.targets.items():
            self.rank[k] = {v: i + 1 for i, v in enumerate(sorted(tg))}
        self.reset()

    def _waits(self, e, reads, writes, skip_self=False):
        need = {}

        def add(k, v):
            if k == ("e", "pe") and e == "pe":
                return
            if skip_self and k == ("e", e):
                return
            if need.get(k, 0) < v:
                need[k] = v

        for b in reads:
            if b.w is not None:
                add(*b.w)
        for b in writes:
            if b.w is not None:
                add(*b.w)
            for k, v in b.r.items():
                add(k, v)
        h = self.eng[e]
        seen = self.seen[e]
        for k, v in need.items():
            if seen.get(k, 0) >= v:
                continue
            seen[k] = v
            if k[0] == "e":
                if self.plan:
                    self.targets[k].add(v)
                else:
                    h.wait_ge(self.semobj[k], self.rank[k][v])
            elif not self.plan:
                h.wait_ge(self.semobj[k], v)

    def _commit(self, t, reads, writes):
        k, v = t
        for b in reads:
            if b.r.get(k, 0) < v:
                b.r[k] = v
        for b in writes:
            b.w = t
            b.r = {}

    def op(self, e, fn, reads=(), writes=(), skip_self=False):
        self._waits(e, reads, writes, skip_self)
        self.cnt[e] += 1
        k = ("e", e)
        if not self.plan:
            ins = fn(self.eng[e])
            if self.cnt[e] in self.rank[k]:
                ins.then_inc(self.sem[e], 1)
        t = (k, self.cnt[e])
        self._commit(t, reads, writes)
        return t

    def dma(self, fn, reads=(), writes=(), q="sp"):
        i = self.dnext
        self.dnext = (self.dnext + 1) % len(self.dsem)
        h = self.eng[q]
        k = ("d", i)
        if self.dcnt[i] > 0 and self.seen[q].get(k, 0) < self.dcnt[i]:
            if not self.plan:
                h.wait_ge(self.dsem[i], self.dcnt[i])
            self.seen[q][k] = self.dcnt[i]
        self._waits(q, reads, writes)
        self.dcnt[i] += 16
        if not self.plan:
            ins = fn(h)
            ins.then_inc(self.dsem[i], 16)
        t = (k, self.dcnt[i])
        self._commit(t, reads, writes)
        return t


def lam_init_of(layer_idx):
    return 0.8 - 0.6 * math.exp(-0.3 * layer_idx)


def build(kinds):
    n = len(kinds)
    nc = bass.Bass("TRN2", target_bir_lowering=False)

    def dram(name, shape, dt, kind):
        return nc.dram_tensor(name, shape, dt, kind=kind).ap()

    x_in = dram("x", [S_LEN, D], F32, "ExternalInput")
    pos_in = dram("pos", [128, NB], I32, "ExternalInput")
    ccol_in = dram("ccol", [128, KC], F32, "ExternalInput")
    adaw_in = dram("adaw", [n, 6, 4, 128, 2, 512], F32, "ExternalInput")
    colp_in = dram("colp", [n, 128, 40], F32, "ExternalInput")
    wu_in = dram("wu", [n, 8, 4, 128, 2, 512], F32, "ExternalInput")
    wo_in = dram("wo", [n, 2, 4, 128, 2, 512], F32, "ExternalInput")
    wf_in = dram("wf", [n, 128, KC, 8], F32, "ExternalInput")
    bfc_in = dram("bfc", [n, 8, 1], F32, "ExternalInput")
    lamrep_in = dram("lamrep", [n, 128, 256], F32, "ExternalInput")
    subln_in = dram("sublnc", [n, 128, 1], F32, "ExternalInput")
    lamc_in = dram("lamc", [n, 128, 2], F32, "ExternalInput")
    cst_in = dram("cst", [128, 32], F32, "ExternalInput")
    cm_in = dram("cm", [128, 2048], BF16, "ExternalInput")
    out = dram("out", [S_LEN, D], F32, "ExternalOutput")
    xs = dram("xs", [S_LEN, D], F32, "Internal")
    gsc = dram("gsc", [n, 8, 128], F32, "Internal")

    with ExitStack() as st:
        S = Sched(nc, st)

        def sb(name, shape, dt):
            return st.enter_context(nc.sbuf_tensor("sb_" + name, shape, dt))

        def ps(name, shape, dt):
            return st.enter_context(nc.psum_tensor("ps_" + name, shape, dt))

        hT = sb("hT", [128, KC, S_LEN], BF16)
        bH = [Buf() for _ in range(NB)]
        attnT = sb("attnT", [128, 8, S_LEN], BF16)
        bAT = [[Buf() for _ in range(4)] for _ in range(8)]
        wbf = sb("wbf", [128, 2, KC, 512], BF16)
        bWB = [[Buf() for _ in range(4)] for _ in range(2)]
        wst = sb("wst", [128, 2, 2, 512], F32)
        bWS = [Buf(), Buf()]
        qk = sb("qk", [128, NB, 280], BF16)
        bQK = [Buf() for _ in range(NB)]
        Rf = sb("Rf", [128, NB, 4, 16], F32)
        bRf = [Buf() for _ in range(NB)]
        T4 = sb("T4", [128, 4, S_LEN], BF16)
        bT4 = [[Buf() for _ in range(4)] for _ in range(4)]
        vaug = sb("vaug", [128, NB, 256], BF16)
        vd = sb("vd", [128, NB, 128], BF16)
        bV = [Buf() for _ in range(NB)]
        sgT = sb("sgT", [128, S_LEN], BF16)
        bSG = [Buf() for _ in range(4)]
        NP = 7
        P = sb("P", [128, NP, 512], BF16)
        bP = [Buf() for _ in range(NP)]
        junk = P[:, 0:2, :].rearrange("p a d -> p (a d)")
        bJ = bP[0]
        bJ2 = bP[1]
        ftmp = sb("ftmp", [128, 4, 512], F32)
        bF = [Buf() for _ in range(4)]
        cm = sb("cm", [128, 2048], BF16)
        bCM = Buf()
        ones128 = sb("ones128", [128, 128], BF16)
        identb = sb("identb", [128, 128], BF16)
        identf = sb("identf", [128, 128], F32)
        bCONST = Buf()
        cst = sb("cst", [128, 32], F32)
        CS = sb("CS", [128, NB, 16], F32)
        SN = sb("SN", [128, NB, 16], F32)
        bROPE = Buf()
        Gbc = sb("Gbc", [128, D], F32)
        bG = Buf()
        ASc = sb("ASc", [128, 2, 16], F32)
        bAS = [Buf(), Buf()]
        modc = sb("modc", [128, 32], F32)
        bMOD = Buf()
        colp = sb("colp", [128, 40], F32)
        bCOLP = Buf()
        grow = sb("grow", [8, 128], F32)
        bGROW = Buf()
        bGSC = [Buf() for _ in range(n)]
        condf = sb("condf", [128, KC], F32)
        condb = sb("condb", [128, KC, 2], BF16)
        bCOND = Buf()
        FQ = sb("FQ", [128, 768], BF16)
        FK = sb("FK", [128, 768], BF16)
        bFQK = Buf()
        cs_tok = sb("cs_tok", [128, 128], F32)
        r_tok = sb("r_tok", [128, 128], F32)
        hml = sb("hml", [128, 3, 128], BF16)
        bSPL = Buf()
        xo = sb("xo", [128, 2, D], F32)
        bXO = [Buf(), Buf()]
        xw = sb("xw", [128, 2, D], F32)
        bXW = [Buf(), Buf()]
        stat = sb("stat", [128, 8, 8], F32)
        bST_ = [Buf() for _ in range(8)]
        lamrep = sb("lamrep", [128, 256], F32)
        lamt = sb("lamt", [128, 8], F32)
        wsub = sb("wsub", [128, 1], F32)
        negb = sb("negb", [8, 1], F32)
        wfst = sb("wfst", [128, KC, 8], F32)
        wfb = sb("wfb", [128, KC, 8], BF16)
        bLAM = Buf()
        bWF = Buf()
        posi = sb("posi", [128, NB], I32)
        posf = sb("posf", [128, NB], F32)
        Yt = sb("Yt", [128, NB, 16], F32)
        Yi = sb("Yi", [128, NB, 16], I32)
        PT = [ps("PT%d" % i, [128, 512], F32) for i in range(2)]
        bPT = [Buf(), Buf()]
        NST = 4
        NACC = 2
        ST = [ps("ST%d" % i, [128, 512], F32) for i in range(NST)]
        bSTb = [Buf() for _ in range(NST)]
        ACC = [ps("ACC%d" % i, [128, 512], F32) for i in range(NACC)]
        bACC = [Buf() for _ in range(NACC)]
        bXS = [Buf() for _ in range(NB)]

        LA_KIND = {'fox': int(os.environ.get('F_LA', '2')), 'diff': int(os.environ.get('F_LA', '2')), 'dil': int(os.environ.get('F_LA', '2'))}
        cnt = {"wst": 0, "stat": 0, "pt": 0, "st": 0, "p": 0, "f": 0, "acc": 0, "x": 0}

        def nxt(key, mod):
            v = cnt[key] % mod
            cnt[key] += 1
            return v

        def program():
            for k_ in cnt:
                cnt[k_] = 0
            S.op("dve", lambda v: v.memset(ones128[:], 1.0), writes=[bCONST])
            S.op("dve", lambda v: v.memset(identf[:], 1.0), writes=[bCONST])
            S.op("pool", lambda g: g.affine_select(out=identf[:], in_=identf[:], pattern=[[1, 128]], compare_op=ALU.is_equal,
                                                   fill=0.0, base=0, channel_multiplier=-1), reads=[bCONST], writes=[bCONST])
            S.op("dve", lambda v: v.tensor_copy(out=identb[:], in_=identf[:]), reads=[bCONST], writes=[bCONST])
            S.op("dve", lambda v: v.memset(vaug[:], 1.0), writes=bV)
            S.op("pool", lambda g: g.memset(T4[64:128, 0, :], 0.0), writes=bT4[0])
            S.op("pool", lambda g: g.memset(T4[0:64, 1, :], 0.0), writes=bT4[1])
            S.dma(lambda q: q.dma_start(out=cm[:], in_=cm_in[:, :]), writes=[bCM])
            S.dma(lambda q: q.dma_start(out=cst[:], in_=cst_in[:, :]), writes=[bROPE])
            S.dma(lambda q: q.dma_start(out=posi[:], in_=pos_in[:, :]), writes=[bROPE])
            S.dma(lambda q: q.dma_start(out=condf[:], in_=ccol_in[:, :]), writes=[bCOND])
            S.op("act", lambda a: a.activation(out=condf[:], in_=condf[:], func=AF.Silu), reads=[bCOND], writes=[bCOND])
            S.op("dve", lambda v: v.tensor_copy(out=condb[:, :, 0], in_=condf[:]), reads=[bCOND], writes=[bCOND])
            S.op("dve", lambda v: v.tensor_copy(out=condb[:, :, 1], in_=condf[:]), reads=[bCOND], writes=[bCOND])
            S.op("dve", lambda v: v.tensor_copy(out=posf[:], in_=posi[:]), reads=[bROPE], writes=[bROPE])
            S.op("dve", lambda v: v.tensor_tensor(out=Yt[:], in0=posf[:].unsqueeze(2).to_broadcast([128, NB, 16]),
                                                  in1=cst[:, 0:16].unsqueeze(1).to_broadcast([128, NB, 16]), op=ALU.mult),
                 reads=[bROPE], writes=[bROPE])
            S.op("dve", lambda v: v.tensor_tensor(out=Yt[:], in0=Yt[:], in1=cst[:, 16:32].unsqueeze(1).to_broadcast([128, NB, 16]),
                                                  op=ALU.add), reads=[bROPE], writes=[bROPE])
            S.op("dve", lambda v: v.tensor_copy(out=Yi[:], in_=Yt[:]), reads=[bROPE], writes=[bROPE])
            S.op("dve", lambda v: v.tensor_copy(out=SN[:], in_=Yi[:]), reads=[bROPE], writes=[bROPE])
            S.op("dve", lambda v: v.tensor_tensor(out=Yt[:], in0=Yt[:], in1=SN[:], op=ALU.subtract), reads=[bROPE], writes=[bROPE])
            S.op("act", lambda a: a.activation(out=Yt[:], in_=Yt[:], func=AF.Sin, scale=2.0 * math.pi * (1.0 - 1e-6)),
                 reads=[bROPE], writes=[bROPE])
            S.op("dve", lambda v: v.tensor_copy(out=CS[:, :, 0:8], in_=Yt[:, :, 8:16]), reads=[bROPE], writes=[bROPE])
            S.op("dve", lambda v: v.tensor_copy(out=CS[:, :, 8:16], in_=Yt[:, :, 8:16]), reads=[bROPE], writes=[bROPE])
            S.op("dve", lambda v: v.tensor_scalar(out=SN[:, :, 0:8], in0=Yt[:, :, 0:8], scalar1=-1.0, scalar2=None, op0=ALU.mult),
                 reads=[bROPE], writes=[bROPE])
            S.op("dve", lambda v: v.tensor_copy(out=SN[:, :, 8:16], in_=Yt[:, :, 0:8]), reads=[bROPE], writes=[bROPE])

            def emit_mod_load(li, j, wslot, engs=("pool",)):
                emit_wload(lambda qtr: adaw_in[li, j, qtr], wslot, engs)

            def emit_mod_mm(li, j, wslot):
                if j == 0:
                    S.dma(lambda q: q.dma_start(out=colp[:], in_=colp_in[li]), writes=[bCOLP])
                b = nxt("pt", 2)
                for m in range(4):
                    for kc in range(KC):
                        S.op("pe", lambda pe: pe.matmul(PT[b][:, 2 * m:2 * m + 2], lhsT=wbf[:, wslot, kc, m * 128:(m + 1) * 128],
                                                        rhs=condb[:, kc, :], start=(kc == 0), stop=(kc == 7)),
                             reads=[bWB[wslot][kc // 2], bCOND], writes=[bPT[b]])
                S.op("dve", lambda v: v.tensor_tensor(out=modc[:, 4 * j:4 * j + 4],
                                                      in0=PT[b][:, 0:8].rearrange("p (i two) -> p i two", two=2)[:, :, 0],
                                                      in1=colp[:, 4 * j:4 * j + 4], op=ALU.add),
                     reads=[bPT[b], bCOLP], writes=[bMOD])
                if j == 5:
                    par = li % 2
                    S.op("dve", lambda v: v.scalar_tensor_tensor(out=ASc[:, par, 0:8], in0=modc[:, 8:16], scalar=1.0,
                                                                 in1=colp[:, 24:32], op0=ALU.add, op1=ALU.mult),
                         reads=[bMOD, bCOLP], writes=[bAS[par]])
                    S.op("dve", lambda v: v.tensor_copy(out=ASc[:, par, 8:16], in_=modc[:, 0:8]), reads=[bMOD], writes=[bAS[par]])
                    S.op("dve", lambda v: v.tensor_tensor(out=modc[:, 24:32], in0=modc[:, 16:24], in1=colp[:, 32:40], op=ALU.mult),
                         reads=[bMOD, bCOLP], writes=[bMOD])
                    b2 = nxt("pt", 2)
                    S.op("pe", lambda pe: pe.transpose(out=PT[b2][0:8, 0:128], in_=modc[:, 24:32], identity=identf[:]),
                         reads=[bMOD, bCONST], writes=[bPT[b2]])
                    S.op("dve", lambda v: v.tensor_copy(out=grow[0:8, :], in_=PT[b2][0:8, 0:128]), reads=[bPT[b2]], writes=[bGROW])
                    S.dma(lambda q: q.dma_start(out=gsc[li], in_=grow[0:8, :]), reads=[bGROW], writes=[bGSC[li]])

            def emit_g_bcast(li):
                S.dma(lambda q: q.dma_start(out=Gbc[:, :], in_=gsc[li:li + 1].rearrange("o a b -> o (a b)").to_broadcast([128, D])),
                      reads=[bGSC[li]], writes=[bG])

            def emit_rstd(s, src, dst, count):
                S.op("act", lambda a: a.activation(out=stat[:, s, dst:dst + 1], in_=stat[:, s, src:src + 1], func=AF.Ln,
                                                   scale=1.0 / count, bias=EPS), reads=[bST_[s]], writes=[bST_[s]])
                S.op("act", lambda a: a.activation(out=stat[:, s, dst:dst + 1], in_=stat[:, s, dst:dst + 1], func=AF.Exp,
                                                   scale=-0.5), reads=[bST_[s]], writes=[bST_[s]])

            def emit_h_block(li, blk, xsrc, bsrc, xdst, bdst):
                emit_h1(li, blk, xsrc, bsrc, xdst, bdst)
                emit_h2(li, blk, xdst, bdst)

            def emit_h1(li, blk, xsrc, bsrc, xdst, bdst):
                s = nxt("stat", 8)
                S.op("act", lambda a: a.activation(out=junk, in_=xsrc, func=AF.Square, accum_out=stat[:, s, 0:1]),
                     reads=[bsrc], writes=[bJ, bJ2, bST_[s]])
                emit_rstd(s, 0, 1, float(D))
                S.op("dve", lambda v: v.tensor_scalar(out=xdst, in0=xsrc, scalar1=stat[:, s, 1:2], scalar2=None, op0=ALU.mult),
                     reads=[bsrc, bST_[s]], writes=[bdst])

            def emit_h2(li, blk, xdst, bdst):
                par = li % 2
                for half in range(2):
                    b = nxt("pt", 2)
                    for k4 in range(4):
                        kc = half * 4 + k4
                        S.op("pe", lambda pe: pe.transpose(out=PT[b][:, k4 * 128:(k4 + 1) * 128], in_=xdst[:, kc * 128:(kc + 1) * 128],
                                                           identity=identf[:]), reads=[bdst, bCONST], writes=[bPT[b]])
                    for k4 in range(4):
                        kc = half * 4 + k4
                        if kc % 2 == 0:
                            S.op("act", lambda a: a.activation(out=hT[:, kc, blk * 128:(blk + 1) * 128], in_=PT[b][:, k4 * 128:(k4 + 1) * 128],
                                                               func=AF.Identity, scale=ASc[:, par, kc:kc + 1], bias=ASc[:, par, 8 + kc:9 + kc]),
                                 reads=[bPT[b], bAS[par]], writes=[bH[blk]])
                        else:
                            S.op("dve", lambda v: v.tensor_scalar(out=hT[:, kc, blk * 128:(blk + 1) * 128], in0=PT[b][:, k4 * 128:(k4 + 1) * 128],
                                                                  scalar1=ASc[:, par, kc:kc + 1], scalar2=ASc[:, par, 8 + kc:9 + kc],
                                                                  op0=ALU.mult, op1=ALU.add),
                                 reads=[bPT[b], bAS[par]], writes=[bH[blk]])

            def emit_wload(src_fn, wslot, engs=("pool",)):
                for qtr in range(4):
                    sl = nxt("wst", 2)
                    S.dma(lambda q: q.dma_start(out=wst[:, sl], in_=src_fn(qtr)), writes=[bWS[sl]])
                    eng = engs[qtr % len(engs)]
                    if eng == "act":
                        S.op("act", lambda a: a.activation(out=wbf[:, wslot, 2 * qtr:2 * qtr + 2, :], in_=wst[:, sl], func=AF.Copy),
                             reads=[bWS[sl]], writes=[bWB[wslot][qtr]])
                    else:
                        S.op(eng, lambda g: g.tensor_copy(out=wbf[:, wslot, 2 * qtr:2 * qtr + 2, :], in_=wst[:, sl]),
                             reads=[bWS[sl]], writes=[bWB[wslot][qtr]])

            def emit_attention(groups, LA):
                SLOT = [(PT[0], PT[1]), (ST[0], ST[1]), (ST[2], ST[3])]
                bSLOT = [(bPT[0], bPT[1]), (bSTb[0], bSTb[1]), (bSTb[2], bSTb[3])]
                flat = []
                for g in groups:
                    c = g["c"]
                    jl = 4 * c + 3
                    for j in range(jl + 1):
                        flat.append((g, j, jl))
                pairs = [flat[i:i + 2] for i in range(0, len(flat), 2)]
                pend = []

                def emit_pv_pair(items, pbufs):
                    for (g, j, jl, n0, p) in items:
                        S.op("pe", lambda pe: pe.matmul(ACC[g["accO"]][:, n0:512], lhsT=g["vfun"](j), rhs=P[:, p, n0:512],
                                                        start=(j == 0), stop=(j == jl)),
                             reads=pbufs + [bV[j]], writes=[bACC[g["accO"]]])
                        if g["accD"] is not None:
                            S.op("pe", lambda pe: pe.matmul(ACC[g["accD"]][:, n0:512], lhsT=ones128[:, :], rhs=P[:, p, n0:512],
                                                            start=(j == 0), stop=(j == jl)),
                                 reads=pbufs + [bCONST], writes=[bACC[g["accD"]]])
                        if j == jl:
                            g["fin"](g)

                for pr in pairs:
                    k = nxt("st", 3)
                    m = nxt("p", 3)
                    sbufs = list(bSLOT[k][:len(pr)])
                    pbufs = [bP[2 * m + t] for t in range(len(pr))]
                    items = []
                    for t, (g, j, jl) in enumerate(pr):
                        c = g["c"]
                        n0 = max(0, j - 4 * c) * 128
                        qi, ki, r0, r1 = g["qk"]
                        bank = SLOT[k][t]
                        S.op("pe", lambda pe: pe.matmul(bank[:, n0:512], lhsT=T4[r0:r1, ki, j * 128:(j + 1) * 128],
                                                        rhs=T4[r0:r1, qi, c * 512 + n0:(c + 1) * 512], start=True, stop=True),
                             reads=[bT4[ki][j // 4], bT4[qi][c]], writes=[bSLOT[k][t]])
                        items.append((g, j, jl, n0, 2 * m + t))
                    for t, (g, j, jl, n0, p) in enumerate(items):
                        bank = SLOT[k][t]
                        S.op("act", lambda a: a.activation(out=P[:, p, n0:512], in_=bank[:, n0:512], func=AF.Exp, scale=g["scale"]),
                             reads=sbufs, writes=[bP[p]], skip_self=(t > 0))
                    for t, (g, j, jl, n0, p) in enumerate(items):
                        c = g["c"]
                        if g["mask"] == "causal":
                            if j >= 4 * c:
                                S.op("pool", lambda gp: gp.affine_select(out=P[:, p, n0:n0 + 128], in_=P[:, p, n0:n0 + 128],
                                                                         pattern=[[1, 128]], compare_op=ALU.is_ge, fill=0.0,
                                                                         base=0, channel_multiplier=-1),
                                     reads=pbufs, writes=[bP[p]], skip_self=(t > 0))
                        else:
                            m0 = max(4 * c - j, 0)
                            S.op("dve", lambda v: v.tensor_tensor(out=P[:, p, n0:512], in0=P[:, p, n0:512],
                                                                  in1=cm[:, m0 * 128:m0 * 128 + (512 - n0)], op=ALU.mult),
                                 reads=pbufs + [bCM], writes=[bP[p]], skip_self=(t > 0))
                    pend.append((items, pbufs))
                    if len(pend) > LA:
                        emit_pv_pair(*pend.pop(0))
                while pend:
                    emit_pv_pair(*pend.pop(0))

            def emit_unit(li, u, kind, next_w, mid_hook=None):
                wslot = u % 2
                if kind == "diff" and u == 4:
                    S.op("pool", lambda g: g.memset(T4[64:128, 0, :], 0.0), writes=bT4[0])
                    S.op("pool", lambda g: g.memset(T4[0:64, 1, :], 0.0), writes=bT4[1])
                if next_w is not None:
                    next_w()
                if kind == "fox":
                    pr = u
                    fqv = FQ[:, :].rearrange("p (b h r) -> p b h r", b=NB, h=8)
                    fkv = FK[:, :].rearrange("p (b h r) -> p b h r", b=NB, h=8)
                    qk4 = qk[:, :, :].rearrange("p b (g e) -> p b g e", g=4)
                    S.op("pool", lambda gp: gp.tensor_copy(out=qk4[:, :, 0:2, 64:70], in_=fqv[:, :, 2 * pr:2 * pr + 2, :]),
                         reads=[bFQK], writes=bQK)
                    S.op("pool", lambda gp: gp.tensor_copy(out=qk4[:, :, 2:4, 64:70], in_=fkv[:, :, 2 * pr:2 * pr + 2, :]),
                         reads=[bFQK], writes=bQK)
                for blk in range(NB):
                    b = nxt("pt", 2)
                    for kc in range(KC):
                        S.op("pe", lambda pe: pe.matmul(PT[b][:, 0:384], lhsT=hT[:, kc, blk * 128:(blk + 1) * 128],
                                                        rhs=wbf[:, wslot, kc, 0:384], start=(kc == 0), stop=(kc == 7)),
                             reads=[bH[blk], bWB[wslot][kc // 2]], writes=[bPT[b]])
                    srcq = PT[b][:, 0:128].rearrange("p (h d) -> p h d", h=2)
                    srck = PT[b][:, 128:256].rearrange("p (h d) -> p h d", h=2)
                    srcv = PT[b][:, 256:384].rearrange("p (h d) -> p h d", h=2)
                    if kind == "fox":
                        qv = qk[:, blk, :].rearrange("p (g e) -> p g e", g=4)
                        S.op("act", lambda a: a.activation(out=qv[:, 0:2, 0:64], in_=srcq, func=AF.Copy, scale=0.125),
                             reads=[bPT[b]], writes=[bQK[blk]])
                        S.op("dve", lambda v: v.tensor_copy(out=qv[:, 2:4, 0:64], in_=srck), reads=[bPT[b]], writes=[bQK[blk]])
                    else:
                        qv = qk[:, blk, 0:256].rearrange("p (g d) -> p g d", g=4)
                        S.op("act", lambda a: a.activation(out=qv[:, 0:2, :], in_=srcq, func=AF.Copy),
                             reads=[bPT[b]], writes=[bQK[blk]])
                        S.op("dve", lambda v: v.tensor_copy(out=qv[:, 2:4, :], in_=srck), reads=[bPT[b]], writes=[bQK[blk]])
                        S.op("dve", lambda v: v.tensor_copy(out=Rf[:, blk], in_=PT[b][:, 0:256].rearrange("p (g d) -> p g d", g=4)[:, :, 0:16]),
                             reads=[bPT[b]], writes=[bRf[blk]])
                    if kind == "diff":
                        S.op("act", lambda a: a.activation(out=vd[:, blk, :], in_=PT[b][:, 256:384], func=AF.Copy),
                             reads=[bPT[b]], writes=[bV[blk]])
                    else:
                        va4 = vaug[:, blk, :].rearrange("p (g d) -> p g d", g=4)
                        S.op("dve", lambda v: v.tensor_copy(out=va4[:, 0, :], in_=srcv[:, 0, :]), reads=[bPT[b]], writes=[bV[blk]])
                        S.op("act", lambda a: a.activation(out=va4[:, 3, :], in_=srcv[:, 1, :], func=AF.Copy), reads=[bPT[b]], writes=[bV[blk]])
                for c in range(4):
                    b = nxt("pt", 2)
                    for kc in range(KC):
                        S.op("pe", lambda pe: pe.matmul(PT[b][:, :], lhsT=wbf[:, wslot, kc, 384:512], rhs=hT[:, kc, c * 512:(c + 1) * 512],
                                                        start=(kc == 0), stop=(kc == 7)),
                             reads=bH[4 * c:4 * c + 4] + [bWB[wslot][kc // 2]], writes=[bPT[b]])
                    S.op("act", lambda a: a.activation(out=sgT[:, c * 512:(c + 1) * 512], in_=PT[b][:, :], func=AF.Silu),
                         reads=[bPT[b]], writes=[bSG[c]])
                if kind != "fox":
                    R2 = ftmp[:, 0:2, :].rearrange("p a (b g d) -> p (a b) g d", g=4, d=16)
                    bR2 = bF[0]
                    bR2b = bF[1]
                    S.op("dve", lambda v: v.tensor_tensor(out=R2[:, :, :, 0:8], in0=Rf[:, :, :, 8:16],
                                                          in1=SN[:, :, 0:8].unsqueeze(2).to_broadcast([128, NB, 4, 8]), op=ALU.mult),
                         reads=bRf + [bROPE], writes=[bR2, bR2b])
                    S.op("dve", lambda v: v.tensor_tensor(out=R2[:, :, :, 8:16], in0=Rf[:, :, :, 0:8],
                                                          in1=SN[:, :, 8:16].unsqueeze(2).to_broadcast([128, NB, 4, 8]), op=ALU.mult),
                         reads=bRf + [bROPE], writes=[bR2, bR2b])
                    S.op("dve", lambda v: v.tensor_tensor(out=Rf[:, :, :, :], in0=Rf[:, :, :, :],
                                                          in1=CS[:, :, :].unsqueeze(2).to_broadcast([128, NB, 4, 16]), op=ALU.mult),
                         reads=bRf + [bROPE, bR2, bR2b], writes=bRf)
                    qkr = qk[:, :, 0:256].rearrange("p b (g d) -> p b g d", g=4)
                    S.op("dve", lambda v: v.tensor_tensor(out=qkr[:, :, :, 0:16], in0=Rf[:, :, :, :], in1=R2[:, :, :, :], op=ALU.add),
                         reads=bRf + [bR2, bR2b], writes=bQK)
                if kind == "fox":
                    for i in range(4):
                        for c in range(4):
                            b = nxt("pt", 2)
                            ptb = PT[b][:, :].bitcast(BF16)
                            for b4 in range(4):
                                blk = c * 4 + b4
                                S.op("pe", lambda pe: pe.transpose(out=ptb[0:70, b4 * 128:(b4 + 1) * 128], in_=qk[:, blk, i * 70:(i + 1) * 70],
                                                                   identity=identb[:]), reads=[bQK[blk], bCONST], writes=[bPT[b]])
                            eng = "act" if (i + c) % 2 == 0 else "dve"
                            if eng == "act":
                                S.op("act", lambda a: a.activation(out=T4[0:70, i, c * 512:(c + 1) * 512], in_=ptb[0:70, 0:512], func=AF.Copy),
                                     reads=[bPT[b]], writes=[bT4[i][c]])
                            else:
                                S.op("dve", lambda v: v.tensor_copy(out=T4[0:70, i, c * 512:(c + 1) * 512], in_=ptb[0:70, 0:512]),
                                     reads=[bPT[b]], writes=[bT4[i][c]])
                else:
                    for i in (0, 2):
                        for c in range(4):
                            b = nxt("pt", 2)
                            ptb = PT[b][:, :].bitcast(BF16)
                            for b4 in range(4):
                                blk = c * 4 + b4
                                S.op("pe", lambda pe: pe.transpose(out=ptb[:, b4 * 128:(b4 + 1) * 128], in_=qk[:, blk, i * 64:i * 64 + 128],
                                                                   identity=identb[:]), reads=[bQK[blk], bCONST], writes=[bPT[b]])
                            if i == 0:
                                S.op("act", lambda a: a.activation(out=T4[0:64, 0, c * 512:(c + 1) * 512], in_=ptb[0:64, 0:512], func=AF.Copy),
                                     reads=[bPT[b]], writes=[bT4[0][c]])
                                S.op("dve", lambda v: v.tensor_copy(out=T4[64:128, 1, c * 512:(c + 1) * 512], in_=ptb[64:128, 0:512]),
                                     reads=[bPT[b]], writes=[bT4[1][c]])
                            elif c % 2 == 0:
                                S.op("act", lambda a: a.activation(out=T4[:, i, c * 512:(c + 1) * 512], in_=ptb[:, 0:512], func=AF.Copy),
                                     reads=[bPT[b]], writes=[bT4[i][c]])
                            else:
                                S.op("dve", lambda v: v.tensor_copy(out=T4[:, i, c * 512:(c + 1) * 512], in_=ptb[:, 0:512]),
                                     reads=[bPT[b]], writes=[bT4[i][c]])
                if mid_hook is not None:
                    mid_hook()
                groups = []
                if kind in ("fox", "dil"):
                    def fin_pair(g):
                        hh = g["hh"]
                        c = g["c"]
                        a = g["accO"]
                        r0, r1 = (0, 64) if hh == 0 else (64, 128)
                        d0, d1 = (64, 128) if hh == 0 else (0, 64)
                        f = nxt("f", 4)
                        S.op("act", lambda a_: a_.activation(out=ftmp[r0:r1, f, :], in_=ACC[a][d0:d1, :], func=AF.Ln), reads=[bACC[a]], writes=[bF[f]])
                        S.op("act", lambda a_: a_.activation(out=ftmp[r0:r1, f, :], in_=ftmp[r0:r1, f, :], func=AF.Exp, scale=-1.0), reads=[bF[f]], writes=[bF[f]])
                        S.op("pool", lambda gp: gp.tensor_tensor(out=ftmp[r0:r1, f, :], in0=ftmp[r0:r1, f, :],
                                                                 in1=sgT[r0:r1, c * 512:(c + 1) * 512], op=ALU.mult),
                             reads=[bF[f], bSG[c]], writes=[bF[f]])
                        S.op("dve", lambda v: v.tensor_tensor(out=attnT[r0:r1, u, c * 512:(c + 1) * 512], in0=ACC[a][r0:r1, :],
                                                              in1=ftmp[r0:r1, f, :], op=ALU.mult),
                             reads=[bACC[a], bF[f]], writes=[bAT[u][c]])

                    for c in range(4):
                        for hh in range(2):
                            if kind == "fox":
                                qkspec = (hh, 2 + hh, 0, 70)
                            else:
                                qkspec = (hh, 2, 0, 128)
                            groups.append(dict(c=c, hh=hh, qk=qkspec, scale=(1.0 if kind == "fox" else 0.125),
                                               mask=("causal" if kind == "fox" else "cm"),
                                               vfun=(lambda j, hh=hh: vaug[:, j, hh * 128:(hh + 1) * 128]),
                                               accO=nxt("acc", 2), accD=None, fin=fin_pair))
                else:
                    state = {}

                    def fin_diff(g):
                        m = g["hh"]
                        c = g["c"]
                        o, d = g["accO"], g["accD"]
                        f = nxt("f", 4)
                        S.op("act", lambda a_: a_.activation(out=ftmp[:, f, :], in_=ACC[d][:, :], func=AF.Ln), reads=[bACC[d]], writes=[bF[f]])
                        S.op("act", lambda a_: a_.activation(out=ftmp[:, f, :], in_=ftmp[:, f, :], func=AF.Exp, scale=-1.0), reads=[bF[f]], writes=[bF[f]])
                        S.op("dve", lambda v: v.tensor_tensor(out=ftmp[:, f, :], in0=ACC[o][:, :], in1=ftmp[:, f, :], op=ALU.mult),
                             reads=[bACC[o], bF[f]], writes=[bF[f]])
                        if m == 0:
                            state["f1"] = f
                            return
                        f1 = state["f1"]
                        S.op("dve", lambda v: v.scalar_tensor_tensor(out=ftmp[:, f1, :], in0=ftmp[:, f, :], scalar=lamt[:, 4:5],
                                                                     in1=ftmp[:, f1, :], op0=ALU.mult, op1=ALU.add),
                             reads=[bF[f], bF[f1], bLAM], writes=[bF[f1]])
                        p = 6
                        S.op("pool", lambda gp: gp.tensor_tensor(out=P[:, p, :], in0=ftmp[:, f1, :], in1=ftmp[:, f1, :], op=ALU.mult),
                             reads=[bF[f1]], writes=[bP[p]])
                        S.op("pe", lambda pe: pe.matmul(ACC[d][:, :], lhsT=ones128[:, :], rhs=P[:, p, :], start=True, stop=True),
                             reads=[bP[p], bCONST], writes=[bACC[d]])
                        S.op("act", lambda a: a.activation(out=ftmp[:, f, :], in_=ACC[d][:, :], func=AF.Ln, scale=1.0 / 128.0, bias=EPS),
                             reads=[bACC[d]], writes=[bF[f]])
                        S.op("act", lambda a: a.activation(out=ftmp[:, f, :], in_=ftmp[:, f, :], func=AF.Exp, scale=-0.5),
                             reads=[bF[f]], writes=[bF[f]])
                        S.op("dve", lambda v: v.tensor_tensor(out=ftmp[:, f1, :], in0=ftmp[:, f1, :], in1=ftmp[:, f, :], op=ALU.mult),
                             reads=[bF[f1], bF[f]], writes=[bF[f1]])
                        S.op("dve", lambda v: v.scalar_tensor_tensor(out=attnT[:, u, c * 512:(c + 1) * 512], in0=ftmp[:, f1, :],
                                                                     scalar=wsub[:, 0:1], in1=sgT[:, c * 512:(c + 1) * 512],
                                                                     op0=ALU.mult, op1=ALU.mult),
                             reads=[bF[f1], bSG[c], bLAM], writes=[bAT[u][c]])

                    for c in range(4):
                        for m in range(2):
                            groups.append(dict(c=c, hh=m, qk=(m, 2, 0, 128), scale=0.125, mask="causal",
                                               vfun=(lambda j: vd[:, j, :]), accO=0, accD=1, fin=fin_diff))
                emit_attention(groups, LA_KIND[kind])

            def emit_even_prep(li):
                S.dma(lambda q: q.dma_start(out=lamrep[:], in_=lamrep_in[li]), writes=[bLAM])
                S.dma(lambda q: q.dma_start(out=lamt[:, 6:8], in_=lamc_in[li]), writes=[bLAM])
                S.dma(lambda q: q.dma_start(out=wsub[:], in_=subln_in[li]), writes=[bLAM])
                S.dma(lambda q: q.dma_start(out=negb[:], in_=bfc_in[li]), writes=[bLAM])
                S.dma(lambda q: q.dma_start(out=wfst[:], in_=wf_in[li]), writes=[bWF])
                S.op("dve", lambda v: v.tensor_copy(out=wfb[:], in_=wfst[:]), reads=[bWF], writes=[bWF])
                lr = lamrep[:, :].rearrange("p (a d) -> p a d", a=4)
                S.op("dve", lambda v: v.tensor_tensor(out=lamrep[:, 0:64], in0=lr[:, 0, :], in1=lr[:, 1, :], op=ALU.mult),
                     reads=[bLAM], writes=[bLAM])
                S.op("dve", lambda v: v.tensor_tensor(out=lamrep[:, 128:192], in0=lr[:, 2, :], in1=lr[:, 3, :], op=ALU.mult),
                     reads=[bLAM], writes=[bLAM])
                S.op("dve", lambda v: v.reduce_sum(out=lamt[:, 0:1], in_=lamrep[:, 0:64], axis=AX.X), reads=[bLAM], writes=[bLAM])
                S.op("dve", lambda v: v.reduce_sum(out=lamt[:, 1:2], in_=lamrep[:, 128:192], axis=AX.X), reads=[bLAM], writes=[bLAM])
                S.op("act", lambda a: a.activation(out=lamt[:, 2:4], in_=lamt[:, 0:2], func=AF.Exp), reads=[bLAM], writes=[bLAM])
                S.op("dve", lambda v: v.tensor_tensor(out=lamt[:, 4:5], in0=lamt[:, 3:4], in1=lamt[:, 2:3], op=ALU.subtract),
                     reads=[bLAM], writes=[bLAM])
                S.op("dve", lambda v: v.tensor_tensor(out=lamt[:, 4:5], in0=lamt[:, 4:5], in1=lamt[:, 6:7], op=ALU.add),
                     reads=[bLAM], writes=[bLAM])
                S.op("dve", lambda v: v.tensor_tensor(out=wsub[:], in0=wsub[:], in1=lamt[:, 7:8], op=ALU.mult),
                     reads=[bLAM], writes=[bLAM])
                S.op("dve", lambda v: v.tensor_scalar(out=negb[:], in0=negb[:], scalar1=-1.0, scalar2=None, op0=ALU.mult),
                     reads=[bLAM], writes=[bLAM])
                fza = xo[0:8, :, :].rearrange("p a d -> p (a d)")
                fzb = xw[0:8, :, :].rearrange("p a d -> p (a d)")
                for c in range(4):
                    b = nxt("pt", 2)
                    for kc in range(KC):
                        S.op("pe", lambda pe: pe.matmul(PT[b][0:8, :], lhsT=wfb[:, kc, :], rhs=hT[:, kc, c * 512:(c + 1) * 512],
                                                        start=(kc == 0), stop=(kc == 7)),
                             reads=bH[4 * c:4 * c + 4] + [bWF], writes=[bPT[b]])
                    S.op("act", lambda a: a.activation(out=fza[:, c * 512:(c + 1) * 512], in_=PT[b][0:8, :], func=AF.Exp,
                                                       scale=-1.0, bias=negb[0:8, 0:1]),
                         reads=[bPT[b], bLAM], writes=bXO)
                S.op("act", lambda a: a.activation(out=fza, in_=fza, func=AF.Ln, bias=1.0), reads=bXO, writes=bXO)
                ones8 = P[0:8, 0:4, :].rearrange("p a d -> p (a d)")
                S.op("dve", lambda v: v.memset(ones8, 1.0), writes=bP[0:4])
                S.op("dve", lambda v: v.tensor_tensor_scan(out=fzb, data0=ones8, data1=fza, initial=0.0, op0=ALU.mult, op1=ALU.add),
                     reads=bXO + bP[0:4], writes=bXW)
                b = nxt("pt", 2)
                for blk in range(NB):
                    S.op("pe", lambda pe: pe.transpose(out=PT[b][:, blk * 8:(blk + 1) * 8], in_=fzb[:, blk * 128:(blk + 1) * 128],
                                                       identity=identf[0:8, 0:8]), reads=bXW + [bCONST], writes=[bPT[b]])
                S.op("dve", lambda v: v.tensor_copy(out=cs_tok[:], in_=PT[b][:, 0:128]), reads=[bPT[b]], writes=[bSPL])
                S.op("dve", lambda v: v.tensor_copy(out=hml[:, 0, :], in_=cs_tok[:]), reads=[bSPL], writes=[bSPL])
                S.op("dve", lambda v: v.tensor_tensor(out=r_tok[:], in0=cs_tok[:], in1=hml[:, 0, :], op=ALU.subtract),
                     reads=[bSPL], writes=[bSPL])
                S.op("dve", lambda v: v.tensor_copy(out=hml[:, 1, :], in_=r_tok[:]), reads=[bSPL], writes=[bSPL])
                S.op("dve", lambda v: v.tensor_tensor(out=r_tok[:], in0=r_tok[:], in1=hml[:, 1, :], op=ALU.subtract),
                     reads=[bSPL], writes=[bSPL])
                S.op("dve", lambda v: v.tensor_copy(out=hml[:, 2, :], in_=r_tok[:]), reads=[bSPL], writes=[bSPL])
                S.op("dve", lambda v: v.memset(FQ[:], 1.0), writes=[bFQK])
                S.op("dve", lambda v: v.memset(FK[:], 1.0), writes=[bFQK])
                fq3 = FQ[:, :].rearrange("p (n r) -> p n r", r=6)
                fk3 = FK[:, :].rearrange("p (n r) -> p n r", r=6)
                for r in range(3):
                    S.op("dve", lambda v: v.tensor_scalar(out=fq3[:, :, r], in0=hml[:, r, :], scalar1=-1.0, scalar2=None, op0=ALU.mult),
                         reads=[bSPL], writes=[bFQK])
                    S.op("dve", lambda v: v.tensor_copy(out=fk3[:, :, 3 + r], in_=hml[:, r, :]), reads=[bSPL], writes=[bFQK])


            def emit_out_block(li, blk, x_src, x_dst, bsrc_list, bdst_list, has_next):
                par = li % 2
                xsl = blk % 2
                YB = [ACC[0], ACC[1]] if blk % 2 == 0 else [ST[0], ST[1]]
                bYB = [bACC[0], bACC[1]] if blk % 2 == 0 else [bSTb[0], bSTb[1]]
                for half in range(2):
                    for kc in range(KC):
                        S.op("pe", lambda pe: pe.matmul(YB[half][:, :], lhsT=attnT[:, kc, blk * 128:(blk + 1) * 128],
                                                        rhs=wbf[:, half, kc, :], start=(kc == 0), stop=(kc == 7)),
                             reads=[bAT[kc][blk // 4], bWB[half][kc // 2]], writes=[bYB[half]])
                s = nxt("stat", 8)
                for half in range(2):
                    S.op("act", lambda a: a.activation(out=junk[:, half * 512:(half + 1) * 512], in_=YB[half][:, :], func=AF.Square,
                                                       accum_out=stat[:, s, half:half + 1]),
                         reads=[bYB[half]], writes=[bJ, bJ2, bST_[s]])
                S.op("dve", lambda v: v.tensor_tensor(out=stat[:, s, 2:3], in0=stat[:, s, 0:1], in1=stat[:, s, 1:2], op=ALU.add),
                     reads=[bST_[s]], writes=[bST_[s]])
                emit_rstd(s, 2, 3, float(D))
                for half in range(2):
                    S.op("dve", lambda v: v.scalar_tensor_tensor(out=xw[:, xsl, half * 512:(half + 1) * 512], in0=YB[half][:, :],
                                                                 scalar=stat[:, s, 3:4], in1=Gbc[:, half * 512:(half + 1) * 512],
                                                                 op0=ALU.mult, op1=ALU.mult),
                         reads=[bYB[half], bST_[s], bG], writes=[bXW[xsl]])
                for half in range(2):
                    eng = "pool" if half == 0 else "dve"
                    S.op(eng, lambda gp: gp.tensor_tensor(out=xw[:, xsl, half * 512:(half + 1) * 512], in0=xw[:, xsl, half * 512:(half + 1) * 512],
                                                          in1=xo[:, xsl, half * 512:(half + 1) * 512], op=ALU.add),
                         reads=[bXW[xsl], bXO[xsl]], writes=[bXW[xsl]])
                S.dma(lambda q: q.dma_start(out=x_dst[blk * 128:(blk + 1) * 128, :], in_=xw[:, xsl, :]),
                      reads=[bXW[xsl]], writes=[bdst_list[blk]])

            bXIN = [Buf() for _ in range(NB)]
            bOUT = [Buf() for _ in range(NB)]
            for j in range(6):
                emit_mod_load(0, j, j % 2, ("dve", "act", "dve", "pool"))
                emit_mod_mm(0, j, j % 2)
            emit_g_bcast(0)
            def p1_stage1(blk):
                sl = blk % 2
                S.dma(lambda q: q.dma_start(out=xw[:, sl, :], in_=x_in[blk * 128:(blk + 1) * 128, :]), writes=[bXW[sl]])
                emit_h1(0, blk, xw[:, sl, :], bXW[sl], xo[:, sl, :], bXO[sl])
            p1_stage1(0)
            for blk in range(NB):
                if blk + 1 < NB:
                    p1_stage1(blk + 1)
                emit_h2(0, blk, xo[:, blk % 2, :], bXO[blk % 2])

            for li, kd in enumerate(kinds):
                first = li == 0
                last = li == n - 1
                x_src, bsrc = (x_in, bXIN) if first else (xs, bXS)
                x_dst, bdst = (out, bOUT) if last else (xs, bXS)
                if kd == "e":
                    emit_even_prep(li)
                    ukinds = ["fox"] * 4 + ["diff"] * 4
                else:
                    ukinds = ["dil"] * 8
                emit_wload(lambda qtr: wu_in[li, 0, qtr], 0)
                for u in range(8):
                    if u < 7:
                        nw = (lambda u=u: emit_wload(lambda qtr: wu_in[li, u + 1, qtr], (u + 1) % 2))
                    else:
                        nw = (lambda: emit_wload(lambda qtr: wo_in[li, 0, qtr], 0))
                    mid = post = None
                    if (not last) and u < 6:
                        mid = (lambda u=u: emit_mod_load(li + 1, u, u % 2))
                        post = (lambda u=u: emit_mod_mm(li + 1, u, u % 2))
                    emit_unit(li, u, ukinds[u], nw, mid)
                    if post is not None:
                        post()
                emit_wload(lambda qtr: wo_in[li, 1, qtr], 1)
                def ld_xo(blk):
                    S.dma(lambda q: q.dma_start(out=xo[:, blk % 2, :], in_=x_src[blk * 128:(blk + 1) * 128, :]),
                          reads=[bsrc[blk]], writes=[bXO[blk % 2]])
                ld_xo(0)
                ld_xo(1)
                emit_out_block(li, 0, x_src, x_dst, bsrc, bdst, False)
                for blk in range(NB):
                    if blk + 1 < NB:
                        emit_out_block(li, blk + 1, x_src, x_dst, bsrc, bdst, False)
                    if not last:
                        xsl = blk % 2
                        emit_h_block(li + 1, blk, xw[:, xsl, :], bXW[xsl], xo[:, xsl, :], bXO[xsl])
                    if blk + 2 < NB:
                        ld_xo(blk + 2)
                if not last:
                    emit_g_bcast(li + 1)
        program()
        S.finish_plan()
        program()
        for i in range(len(S.dsem)):
            if S.dcnt[i]:
                nc.sync.wait_ge(S.dsem[i], S.dcnt[i])
    return nc


def _pieces(w512):
    return np.ascontiguousarray(w512.reshape(4, 2, 128, 512).transpose(0, 2, 1, 3))


def _prep_shared(inp):
    depth = 4
    f32 = np.float32
    adaw = np.zeros((depth, 6, 4, 128, 2, 512), f32)
    colp = np.zeros((depth, 128, 40), f32)
    for l in range(depth):
        for j in range(6):
            adaw[l, j] = _pieces(inp["ada_w"][l][:, j * 512:(j + 1) * 512])
        colp[l, :, 0:24] = inp["ada_b"][l].reshape(24, 128).T
        colp[l, :, 24:32] = inp["norm_pre"][l].reshape(8, 128).T
        colp[l, :, 32:40] = inp["norm_post"][l].reshape(8, 128).T
    wu = np.zeros((depth, 8, 4, 128, 2, 512), f32)
    wo = np.zeros((depth, 2, 4, 128, 2, 512), f32)
    wf = np.zeros((depth, 128, KC, 8), f32)
    bfc = np.zeros((depth, 8, 1), f32)
    lamrep = np.zeros((depth, 128, 256), f32)
    sublnc = np.zeros((depth, 128, 1), f32)
    lamc = np.zeros((depth, 128, 2), f32)
    for l in range(depth):
        if l % 2 == 0:
            i = l // 2
            W = inp["ev_w_in"][i]
            for u in range(8):
                if u < 4:
                    offs = [128 * u, 512 + 128 * u, 1024 + 128 * u, 1544 + 128 * u]
                else:
                    d = u - 4
                    offs = [2056 + 128 * d, 2568 + 128 * d, 3080 + 128 * d, 3592 + 128 * d]
                w512 = np.concatenate([W[:, o:o + 128] for o in offs], axis=1)
                wu[l, u] = _pieces(w512)
            wf[l] = W[:, 1536:1544].reshape(KC, 128, 8).transpose(1, 0, 2)
            bfc[l, :, 0] = inp["ev_b_forget"][i]
            lamrep[l] = np.broadcast_to(np.concatenate([inp["ev_lambda_q1"][i], inp["ev_lambda_k1"][i],
                                                        inp["ev_lambda_q2"][i], inp["ev_lambda_k2"][i]])[None, :], (128, 256))
            sublnc[l, :, 0] = inp["ev_subln"][i]
            li0 = lam_init_of(l)
            lamc[l, :, 0] = -li0
            lamc[l, :, 1] = 1.0 - li0
            Wo = inp["ev_w_out"][i]
        else:
            j = l // 2
            W = inp["od_w_in"][j]
            for u in range(8):
                offs = [128 * u, 1024 + 128 * u, 2048 + 128 * u, 3072 + 128 * u]
                w512 = np.concatenate([W[:, o:o + 128] for o in offs], axis=1)
                wu[l, u] = _pieces(w512)
            Wo = inp["od_w_out"][j]
        for half in range(2):
            wo[l, half] = _pieces(Wo[:, half * 512:(half + 1) * 512])
    inv_freq = (1.0 / (THETA ** (np.arange(0, 16, 2, dtype=np.float32) / np.float32(16)))).astype(np.float32)
    cst = np.zeros((128, 32), f32)
    cst[:, 0:8] = inv_freq / (2.0 * np.pi)
    cst[:, 8:16] = inv_freq / (2.0 * np.pi)
    cst[:, 24:32] = 0.25
    k = np.arange(128)[:, None]
    q = np.arange(2048)[None, :]
    dlt = q - k
    cmv = ((dlt >= 0) & (dlt <= 128)).astype(np.float32) + ((dlt >= 0) & (dlt % 4 == 0) & (dlt <= 512)).astype(np.float32) \
        + ((dlt >= 0) & (dlt % 16 == 0)).astype(np.float32)
    cmm = cmv.astype(ml_dtypes.bfloat16)
    return dict(adaw=adaw, colp=colp, wu=wu, wo=wo, wf=wf, bfc=bfc, lamrep=lamrep, sublnc=sublnc, lamc=lamc, cst=cst, cm=cmm)


_LAYER_KEYS = ["adaw", "colp", "wu", "wo", "wf", "bfc", "lamrep", "sublnc", "lamc"]
_PROG_CACHE = {}


def _get_prog(kinds):
    key = tuple(kinds)
    if key not in _PROG_CACHE:
        _PROG_CACHE[key] = build(list(kinds))
    return _PROG_CACHE[key]


LAUNCH_GROUPS = [[0, 1, 2, 3]]


def kernel(**inputs):
    inp = {k: np.asarray(v) for k, v in inputs.items()}
    sh = _prep_shared(inp)
    x = np.ascontiguousarray(inp["x"].astype(np.float32, copy=False))
    pos = inp["positions"].astype(np.int32)
    c = inp["c"].astype(np.float32)
    cur = [x[b] for b in range(N_CORES)]
    for grp in LAUNCH_GROUPS:
        kinds = ["e" if l % 2 == 0 else "o" for l in grp]
        nc = _get_prog(kinds)
        in_maps = []
        for b in range(N_CORES):
            m = {"x": np.ascontiguousarray(cur[b]),
                 "pos": np.ascontiguousarray(pos[b].reshape(NB, 128).T),
                 "ccol": np.ascontiguousarray(c[b].reshape(KC, 128).T),
                 "cst": sh["cst"], "cm": sh["cm"]}
            for kk in _LAYER_KEYS:
                m[kk] = np.ascontiguousarray(sh[kk][grp[0]:grp[-1] + 1])
            in_maps.append(m)
        res = run_bass_kernel_spmd(nc, in_maps, core_ids=list(range(N_CORES)))
        cur = [np.asarray(res.results[b]["out"]) for b in range(N_CORES)]
    return np.stack(cur, axis=0).astype(np.float32)
```

```python
import math
import os
from contextlib import ExitStack

import numpy as np
import ml_dtypes
import concourse.bass as bass
import concourse.mybir as mybir
from concourse.bass_utils import run_bass_kernel_spmd

F32 = mybir.dt.float32
BF16 = mybir.dt.bfloat16
I32 = mybir.dt.int32
AF = mybir.ActivationFunctionType
ALU = mybir.AluOpType
AX = mybir.AxisListType

S_LEN = 2048
D = 1024
NB = 16
KC = 8
EPS = 1e-6
N_CORES = 8
THETA = 500000.0


ALL_BUFS = []


class Buf:
    __slots__ = ("w", "r")

    def __init__(self):
        self.w = None
        self.r = {}
        ALL_BUFS.append(self)


class Sched:
    def __init__(self, nc, stack, n_dma_sems=16):
        self.nc = nc
        self.eng = {"pe": nc.tensor, "dve": nc.vector, "act": nc.scalar, "pool": nc.gpsimd, "sp": nc.sync}
        self.sem = {k: stack.enter_context(nc.semaphore("s_" + k)) for k in self.eng}
        self.dsem = [stack.enter_context(nc.semaphore("d%d" % i)) for i in range(n_dma_sems)]
        self.semobj = {}
        for k in self.eng:
            self.semobj[("e", k)] = self.sem[k]
        for i in range(n_dma_sems):
            self.semobj[("d", i)] = self.dsem[i]
        self.plan = True
        self.targets = {("e", k): set() for k in self.eng}
        self.rank = {}
        self.reset()

    def reset(self):
        self.cnt = {k: 0 for k in self.eng}
        self.seen = {k: {} for k in self.eng}
        self.dcnt = [0] * len(self.dsem)
        self.dnext = 0
        for b in ALL_BUFS:
            b.w = None
            b.r = {}

    def finish_plan(self):
        self.plan = False
        for k, tg in self.targets.items():
            self.rank[k] = {v: i + 1 for i, v in enumerate(sorted(tg))}
        self.reset()

    def _waits(self, e, reads, writes, skip_self=False):
        need = {}

        def add(k, v):
            if k == ("e", "pe") and e == "pe":
                return
            if skip_self and k == ("e", e):
                return
            if need.get(k, 0) < v:
                need[k] = v

        for b in reads:
            if b.w is not None:
                add(*b.w)
        for b in writes:
            if b.w is not None:
                add(*b.w)
            for k, v in b.r.items():
                add(k, v)
        h = self.eng[e]
        seen = self.seen[e]
        for k, v in need.items():
            if seen.get(k, 0) >= v:
                continue
            seen[k] = v
            if k[0] == "e":
                if self.plan:
                    self.targets[k].add(v)
                else:
                    h.wait_ge(self.semobj[k], self.rank[k][v])
            elif not self.plan:
                h.wait_ge(self.semobj[k], v)

    def _commit(self, t, reads, writes):
        k, v = t
        for b in reads:
            if b.r.get(k, 0) < v:
                b.r[k] = v
        for b in writes:
            b.w = t
            b.r = {}

    def op(self, e, fn, reads=(), writes=(), skip_self=False):
        self._waits(e, reads, writes, skip_self)
        self.cnt[e] += 1
        k = ("e", e)
        if not self.plan:
            ins = fn(self.eng[e])
            if self.cnt[e] in self.rank[k]:
                ins.then_inc(self.sem[e], 1)
        t = (k, self.cnt[e])
        self._commit(t, reads, writes)
        return t

    def dma(self, fn, reads=(), writes=(), q="sp"):
        i = self.dnext
        self.dnext = (self.dnext + 1) % len(self.dsem)
        h = self.eng[q]
        k = ("d", i)
        if self.dcnt[i] > 0 and self.seen[q].get(k, 0) < self.dcnt[i]:
            if not self.plan:
                h.wait_ge(self.dsem[i], self.dcnt[i])
            self.seen[q][k] = self.dcnt[i]
        self._waits(q, reads, writes)
        self.dcnt[i] += 16
        if not self.plan:
            ins = fn(h)
            ins.then_inc(self.dsem[i], 16)
        t = (k, self.dcnt[i])
        self._commit(t, reads, writes)
        return t


def lam_init_of(layer_idx):
    return 0.8 - 0.6 * math.exp(-0.3 * layer_idx)


def build(kinds):
    n = len(kinds)
    nc = bass.Bass("TRN2", target_bir_lowering=False)

    def dram(name, shape, dt, kind):
        return nc.dram_tensor(name, shape, dt, kind=kind).ap()

    x_in = dram("x", [S_LEN, D], F32, "ExternalInput")
    pos_in = dram("pos", [128, NB], I32, "ExternalInput")
    ccol_in = dram("ccol", [128, KC], F32, "ExternalInput")
    adaw_in = dram("adaw", [n, 6, 4, 128, 2, 512], F32, "ExternalInput")
    colp_in = dram("colp", [n, 128, 40], F32, "ExternalInput")
    wu_in = dram("wu", [n, 8, 4, 128, 2, 512], F32, "ExternalInput")
    wo_in = dram("wo", [n, 2, 4, 128, 2, 512], F32, "ExternalInput")
    wf_in = dram("wf", [n, 128, KC, 8], F32, "ExternalInput")
    bfc_in = dram("bfc", [n, 8, 1], F32, "ExternalInput")
    lamrep_in = dram("lamrep", [n, 128, 256], F32, "ExternalInput")
    subln_in = dram("sublnc", [n, 128, 1], F32, "ExternalInput")
    lamc_in = dram("lamc", [n, 128, 2], F32, "ExternalInput")
    cst_in = dram("cst", [128, 32], F32, "ExternalInput")
    cm_in = dram("cm", [128, 2048], BF16, "ExternalInput")
    out = dram("out", [S_LEN, D], F32, "ExternalOutput")
    xs = dram("xs", [S_LEN, D], F32, "Internal")
    gsc = dram("gsc", [n, 8, 128], F32, "Internal")

    with ExitStack() as st:
        S = Sched(nc, st)

        def sb(name, shape, dt):
            return st.enter_context(nc.sbuf_tensor("sb_" + name, shape, dt))

        def ps(name, shape, dt):
            return st.enter_context(nc.psum_tensor("ps_" + name, shape, dt))

        hT = sb("hT", [128, KC, S_LEN], BF16)
        bH = [Buf() for _ in range(NB)]
        attnT = sb("attnT", [128, 8, S_LEN], BF16)
        bAT = [[Buf() for _ in range(4)] for _ in range(8)]
        wbf = sb("wbf", [128, 2, KC, 512], BF16)
        bWB = [[Buf() for _ in range(4)] for _ in range(2)]
        wst = sb("wst", [128, 2, 2, 512], F32)
        bWS = [Buf(), Buf()]
        qk = sb("qk", [128, NB, 280], BF16)
        bQK = [Buf() for _ in range(NB)]
        Rf = sb("Rf", [128, NB, 4, 16], F32)
        bRf = [Buf() for _ in range(NB)]
        T4 = sb("T4", [128, 4, S_LEN], BF16)
        bT4 = [[Buf() for _ in range(4)] for _ in range(4)]
        vaug = sb("vaug", [128, NB, 256], BF16)
        vd = sb("vd", [128, NB, 128], BF16)
        bV = [Buf() for _ in range(NB)]
        sgT = sb("sgT", [128, S_LEN], BF16)
        bSG = [Buf() for _ in range(4)]
        NP = 7
        P = sb("P", [128, NP, 512], BF16)
        bP = [Buf() for _ in range(NP)]
        junk = P[:, 0:2, :].rearrange("p a d -> p (a d)")
        bJ = bP[0]
        bJ2 = bP[1]
        ftmp = sb("ftmp", [128, 4, 512], F32)
        bF = [Buf() for _ in range(4)]
        cm = sb("cm", [128, 2048], BF16)
        bCM = Buf()
        ones128 = sb("ones128", [128, 128], BF16)
        identb = sb("identb", [128, 128], BF16)
        identf = sb("identf", [128, 128], F32)
        bCONST = Buf()
        cst = sb("cst", [128, 32], F32)
        CS = sb("CS", [128, NB, 16], F32)
        SN = sb("SN", [128, NB, 16], F32)
        bROPE = Buf()
        Gbc = sb("Gbc", [128, D], F32)
        bG = Buf()
        ASc = sb("ASc", [128, 2, 16], F32)
        bAS = [Buf(), Buf()]
        modc = sb("modc", [128, 32], F32)
        bMOD = Buf()
        colp = sb("colp", [128, 40], F32)
        bCOLP = Buf()
        grow = sb("grow", [8, 128], F32)
        bGROW = Buf()
        bGSC = [Buf() for _ in range(n)]
        condf = sb("condf", [128, KC], F32)
        condb = sb("condb", [128, KC, 2], BF16)
        bCOND = Buf()
        FQ = sb("FQ", [128, 768], BF16)
        FK = sb("FK", [128, 768], BF16)
        bFQK = Buf()
        cs_tok = sb("cs_tok", [128, 128], F32)
        r_tok = sb("r_tok", [128, 128], F32)
        hml = sb("hml", [128, 3, 128], BF16)
        bSPL = Buf()
        xo = sb("xo", [128, 2, D], F32)
        bXO = [Buf(), Buf()]
        xw = sb("xw", [128, 2, D], F32)
        bXW = [Buf(), Buf()]
        stat = sb("stat", [128, 8, 8], F32)
        bST_ = [Buf() for _ in range(8)]
        lamrep = sb("lamrep", [128, 256], F32)
        lamt = sb("lamt", [128, 8], F32)
        wsub = sb("wsub", [128, 1], F32)
        negb = sb("negb", [8, 1], F32)
        wfst = sb("wfst", [128, KC, 8], F32)
        wfb = sb("wfb", [128, KC, 8], BF16)
        bLAM = Buf()
        bWF = Buf()
        posi = sb("posi", [128, NB], I32)
        posf = sb("posf", [128, NB], F32)
        Yt = sb("Yt", [128, NB, 16], F32)
        Yi = sb("Yi", [128, NB, 16], I32)
        PT = [ps("PT%d" % i, [128, 512], F32) for i in range(2)]
        bPT = [Buf(), Buf()]
        NST = 4
        NACC = 2
        ST = [ps("ST%d" % i, [128, 512], F32) for i in range(NST)]
        bSTb = [Buf() for _ in range(NST)]
        ACC = [ps("ACC%d" % i, [128, 512], F32) for i in range(NACC)]
        bACC = [Buf() for _ in range(NACC)]
        bXS = [Buf() for _ in range(NB)]

        LA_KIND = {'fox': int(os.environ.get('F_LA', '2')), 'diff': int(os.environ.get('F_LA', '2')), 'dil': int(os.environ.get('F_LA', '2'))}
        cnt = {"wst": 0, "stat": 0, "pt": 0, "st": 0, "p": 0, "f": 0, "acc": 0, "x": 0, "tb": 0}

        def nxt(key, mod):
            v = cnt[key] % mod
            cnt[key] += 1
            return v

        def program():
            for k_ in cnt:
                cnt[k_] = 0
            S.op("dve", lambda v: v.memset(ones128[:], 1.0), writes=[bCONST])
            S.op("dve", lambda v: v.memset(identf[:], 1.0), writes=[bCONST])
            S.op("pool", lambda g: g.affine_select(out=identf[:], in_=identf[:], pattern=[[1, 128]], compare_op=ALU.is_equal,
                                                   fill=0.0, base=0, channel_multiplier=-1), reads=[bCONST], writes=[bCONST])
            S.op("dve", lambda v: v.tensor_copy(out=identb[:], in_=identf[:]), reads=[bCONST], writes=[bCONST])
            S.op("dve", lambda v: v.memset(vaug[:], 1.0), writes=bV)
            S.op("pool", lambda g: g.memset(T4[64:128, 0, :], 0.0), writes=bT4[0])
            S.op("pool", lambda g: g.memset(T4[0:64, 1, :], 0.0), writes=bT4[1])
            S.dma(lambda q: q.dma_start(out=cm[:], in_=cm_in[:, :]), writes=[bCM])
            S.dma(lambda q: q.dma_start(out=cst[:], in_=cst_in[:, :]), writes=[bROPE])
            S.dma(lambda q: q.dma_start(out=posi[:], in_=pos_in[:, :]), writes=[bROPE])
            S.dma(lambda q: q.dma_start(out=condf[:], in_=ccol_in[:, :]), writes=[bCOND])
            S.op("act", lambda a: a.activation(out=condf[:], in_=condf[:], func=AF.Silu), reads=[bCOND], writes=[bCOND])
            S.op("dve", lambda v: v.tensor_copy(out=condb[:, :, 0], in_=condf[:]), reads=[bCOND], writes=[bCOND])
            S.op("dve", lambda v: v.tensor_copy(out=condb[:, :, 1], in_=condf[:]), reads=[bCOND], writes=[bCOND])
            S.op("dve", lambda v: v.tensor_copy(out=posf[:], in_=posi[:]), reads=[bROPE], writes=[bROPE])
            S.op("dve", lambda v: v.tensor_tensor(out=Yt[:], in0=posf[:].unsqueeze(2).to_broadcast([128, NB, 16]),
                                                  in1=cst[:, 0:16].unsqueeze(1).to_broadcast([128, NB, 16]), op=ALU.mult),
                 reads=[bROPE], writes=[bROPE])
            S.op("dve", lambda v: v.tensor_tensor(out=Yt[:], in0=Yt[:], in1=cst[:, 16:32].unsqueeze(1).to_broadcast([128, NB, 16]),
                                                  op=ALU.add), reads=[bROPE], writes=[bROPE])
            S.op("dve", lambda v: v.tensor_copy(out=Yi[:], in_=Yt[:]), reads=[bROPE], writes=[bROPE])
            S.op("dve", lambda v: v.tensor_copy(out=SN[:], in_=Yi[:]), reads=[bROPE], writes=[bROPE])
            S.op("dve", lambda v: v.tensor_tensor(out=Yt[:], in0=Yt[:], in1=SN[:], op=ALU.subtract), reads=[bROPE], writes=[bROPE])
            S.op("act", lambda a: a.activation(out=Yt[:], in_=Yt[:], func=AF.Sin, scale=2.0 * math.pi * (1.0 - 1e-6)),
                 reads=[bROPE], writes=[bROPE])
            S.op("dve", lambda v: v.tensor_copy(out=CS[:, :, 0:8], in_=Yt[:, :, 8:16]), reads=[bROPE], writes=[bROPE])
            S.op("dve", lambda v: v.tensor_copy(out=CS[:, :, 8:16], in_=Yt[:, :, 8:16]), reads=[bROPE], writes=[bROPE])
            S.op("dve", lambda v: v.tensor_scalar(out=SN[:, :, 0:8], in0=Yt[:, :, 0:8], scalar1=-1.0, scalar2=None, op0=ALU.mult),
                 reads=[bROPE], writes=[bROPE])
            S.op("dve", lambda v: v.tensor_copy(out=SN[:, :, 8:16], in_=Yt[:, :, 0:8]), reads=[bROPE], writes=[bROPE])

            def emit_mod_load(li, j, wslot, engs=("pool",)):
                emit_wload(lambda qtr: adaw_in[li, j, qtr], wslot, engs)

            def emit_mod_mm(li, j, wslot):
                if j == 0:
                    S.dma(lambda q: q.dma_start(out=colp[:], in_=colp_in[li]), writes=[bCOLP])
                b = nxt("pt", 2)
                for m in range(4):
                    for kc in range(KC):
                        S.op("pe", lambda pe: pe.matmul(PT[b][:, 2 * m:2 * m + 2], lhsT=wbf[:, wslot, kc, m * 128:(m + 1) * 128],
                                                        rhs=condb[:, kc, :], start=(kc == 0), stop=(kc == 7)),
                             reads=[bWB[wslot][kc // 2], bCOND], writes=[bPT[b]])
                S.op("dve", lambda v: v.tensor_tensor(out=modc[:, 4 * j:4 * j + 4],
                                                      in0=PT[b][:, 0:8].rearrange("p (i two) -> p i two", two=2)[:, :, 0],
                                                      in1=colp[:, 4 * j:4 * j + 4], op=ALU.add),
                     reads=[bPT[b], bCOLP], writes=[bMOD])
                if j == 5:
                    par = li % 2
                    S.op("dve", lambda v: v.scalar_tensor_tensor(out=ASc[:, par, 0:8], in0=modc[:, 8:16], scalar=1.0,
                                                                 in1=colp[:, 24:32], op0=ALU.add, op1=ALU.mult),
                         reads=[bMOD, bCOLP], writes=[bAS[par]])
                    S.op("dve", lambda v: v.tensor_copy(out=ASc[:, par, 8:16], in_=modc[:, 0:8]), reads=[bMOD], writes=[bAS[par]])
                    S.op("dve", lambda v: v.tensor_tensor(out=modc[:, 24:32], in0=modc[:, 16:24], in1=colp[:, 32:40], op=ALU.mult),
                         reads=[bMOD, bCOLP], writes=[bMOD])
                    b2 = nxt("pt", 2)
                    S.op("pe", lambda pe: pe.transpose(out=PT[b2][0:8, 0:128], in_=modc[:, 24:32], identity=identf[:]),
                         reads=[bMOD, bCONST], writes=[bPT[b2]])
                    S.op("dve", lambda v: v.tensor_copy(out=grow[0:8, :], in_=PT[b2][0:8, 0:128]), reads=[bPT[b2]], writes=[bGROW])
                    S.dma(lambda q: q.dma_start(out=gsc[li], in_=grow[0:8, :]), reads=[bGROW], writes=[bGSC[li]])

            def emit_g_bcast(li):
                S.dma(lambda q: q.dma_start(out=Gbc[:, :], in_=gsc[li:li + 1].rearrange("o a b -> o (a b)").to_broadcast([128, D])),
                      reads=[bGSC[li]], writes=[bG])

            def emit_rstd(s, src, dst, count):
                S.op("act", lambda a: a.activation(out=stat[:, s, dst:dst + 1], in_=stat[:, s, src:src + 1], func=AF.Ln,
                                                   scale=1.0 / count, bias=EPS), reads=[bST_[s]], writes=[bST_[s]])
                S.op("act", lambda a: a.activation(out=stat[:, s, dst:dst + 1], in_=stat[:, s, dst:dst + 1], func=AF.Exp,
                                                   scale=-0.5), reads=[bST_[s]], writes=[bST_[s]])

            def emit_h_block(li, blk, xsrc, bsrc, xdst, bdst):
                par = li % 2
                s = nxt("stat", 8)
                S.op("act", lambda a: a.activation(out=junk, in_=xsrc, func=AF.Square, accum_out=stat[:, s, 0:1]),
                     reads=[bsrc], writes=[bJ, bJ2, bST_[s]])
                emit_rstd(s, 0, 1, float(D))
                S.op("dve", lambda v: v.tensor_scalar(out=xdst, in0=xsrc, scalar1=stat[:, s, 1:2], scalar2=None, op0=ALU.mult),
                     reads=[bsrc, bST_[s]], writes=[bdst])
                for half in range(2):
                    b = nxt("pt", 2)
                    for k4 in range(4):
                        kc = half * 4 + k4
                        S.op("pe", lambda pe: pe.transpose(out=PT[b][:, k4 * 128:(k4 + 1) * 128], in_=xdst[:, kc * 128:(kc + 1) * 128],
                                                           identity=identf[:]), reads=[bdst, bCONST], writes=[bPT[b]])
                    for k4 in range(4):
                        kc = half * 4 + k4
                        if kc % 2 == 0:
                            S.op("act", lambda a: a.activation(out=hT[:, kc, blk * 128:(blk + 1) * 128], in_=PT[b][:, k4 * 128:(k4 + 1) * 128],
                                                               func=AF.Identity, scale=ASc[:, par, kc:kc + 1], bias=ASc[:, par, 8 + kc:9 + kc]),
                                 reads=[bPT[b], bAS[par]], writes=[bH[blk]])
                        else:
                            S.op("dve", lambda v: v.tensor_scalar(out=hT[:, kc, blk * 128:(blk + 1) * 128], in0=PT[b][:, k4 * 128:(k4 + 1) * 128],
                                                                  scalar1=ASc[:, par, kc:kc + 1], scalar2=ASc[:, par, 8 + kc:9 + kc],
                                                                  op0=ALU.mult, op1=ALU.add),
                                 reads=[bPT[b], bAS[par]], writes=[bH[blk]])

            def emit_wload(src_fn, wslot, engs=("pool",)):
                for qtr in range(4):
                    sl = nxt("wst", 2)
                    S.dma(lambda q: q.dma_start(out=wst[:, sl], in_=src_fn(qtr)), writes=[bWS[sl]])
                    eng = engs[qtr % len(engs)]
                    if eng == "act":
                        S.op("act", lambda a: a.activation(out=wbf[:, wslot, 2 * qtr:2 * qtr + 2, :], in_=wst[:, sl], func=AF.Copy),
                             reads=[bWS[sl]], writes=[bWB[wslot][qtr]])
                    else:
                        S.op(eng, lambda g: g.tensor_copy(out=wbf[:, wslot, 2 * qtr:2 * qtr + 2, :], in_=wst[:, sl]),
                             reads=[bWS[sl]], writes=[bWB[wslot][qtr]])

            def emit_attention(groups, LA):
                SLOT = [(PT[0], PT[1]), (ST[0], ST[1]), (ST[2], ST[3])]
                bSLOT = [(bPT[0], bPT[1]), (bSTb[0], bSTb[1]), (bSTb[2], bSTb[3])]
                flat = []
                for g in groups:
                    c = g["c"]
                    jl = 4 * c + 3
                    for j in range(jl + 1):
                        flat.append((g, j, jl))
                pairs = [flat[i:i + 2] for i in range(0, len(flat), 2)]
                pend = []

                def emit_pv_pair(items, pbufs):
                    for (g, j, jl, n0, p) in items:
                        S.op("pe", lambda pe: pe.matmul(ACC[g["accO"]][:, n0:512], lhsT=g["vfun"](j), rhs=P[:, p, n0:512],
                                                        start=(j == 0), stop=(j == jl)),
                             reads=pbufs + [bV[j]], writes=[bACC[g["accO"]]])
                        if g["accD"] is not None:
                            S.op("pe", lambda pe: pe.matmul(ACC[g["accD"]][:, n0:512], lhsT=ones128[:, :], rhs=P[:, p, n0:512],
                                                            start=(j == 0), stop=(j == jl)),
                                 reads=pbufs + [bCONST], writes=[bACC[g["accD"]]])
                        if j == jl:
                            g["fin"](g)

                for pr in pairs:
                    k = nxt("st", 3)
                    m = nxt("p", 3)
                    sbufs = list(bSLOT[k][:len(pr)])
                    pbufs = [bP[2 * m + t] for t in range(len(pr))]
                    items = []
                    for t, (g, j, jl) in enumerate(pr):
                        c = g["c"]
                        n0 = max(0, j - 4 * c) * 128
                        qi, ki, r0, r1 = g["qk"]
                        bank = SLOT[k][t]
                        S.op("pe", lambda pe: pe.matmul(bank[:, n0:512], lhsT=T4[r0:r1, ki, j * 128:(j + 1) * 128],
                                                        rhs=T4[r0:r1, qi, c * 512 + n0:(c + 1) * 512], start=True, stop=True),
                             reads=[bT4[ki][j // 4], bT4[qi][c]], writes=[bSLOT[k][t]])
                        items.append((g, j, jl, n0, 2 * m + t))
                    for t, (g, j, jl, n0, p) in enumerate(items):
                        bank = SLOT[k][t]
                        S.op("act", lambda a: a.activation(out=P[:, p, n0:512], in_=bank[:, n0:512], func=AF.Exp, scale=g["scale"]),
                             reads=sbufs, writes=[bP[p]], skip_self=(t > 0))
                    for t, (g, j, jl, n0, p) in enumerate(items):
                        c = g["c"]
                        if g["mask"] == "causal":
                            if j >= 4 * c:
                                S.op("pool", lambda gp: gp.affine_select(out=P[:, p, n0:n0 + 128], in_=P[:, p, n0:n0 + 128],
                                                                         pattern=[[1, 128]], compare_op=ALU.is_ge, fill=0.0,
                                                                         base=0, channel_multiplier=-1),
                                     reads=pbufs, writes=[bP[p]], skip_self=(t > 0))
                        else:
                            m0 = max(4 * c - j, 0)
                            S.op("dve", lambda v: v.tensor_tensor(out=P[:, p, n0:512], in0=P[:, p, n0:512],
                                                                  in1=cm[:, m0 * 128:m0 * 128 + (512 - n0)], op=ALU.mult),
                                 reads=pbufs + [bCM], writes=[bP[p]], skip_self=(t > 0))
                    pend.append((items, pbufs))
                    if len(pend) > LA:
                        emit_pv_pair(*pend.pop(0))
                while pend:
                    emit_pv_pair(*pend.pop(0))

            def emit_unit(li, u, kind, next_w, mid_hook=None):
                wslot = u % 2
                if kind == "diff" and u == 4:
                    S.op("pool", lambda g: g.memset(T4[64:128, 0, :], 0.0), writes=bT4[0])
                    S.op("pool", lambda g: g.memset(T4[0:64, 1, :], 0.0), writes=bT4[1])
                if next_w is not None:
                    next_w()
                if kind == "fox":
                    pr = u
                    fqv = FQ[:, :].rearrange("p (b h r) -> p b h r", b=NB, h=8)
                    fkv = FK[:, :].rearrange("p (b h r) -> p b h r", b=NB, h=8)
                    qk4 = qk[:, :, :].rearrange("p b (g e) -> p b g e", g=4)
                    S.op("pool", lambda gp: gp.tensor_copy(out=qk4[:, :, 0:2, 64:70], in_=fqv[:, :, 2 * pr:2 * pr + 2, :]),
                         reads=[bFQK], writes=bQK)
                    S.op("pool", lambda gp: gp.tensor_copy(out=qk4[:, :, 2:4, 64:70], in_=fkv[:, :, 2 * pr:2 * pr + 2, :]),
                         reads=[bFQK], writes=bQK)
                for blk in range(NB):
                    b = nxt("pt", 2)
                    for kc in range(KC):
                        S.op("pe", lambda pe: pe.matmul(PT[b][:, 0:384], lhsT=hT[:, kc, blk * 128:(blk + 1) * 128],
                                                        rhs=wbf[:, wslot, kc, 0:384], start=(kc == 0), stop=(kc == 7)),
                             reads=[bH[blk], bWB[wslot][kc // 2]], writes=[bPT[b]])
                    srcq = PT[b][:, 0:128].rearrange("p (h d) -> p h d", h=2)
                    srck = PT[b][:, 128:256].rearrange("p (h d) -> p h d", h=2)
                    srcv = PT[b][:, 256:384].rearrange("p (h d) -> p h d", h=2)
                    if kind == "fox":
                        qv = qk[:, blk, :].rearrange("p (g e) -> p g e", g=4)
                        S.op("act", lambda a: a.activation(out=qv[:, 0:2, 0:64], in_=srcq, func=AF.Copy, scale=0.125),
                             reads=[bPT[b]], writes=[bQK[blk]])
                        S.op("dve", lambda v: v.tensor_copy(out=qv[:, 2:4, 0:64], in_=srck), reads=[bPT[b]], writes=[bQK[blk]])
                    else:
                        qv = qk[:, blk, 0:256].rearrange("p (g d) -> p g d", g=4)
                        S.op("act", lambda a: a.activation(out=qv[:, 0:2, :], in_=srcq, func=AF.Copy),
                             reads=[bPT[b]], writes=[bQK[blk]])
                        S.op("dve", lambda v: v.tensor_copy(out=qv[:, 2:4, :], in_=srck), reads=[bPT[b]], writes=[bQK[blk]])
                        S.op("dve", lambda v: v.tensor_copy(out=Rf[:, blk], in_=PT[b][:, 0:256].rearrange("p (g d) -> p g d", g=4)[:, :, 0:16]),
                             reads=[bPT[b]], writes=[bRf[blk]])
                    if kind == "diff":
                        S.op("act", lambda a: a.activation(out=vd[:, blk, :], in_=PT[b][:, 256:384], func=AF.Copy),
                             reads=[bPT[b]], writes=[bV[blk]])
                    else:
                        va4 = vaug[:, blk, :].rearrange("p (g d) -> p g d", g=4)
                        S.op("dve", lambda v: v.tensor_copy(out=va4[:, 0, :], in_=srcv[:, 0, :]), reads=[bPT[b]], writes=[bV[blk]])
                        S.op("act", lambda a: a.activation(out=va4[:, 3, :], in_=srcv[:, 1, :], func=AF.Copy), reads=[bPT[b]], writes=[bV[blk]])
                for c in range(4):
                    b = nxt("pt", 2)
                    for kc in range(KC):
                        S.op("pe", lambda pe: pe.matmul(PT[b][:, :], lhsT=wbf[:, wslot, kc, 384:512], rhs=hT[:, kc, c * 512:(c + 1) * 512],
                                                        start=(kc == 0), stop=(kc == 7)),
                             reads=bH[4 * c:4 * c + 4] + [bWB[wslot][kc // 2]], writes=[bPT[b]])
                    S.op("act", lambda a: a.activation(out=sgT[:, c * 512:(c + 1) * 512], in_=PT[b][:, :], func=AF.Silu),
                         reads=[bPT[b]], writes=[bSG[c]])
                if kind != "fox":
                    R2 = ftmp[:, 0:2, :].rearrange("p a (b g d) -> p (a b) g d", g=4, d=16)
                    bR2 = bF[0]
                    bR2b = bF[1]
                    S.op("dve", lambda v: v.tensor_tensor(out=R2[:, :, :, 0:8], in0=Rf[:, :, :, 8:16],
                                                          in1=SN[:, :, 0:8].unsqueeze(2).to_broadcast([128, NB, 4, 8]), op=ALU.mult),
                         reads=bRf + [bROPE], writes=[bR2, bR2b])
                    S.op("dve", lambda v: v.tensor_tensor(out=R2[:, :, :, 8:16], in0=Rf[:, :, :, 0:8],
                                                          in1=SN[:, :, 8:16].unsqueeze(2).to_broadcast([128, NB, 4, 8]), op=ALU.mult),
                         reads=bRf + [bROPE], writes=[bR2, bR2b])
                    S.op("dve", lambda v: v.tensor_tensor(out=Rf[:, :, :, :], in0=Rf[:, :, :, :],
                                                          in1=CS[:, :, :].unsqueeze(2).to_broadcast([128, NB, 4, 16]), op=ALU.mult),
                         reads=bRf + [bROPE, bR2, bR2b], writes=bRf)
                    qkr = qk[:, :, 0:256].rearrange("p b (g d) -> p b g d", g=4)
                    S.op("dve", lambda v: v.tensor_tensor(out=qkr[:, :, :, 0:16], in0=Rf[:, :, :, :], in1=R2[:, :, :, :], op=ALU.add),
                         reads=bRf + [bR2, bR2b], writes=bQK)
                TB = [PT[0], PT[1], ST[0], ST[1], ST[2], ST[3]]
                bTB = [bPT[0], bPT[1], bSTb[0], bSTb[1], bSTb[2], bSTb[3]]
                if kind == "fox":
                    for i in range(4):
                        for c in range(4):
                            b = nxt("tb", 6)
                            ptb = TB[b][:, :].bitcast(BF16)
                            for b4 in range(4):
                                blk = c * 4 + b4
                                S.op("pe", lambda pe: pe.transpose(out=ptb[0:70, b4 * 128:(b4 + 1) * 128], in_=qk[:, blk, i * 70:(i + 1) * 70],
                                                                   identity=identb[:]), reads=[bQK[blk], bCONST], writes=[bTB[b]])
                            eng = "act" if (i + c) % 2 == 0 else "dve"
                            if eng == "act":
                                S.op("act", lambda a: a.activation(out=T4[0:70, i, c * 512:(c + 1) * 512], in_=ptb[0:70, 0:512], func=AF.Copy),
                                     reads=[bTB[b]], writes=[bT4[i][c]])
                            else:
                                S.op("dve", lambda v: v.tensor_copy(out=T4[0:70, i, c * 512:(c + 1) * 512], in_=ptb[0:70, 0:512]),
                                     reads=[bTB[b]], writes=[bT4[i][c]])
                else:
                    for i in (0, 2):
                        for c in range(4):
                            b = nxt("tb", 6)
                            ptb = TB[b][:, :].bitcast(BF16)
                            for b4 in range(4):
                                blk = c * 4 + b4
                                S.op("pe", lambda pe: pe.transpose(out=ptb[:, b4 * 128:(b4 + 1) * 128], in_=qk[:, blk, i * 64:i * 64 + 128],
                                                                   identity=identb[:]), reads=[bQK[blk], bCONST], writes=[bTB[b]])
                            if i == 0:
                                S.op("act", lambda a: a.activation(out=T4[0:64, 0, c * 512:(c + 1) * 512], in_=ptb[0:64, 0:512], func=AF.Copy),
                                     reads=[bTB[b]], writes=[bT4[0][c]])
                                S.op("dve", lambda v: v.tensor_copy(out=T4[64:128, 1, c * 512:(c + 1) * 512], in_=ptb[64:128, 0:512]),
                                     reads=[bTB[b]], writes=[bT4[1][c]])
                            elif c % 2 == 0:
                                S.op("act", lambda a: a.activation(out=T4[:, i, c * 512:(c + 1) * 512], in_=ptb[:, 0:512], func=AF.Copy),
                                     reads=[bTB[b]], writes=[bT4[i][c]])
                            else:
                                S.op("dve", lambda v: v.tensor_copy(out=T4[:, i, c * 512:(c + 1) * 512], in_=ptb[:, 0:512]),
                                     reads=[bTB[b]], writes=[bT4[i][c]])
                if mid_hook is not None:
                    mid_hook()
                groups = []
                if kind in ("fox", "dil"):
                    def fin_pair(g):
                        hh = g["hh"]
                        c = g["c"]
                        a = g["accO"]
                        r0, r1 = (0, 64) if hh == 0 else (64, 128)
                        d0, d1 = (64, 128) if hh == 0 else (0, 64)
                        f = nxt("f", 4)
                        S.op("act", lambda a_: a_.activation(out=ftmp[r0:r1, f, :], in_=ACC[a][d0:d1, :], func=AF.Ln), reads=[bACC[a]], writes=[bF[f]])
                        S.op("act", lambda a_: a_.activation(out=ftmp[r0:r1, f, :], in_=ftmp[r0:r1, f, :], func=AF.Exp, scale=-1.0), reads=[bF[f]], writes=[bF[f]])
                        S.op("pool", lambda gp: gp.tensor_tensor(out=ftmp[r0:r1, f, :], in0=ftmp[r0:r1, f, :],
                                                                 in1=sgT[r0:r1, c * 512:(c + 1) * 512], op=ALU.mult),
                             reads=[bF[f], bSG[c]], writes=[bF[f]])
                        S.op("dve", lambda v: v.tensor_tensor(out=attnT[r0:r1, u, c * 512:(c + 1) * 512], in0=ACC[a][r0:r1, :],
                                                              in1=ftmp[r0:r1, f, :], op=ALU.mult),
                             reads=[bACC[a], bF[f]], writes=[bAT[u][c]])

                    for c in range(4):
                        for hh in range(2):
                            if kind == "fox":
                                qkspec = (hh, 2 + hh, 0, 70)
                            else:
                                qkspec = (hh, 2, 0, 128)
                            groups.append(dict(c=c, hh=hh, qk=qkspec, scale=(1.0 if kind == "fox" else 0.125),
                                               mask=("causal" if kind == "fox" else "cm"),
                                               vfun=(lambda j, hh=hh: vaug[:, j, hh * 128:(hh + 1) * 128]),
                                               accO=nxt("acc", 2), accD=None, fin=fin_pair))
                else:
                    state = {}

                    def fin_diff(g):
                        m = g["hh"]
                        c = g["c"]
                        o, d = g["accO"], g["accD"]
                        f = nxt("f", 4)
                        S.op("act", lambda a_: a_.activation(out=ftmp[:, f, :], in_=ACC[d][:, :], func=AF.Ln), reads=[bACC[d]], writes=[bF[f]])
                        S.op("act", lambda a_: a_.activation(out=ftmp[:, f, :], in_=ftmp[:, f, :], func=AF.Exp, scale=-1.0), reads=[bF[f]], writes=[bF[f]])
                        S.op("dve", lambda v: v.tensor_tensor(out=ftmp[:, f, :], in0=ACC[o][:, :], in1=ftmp[:, f, :], op=ALU.mult),
                             reads=[bACC[o], bF[f]], writes=[bF[f]])
                        if m == 0:
                            state["f1"] = f
                            return
                        f1 = state["f1"]
                        S.op("dve", lambda v: v.scalar_tensor_tensor(out=ftmp[:, f1, :], in0=ftmp[:, f, :], scalar=lamt[:, 4:5],
                                                                     in1=ftmp[:, f1, :], op0=ALU.mult, op1=ALU.add),
                             reads=[bF[f], bF[f1], bLAM], writes=[bF[f1]])
                        p = 6
                        S.op("pool", lambda gp: gp.tensor_tensor(out=P[:, p, :], in0=ftmp[:, f1, :], in1=ftmp[:, f1, :], op=ALU.mult),
                             reads=[bF[f1]], writes=[bP[p]])
                        S.op("pe", lambda pe: pe.matmul(ACC[d][:, :], lhsT=ones128[:, :], rhs=P[:, p, :], start=True, stop=True),
                             reads=[bP[p], bCONST], writes=[bACC[d]])
                        S.op("act", lambda a: a.activation(out=ftmp[:, f, :], in_=ACC[d][:, :], func=AF.Ln, scale=1.0 / 128.0, bias=EPS),
                             reads=[bACC[d]], writes=[bF[f]])
                        S.op("act", lambda a: a.activation(out=ftmp[:, f, :], in_=ftmp[:, f, :], func=AF.Exp, scale=-0.5),
                             reads=[bF[f]], writes=[bF[f]])
                        S.op("dve", lambda v: v.tensor_tensor(out=ftmp[:, f1, :], in0=ftmp[:, f1, :], in1=ftmp[:, f, :], op=ALU.mult),
                             reads=[bF[f1], bF[f]], writes=[bF[f1]])
                        S.op("dve", lambda v: v.scalar_tensor_tensor(out=attnT[:, u, c * 512:(c + 1) * 512], in0=ftmp[:, f1, :],
                                                                     scalar=wsub[:, 0:1], in1=sgT[:, c * 512:(c + 1) * 512],
                                                                     op0=ALU.mult, op1=ALU.mult),
                             reads=[bF[f1], bSG[c], bLAM], writes=[bAT[u][c]])

                    for c in range(4):
                        for m in range(2):
                            groups.append(dict(c=c, hh=m, qk=(m, 2, 0, 128), scale=0.125, mask="causal",
                                               vfun=(lambda j: vd[:, j, :]), accO=0, accD=1, fin=fin_diff))
                emit_attention(groups, LA_KIND[kind])

            def emit_even_prep(li):
                S.dma(lambda q: q.dma_start(out=lamrep[:], in_=lamrep_in[li]), writes=[bLAM])
                S.dma(lambda q: q.dma_start(out=lamt[:, 6:8], in_=lamc_in[li]), writes=[bLAM])
                S.dma(lambda q: q.dma_start(out=wsub[:], in_=subln_in[li]), writes=[bLAM])
                S.dma(lambda q: q.dma_start(out=negb[:], in_=bfc_in[li]), writes=[bLAM])
                S.dma(lambda q: q.dma_start(out=wfst[:], in_=wf_in[li]), writes=[bWF])
                S.op("dve", lambda v: v.tensor_copy(out=wfb[:], in_=wfst[:]), reads=[bWF], writes=[bWF])
                lr = lamrep[:, :].rearrange("p (a d) -> p a d", a=4)
                S.op("dve", lambda v: v.tensor_tensor(out=lamrep[:, 0:64], in0=lr[:, 0, :], in1=lr[:, 1, :], op=ALU.mult),
                     reads=[bLAM], writes=[bLAM])
                S.op("dve", lambda v: v.tensor_tensor(out=lamrep[:, 128:192], in0=lr[:, 2, :], in1=lr[:, 3, :], op=ALU.mult),
                     reads=[bLAM], writes=[bLAM])
                S.op("dve", lambda v: v.reduce_sum(out=lamt[:, 0:1], in_=lamrep[:, 0:64], axis=AX.X), reads=[bLAM], writes=[bLAM])
                S.op("dve", lambda v: v.reduce_sum(out=lamt[:, 1:2], in_=lamrep[:, 128:192], axis=AX.X), reads=[bLAM], writes=[bLAM])
                S.op("act", lambda a: a.activation(out=lamt[:, 2:4], in_=lamt[:, 0:2], func=AF.Exp), reads=[bLAM], writes=[bLAM])
                S.op("dve", lambda v: v.tensor_tensor(out=lamt[:, 4:5], in0=lamt[:, 3:4], in1=lamt[:, 2:3], op=ALU.subtract),
                     reads=[bLAM], writes=[bLAM])
                S.op("dve", lambda v: v.tensor_tensor(out=lamt[:, 4:5], in0=lamt[:, 4:5], in1=lamt[:, 6:7], op=ALU.add),
                     reads=[bLAM], writes=[bLAM])
                S.op("dve", lambda v: v.tensor_tensor(out=wsub[:], in0=wsub[:], in1=lamt[:, 7:8], op=ALU.mult),
                     reads=[bLAM], writes=[bLAM])
                S.op("dve", lambda v: v.tensor_scalar(out=negb[:], in0=negb[:], scalar1=-1.0, scalar2=None, op0=ALU.mult),
                     reads=[bLAM], writes=[bLAM])
                fza = xo[0:8, :, :].rearrange("p a d -> p (a d)")
                fzb = xw[0:8, :, :].rearrange("p a d -> p (a d)")
                for c in range(4):
                    b = nxt("pt", 2)
                    for kc in range(KC):
                        S.op("pe", lambda pe: pe.matmul(PT[b][0:8, :], lhsT=wfb[:, kc, :], rhs=hT[:, kc, c * 512:(c + 1) * 512],
                                                        start=(kc == 0), stop=(kc == 7)),
                             reads=bH[4 * c:4 * c + 4] + [bWF], writes=[bPT[b]])
                    S.op("act", lambda a: a.activation(out=fza[:, c * 512:(c + 1) * 512], in_=PT[b][0:8, :], func=AF.Exp,
                                                       scale=-1.0, bias=negb[0:8, 0:1]),
                         reads=[bPT[b], bLAM], writes=bXO)
                S.op("act", lambda a: a.activation(out=fza, in_=fza, func=AF.Ln, bias=1.0), reads=bXO, writes=bXO)
                ones8 = P[0:8, 0:4, :].rearrange("p a d -> p (a d)")
                S.op("dve", lambda v: v.memset(ones8, 1.0), writes=bP[0:4])
                S.op("dve", lambda v: v.tensor_tensor_scan(out=fzb, data0=ones8, data1=fza, initial=0.0, op0=ALU.mult, op1=ALU.add),
                     reads=bXO + bP[0:4], writes=bXW)
                b = nxt("pt", 2)
                for blk in range(NB):
                    S.op("pe", lambda pe: pe.transpose(out=PT[b][:, blk * 8:(blk + 1) * 8], in_=fzb[:, blk * 128:(blk + 1) * 128],
                                                       identity=identf[0:8, 0:8]), reads=bXW + [bCONST], writes=[bPT[b]])
                S.op("dve", lambda v: v.tensor_copy(out=cs_tok[:], in_=PT[b][:, 0:128]), reads=[bPT[b]], writes=[bSPL])
                S.op("dve", lambda v: v.tensor_copy(out=hml[:, 0, :], in_=cs_tok[:]), reads=[bSPL], writes=[bSPL])
                S.op("dve", lambda v: v.tensor_tensor(out=r_tok[:], in0=cs_tok[:], in1=hml[:, 0, :], op=ALU.subtract),
                     reads=[bSPL], writes=[bSPL])
                S.op("dve", lambda v: v.tensor_copy(out=hml[:, 1, :], in_=r_tok[:]), reads=[bSPL], writes=[bSPL])
                S.op("dve", lambda v: v.tensor_tensor(out=r_tok[:], in0=r_tok[:], in1=hml[:, 1, :], op=ALU.subtract),
                     reads=[bSPL], writes=[bSPL])
                S.op("dve", lambda v: v.tensor_copy(out=hml[:, 2, :], in_=r_tok[:]), reads=[bSPL], writes=[bSPL])
                S.op("dve", lambda v: v.memset(FQ[:], 1.0), writes=[bFQK])
                S.op("dve", lambda v: v.memset(FK[:], 1.0), writes=[bFQK])
                fq3 = FQ[:, :].rearrange("p (n r) -> p n r", r=6)
                fk3 = FK[:, :].rearrange("p (n r) -> p n r", r=6)
                for r in range(3):
                    S.op("dve", lambda v: v.tensor_scalar(out=fq3[:, :, r], in0=hml[:, r, :], scalar1=-1.0, scalar2=None, op0=ALU.mult),
                         reads=[bSPL], writes=[bFQK])
                    S.op("dve", lambda v: v.tensor_copy(out=fk3[:, :, 3 + r], in_=hml[:, r, :]), reads=[bSPL], writes=[bFQK])


            def emit_out_block(li, blk, x_src, x_dst, bsrc_list, bdst_list, has_next):
                par = li % 2
                xsl = blk % 2
                YB = [ACC[0], ACC[1]] if blk % 2 == 0 else [ST[0], ST[1]]
                bYB = [bACC[0], bACC[1]] if blk % 2 == 0 else [bSTb[0], bSTb[1]]
                for half in range(2):
                    for kc in range(KC):
                        S.op("pe", lambda pe: pe.matmul(YB[half][:, :], lhsT=attnT[:, kc, blk * 128:(blk + 1) * 128],
                                                        rhs=wbf[:, half, kc, :], start=(kc == 0), stop=(kc == 7)),
                             reads=[bAT[kc][blk // 4], bWB[half][kc // 2]], writes=[bYB[half]])
                s = nxt("stat", 8)
                for half in range(2):
                    S.op("act", lambda a: a.activation(out=junk[:, half * 512:(half + 1) * 512], in_=YB[half][:, :], func=AF.Square,
                                                       accum_out=stat[:, s, half:half + 1]),
                         reads=[bYB[half]], writes=[bJ, bJ2, bST_[s]])
                S.op("dve", lambda v: v.tensor_tensor(out=stat[:, s, 2:3], in0=stat[:, s, 0:1], in1=stat[:, s, 1:2], op=ALU.add),
                     reads=[bST_[s]], writes=[bST_[s]])
                emit_rstd(s, 2, 3, float(D))
                for half in range(2):
                    S.op("dve", lambda v: v.scalar_tensor_tensor(out=xw[:, xsl, half * 512:(half + 1) * 512], in0=YB[half][:, :],
                                                                 scalar=stat[:, s, 3:4], in1=Gbc[:, half * 512:(half + 1) * 512],
                                                                 op0=ALU.mult, op1=ALU.mult),
                         reads=[bYB[half], bST_[s], bG], writes=[bXW[xsl]])
                S.op("pool", lambda gp: gp.tensor_tensor(out=xw[:, xsl, :], in0=xw[:, xsl, :], in1=xo[:, xsl, :], op=ALU.add),
                     reads=[bXW[xsl], bXO[xsl]], writes=[bXW[xsl]])
                S.dma(lambda q: q.dma_start(out=x_dst[blk * 128:(blk + 1) * 128, :], in_=xw[:, xsl, :]),
                      reads=[bXW[xsl]], writes=[bdst_list[blk]])

            bXIN = [Buf() for _ in range(NB)]
            bOUT = [Buf() for _ in range(NB)]
            for j in range(6):
                emit_mod_load(0, j, j % 2, ("dve", "act", "dve", "pool"))
                emit_mod_mm(0, j, j % 2)
            emit_g_bcast(0)
            for blk in range(NB):
                sl = blk % 2
                S.dma(lambda q: q.dma_start(out=xw[:, sl, :], in_=x_in[blk * 128:(blk + 1) * 128, :]), writes=[bXW[sl]])
                emit_h_block(0, blk, xw[:, sl, :], bXW[sl], xo[:, sl, :], bXO[sl])

            for li, kd in enumerate(kinds):
                first = li == 0
                last = li == n - 1
                x_src, bsrc = (x_in, bXIN) if first else (xs, bXS)
                x_dst, bdst = (out, bOUT) if last else (xs, bXS)
                if kd == "e":
                    emit_even_prep(li)
                    ukinds = ["fox"] * 4 + ["diff"] * 4
                else:
                    ukinds = ["dil"] * 8
                emit_wload(lambda qtr: wu_in[li, 0, qtr], 0)
                for u in range(8):
                    if u < 7:
                        nw = (lambda u=u: emit_wload(lambda qtr: wu_in[li, u + 1, qtr], (u + 1) % 2))
                    else:
                        nw = (lambda: emit_wload(lambda qtr: wo_in[li, 0, qtr], 0))
                    mid = post = None
                    if (not last) and u < 6:
                        mid = (lambda u=u: emit_mod_load(li + 1, u, u % 2))
                        post = (lambda u=u: emit_mod_mm(li + 1, u, u % 2))
                    emit_unit(li, u, ukinds[u], nw, mid)
                    if post is not None:
                        post()
                emit_wload(lambda qtr: wo_in[li, 1, qtr], 1)
                def ld_xo(blk):
                    S.dma(lambda q: q.dma_start(out=xo[:, blk % 2, :], in_=x_src[blk * 128:(blk + 1) * 128, :]),
                          reads=[bsrc[blk]], writes=[bXO[blk % 2]])
                ld_xo(0)
                ld_xo(1)
                emit_out_block(li, 0, x_src, x_dst, bsrc, bdst, False)
                for blk in range(NB):
                    if blk + 1 < NB:
                        emit_out_block(li, blk + 1, x_src, x_dst, bsrc, bdst, False)
                    if not last:
                        xsl = blk % 2
                        emit_h_block(li + 1, blk, xw[:, xsl, :], bXW[xsl], xo[:, xsl, :], bXO[xsl])
                    if blk + 2 < NB:
                        ld_xo(blk + 2)
                if not last:
                    emit_g_bcast(li + 1)
        program()
        S.finish_plan()
        program()
        for i in range(len(S.dsem)):
            if S.dcnt[i]:
                nc.sync.wait_ge(S.dsem[i], S.dcnt[i])
    return nc


def _pieces(w512):
    return np.ascontiguousarray(w512.reshape(4, 2, 128, 512).transpose(0, 2, 1, 3))


def _prep_shared(inp):
    depth = 4
    f32 = np.float32
    adaw = np.zeros((depth, 6, 4, 128, 2, 512), f32)
    colp = np.zeros((depth, 128, 40), f32)
    for l in range(depth):
        for j in range(6):
            adaw[l, j] = _pieces(inp["ada_w"][l][:, j * 512:(j + 1) * 512])
        colp[l, :, 0:24] = inp["ada_b"][l].reshape(24, 128).T
        colp[l, :, 24:32] = inp["norm_pre"][l].reshape(8, 128).T
        colp[l, :, 32:40] = inp["norm_post"][l].reshape(8, 128).T
    wu = np.zeros((depth, 8, 4, 128, 2, 512), f32)
    wo = np.zeros((depth, 2, 4, 128, 2, 512), f32)
    wf = np.zeros((depth, 128, KC, 8), f32)
    bfc = np.zeros((depth, 8, 1), f32)
    lamrep = np.zeros((depth, 128, 256), f32)
    sublnc = np.zeros((depth, 128, 1), f32)
    lamc = np.zeros((depth, 128, 2), f32)
    for l in range(depth):
        if l % 2 == 0:
            i = l // 2
            W = inp["ev_w_in"][i]
            for u in range(8):
                if u < 4:
                    offs = [128 * u, 512 + 128 * u, 1024 + 128 * u, 1544 + 128 * u]
                else:
                    d = u - 4
                    offs = [2056 + 128 * d, 2568 + 128 * d, 3080 + 128 * d, 3592 + 128 * d]
                w512 = np.concatenate([W[:, o:o + 128] for o in offs], axis=1)
                wu[l, u] = _pieces(w512)
            wf[l] = W[:, 1536:1544].reshape(KC, 128, 8).transpose(1, 0, 2)
            bfc[l, :, 0] = inp["ev_b_forget"][i]
            lamrep[l] = np.broadcast_to(np.concatenate([inp["ev_lambda_q1"][i], inp["ev_lambda_k1"][i],
                                                        inp["ev_lambda_q2"][i], inp["ev_lambda_k2"][i]])[None, :], (128, 256))
            sublnc[l, :, 0] = inp["ev_subln"][i]
            li0 = lam_init_of(l)
            lamc[l, :, 0] = -li0
            lamc[l, :, 1] = 1.0 - li0
            Wo = inp["ev_w_out"][i]
        else:
            j = l // 2
            W = inp["od_w_in"][j]
            for u in range(8):
                offs = [128 * u, 1024 + 128 * u, 2048 + 128 * u, 3072 + 128 * u]
                w512 = np.concatenate([W[:, o:o + 128] for o in offs], axis=1)
                wu[l, u] = _pieces(w512)
            Wo = inp["od_w_out"][j]
        for half in range(2):
            wo[l, half] = _pieces(Wo[:, half * 512:(half + 1) * 512])
    inv_freq = (1.0 / (THETA ** (np.arange(0, 16, 2, dtype=np.float32) / np.float32(16)))).astype(np.float32)
    cst = np.zeros((128, 32), f32)
    cst[:, 0:8] = inv_freq / (2.0 * np.pi)
    cst[:, 8:16] = inv_freq / (2.0 * np.pi)
    cst[:, 24:32] = 0.25
    k = np.arange(128)[:, None]
    q = np.arange(2048)[None, :]
    dlt = q - k
    cmv = ((dlt >= 0) & (dlt <= 128)).astype(np.float32) + ((dlt >= 0) & (dlt % 4 == 0) & (dlt <= 512)).astype(np.float32) \
        + ((dlt >= 0) & (dlt % 16 == 0)).astype(np.float32)
    cmm = cmv.astype(ml_dtypes.bfloat16)
    return dict(adaw=adaw, colp=colp, wu=wu, wo=wo, wf=wf, bfc=bfc, lamrep=lamrep, sublnc=sublnc, lamc=lamc, cst=cst, cm=cmm)


_LAYER_KEYS = ["adaw", "colp", "wu", "wo", "wf", "bfc", "lamrep", "sublnc", "lamc"]
_PROG_CACHE = {}


def _get_prog(kinds):
    key = tuple(kinds)
    if key not in _PROG_CACHE:
        _PROG_CACHE[key] = build(list(kinds))
    return _PROG_CACHE[key]


LAUNCH_GROUPS = [[0, 1, 2, 3]]


def kernel(**inputs):
    inp = {k: np.asarray(v) for k, v in inputs.items()}
    sh = _prep_shared(inp)
    x = np.ascontiguousarray(inp["x"].astype(np.float32, copy=False))
    pos = inp["positions"].astype(np.int32)
    c = inp["c"].astype(np.float32)
    cur = [x[b] for b in range(N_CORES)]
    for grp in LAUNCH_GROUPS:
        kinds = ["e" if l % 2 == 0 else "o" for l in grp]
        nc = _get_prog(kinds)
        in_maps = []
        for b in range(N_CORES):
            m = {"x": np.ascontiguousarray(cur[b]),
                 "pos": np.ascontiguousarray(pos[b].reshape(NB, 128).T),
                 "ccol": np.ascontiguousarray(c[b].reshape(KC, 128).T),
                 "cst": sh["cst"], "cm": sh["cm"]}
            for kk in _LAYER_KEYS:
                m[kk] = np.ascontiguousarray(sh[kk][grp[0]:grp[-1] + 1])
            in_maps.append(m)
        res = run_bass_kernel_spmd(nc, in_maps, core_ids=list(range(N_CORES)))
        cur = [np.asarray(res.results[b]["out"]) for b in range(N_CORES)]
    return np.stack(cur, axis=0).astype(np.float32)
```

```python
import math
import os
from contextlib import ExitStack

import numpy as np
import ml_dtypes
import concourse.bass as bass
import concourse.mybir as mybir
from concourse.bass_utils import run_bass_kernel_spmd

F32 = mybir.dt.float32
BF16 = mybir.dt.bfloat16
I32 = mybir.dt.int32
AF = mybir.ActivationFunctionType
ALU = mybir.AluOpType
AX = mybir.AxisListType

S_LEN = 2048
D = 1024
NB = 16
KC = 8
EPS = 1e-6
N_CORES = 8
THETA = 500000.0


ALL_BUFS = []


class Buf:
    __slots__ = ("w", "r")

    def __init__(self):
        self.w = None
        self.r = {}
        ALL_BUFS.append(self)


class Sched:
    def __init__(self, nc, stack, n_dma_sems=16):
        self.nc = nc
        self.eng = {"pe": nc.tensor, "dve": nc.vector, "act": nc.scalar, "pool": nc.gpsimd, "sp": nc.sync}
        self.sem = {k: stack.enter_context(nc.semaphore("s_" + k)) for k in self.eng}
        self.dsem = [stack.enter_context(nc.semaphore("d%d" % i)) for i in range(n_dma_sems)]
        self.semobj = {}
        for k in self.eng:
            self.semobj[("e", k)] = self.sem[k]
        for i in range(n_dma_sems):
            self.semobj[("d", i)] = self.dsem[i]
        self.plan = True
        self.targets = {("e", k): set() for k in self.eng}
        self.rank = {}
        self.reset()

    def reset(self):
        self.cnt = {k: 0 for k in self.eng}
        self.seen = {k: {} for k in self.eng}
        self.dcnt = [0] * len(self.dsem)
        self.dnext = 0
        for b in ALL_BUFS:
            b.w = None
            b.r = {}

    def finish_plan(self):
        self.plan = False
        for k, tg in self.targets.items():
            self.rank[k] = {v: i + 1 for i, v in enumerate(sorted(tg))}
        self.reset()

    def _waits(self, e, reads, writes, skip_self=False):
        need = {}

        def add(k, v):
            if k == ("e", "pe") and e == "pe":
                return
            if skip_self and k == ("e", e):
                return
            if need.get(k, 0) < v:
                need[k] = v

        for b in reads:
            if b.w is not None:
                add(*b.w)
        for b in writes:
            if b.w is not None:
                add(*b.w)
            for k, v in b.r.items():
                add(k, v)
        h = self.eng[e]
        seen = self.seen[e]
        for k, v in need.items():
            if seen.get(k, 0) >= v:
                continue
            seen[k] = v
            if k[0] == "e":
                if self.plan:
                    self.targets[k].add(v)
                else:
                    h.wait_ge(self.semobj[k], self.rank[k][v])
            elif not self.plan:
                h.wait_ge(self.semobj[k], v)

    def _commit(self, t, reads, writes):
        k, v = t
        for b in reads:
            if b.r.get(k, 0) < v:
                b.r[k] = v
        for b in writes:
            b.w = t
            b.r = {}

    def op(self, e, fn, reads=(), writes=(), skip_self=False):
        self._waits(e, reads, writes, skip_self)
        self.cnt[e] += 1
        k = ("e", e)
        if not self.plan:
            ins = fn(self.eng[e])
            if self.cnt[e] in self.rank[k]:
                ins.then_inc(self.sem[e], 1)
        t = (k, self.cnt[e])
        self._commit(t, reads, writes)
        return t

    def dma(self, fn, reads=(), writes=(), q="sp"):
        i = self.dnext
        self.dnext = (self.dnext + 1) % len(self.dsem)
        h = self.eng[q]
        k = ("d", i)
        if self.dcnt[i] > 0 and self.seen[q].get(k, 0) < self.dcnt[i]:
            if not self.plan:
                h.wait_ge(self.dsem[i], self.dcnt[i])
            self.seen[q][k] = self.dcnt[i]
        self._waits(q, reads, writes)
        self.dcnt[i] += 16
        if not self.plan:
            ins = fn(h)
            ins.then_inc(self.dsem[i], 16)
        t = (k, self.dcnt[i])
        self._commit(t, reads, writes)
        return t


def lam_init_of(layer_idx):
    return 0.8 - 0.6 * math.exp(-0.3 * layer_idx)


def build(kinds):
    n = len(kinds)
    nc = bass.Bass("TRN2", target_bir_lowering=False)

    def dram(name, shape, dt, kind):
        return nc.dram_tensor(name, shape, dt, kind=kind).ap()

    x_in = dram("x", [S_LEN, D], F32, "ExternalInput")
    pos_in = dram("pos", [128, NB], I32, "ExternalInput")
    ccol_in = dram("ccol", [128, KC], F32, "ExternalInput")
    adaw_in = dram("adaw", [n, 6, 4, 128, 2, 512], F32, "ExternalInput")
    colp_in = dram("colp", [n, 128, 40], F32, "ExternalInput")
    wu_in = dram("wu", [n, 8, 4, 128, 2, 512], F32, "ExternalInput")
    wo_in = dram("wo", [n, 2, 4, 128, 2, 512], F32, "ExternalInput")
    wf_in = dram("wf", [n, 128, KC, 8], F32, "ExternalInput")
    bfc_in = dram("bfc", [n, 8, 1], F32, "ExternalInput")
    lamrep_in = dram("lamrep", [n, 128, 256], F32, "ExternalInput")
    subln_in = dram("sublnc", [n, 128, 1], F32, "ExternalInput")
    lamc_in = dram("lamc", [n, 128, 2], F32, "ExternalInput")
    cst_in = dram("cst", [128, 32], F32, "ExternalInput")
    cm_in = dram("cm", [128, 2048], BF16, "ExternalInput")
    out = dram("out", [S_LEN, D], F32, "ExternalOutput")
    xs = dram("xs", [S_LEN, D], F32, "Internal")
    gsc = dram("gsc", [n, 8, 128], F32, "Internal")

    with ExitStack() as st:
        S = Sched(nc, st)

        def sb(name, shape, dt):
            return st.enter_context(nc.sbuf_tensor("sb_" + name, shape, dt))

        def ps(name, shape, dt):
            return st.enter_context(nc.psum_tensor("ps_" + name, shape, dt))

        hT = sb("hT", [128, KC, S_LEN], BF16)
        bH = [Buf() for _ in range(NB)]
        attnT = sb("attnT", [128, 8, S_LEN], BF16)
        bAT = [[Buf() for _ in range(4)] for _ in range(8)]
        wbf = sb("wbf", [128, 2, KC, 512], BF16)
        bWB = [[Buf() for _ in range(4)] for _ in range(2)]
        wst = sb("wst", [128, 2, 2, 512], F32)
        bWS = [Buf(), Buf()]
        qk = sb("qk", [128, NB, 280], BF16)
        bQK = [Buf() for _ in range(NB)]
        Rf = sb("Rf", [128, NB, 4, 16], F32)
        bRf = [Buf() for _ in range(NB)]
        T4 = sb("T4", [128, 4, S_LEN], BF16)
        bT4 = [[Buf() for _ in range(4)] for _ in range(4)]
        vaug = sb("vaug", [128, NB, 256], BF16)
        vd = sb("vd", [128, NB, 128], BF16)
        bV = [Buf() for _ in range(NB)]
        sgT = sb("sgT", [128, S_LEN], BF16)
        bSG = [Buf() for _ in range(4)]
        NP = 7
        P = sb("P", [128, NP, 512], BF16)
        bP = [Buf() for _ in range(NP)]
        junk = P[:, 0:2, :].rearrange("p a d -> p (a d)")
        bJ = bP[0]
        bJ2 = bP[1]
        ftmp = sb("ftmp", [128, 4, 512], F32)
        bF = [Buf() for _ in range(4)]
        cm = sb("cm", [128, 2048], BF16)
        bCM = Buf()
        ones128 = sb("ones128", [128, 128], BF16)
        identb = sb("identb", [128, 128], BF16)
        identf = sb("identf", [128, 128], F32)
        bCONST = Buf()
        cst = sb("cst", [128, 32], F32)
        CS = sb("CS", [128, NB, 16], F32)
        SN = sb("SN", [128, NB, 16], F32)
        bROPE = Buf()
        Gbc = sb("Gbc", [128, D], F32)
        bG = Buf()
        ASc = sb("ASc", [128, 2, 16], F32)
        bAS = [Buf(), Buf()]
        modc = sb("modc", [128, 32], F32)
        bMOD = Buf()
        colp = sb("colp", [128, 40], F32)
        bCOLP = Buf()
        grow = sb("grow", [8, 128], F32)
        bGROW = Buf()
        bGSC = [Buf() for _ in range(n)]
        condf = sb("condf", [128, KC], F32)
        condb = sb("condb", [128, KC, 2], BF16)
        bCOND = Buf()
        FQ = sb("FQ", [128, 768], BF16)
        FK = sb("FK", [128, 768], BF16)
        bFQK = Buf()
        cs_tok = sb("cs_tok", [128, 128], F32)
        r_tok = sb("r_tok", [128, 128], F32)
        hml = sb("hml", [128, 3, 128], BF16)
        bSPL = Buf()
        xo = sb("xo", [128, 2, D], F32)
        bXO = [Buf(), Buf()]
        xw = sb("xw", [128, 2, D], F32)
        bXW = [Buf(), Buf()]
        stat = sb("stat", [128, 8, 8], F32)
        bST_ = [Buf() for _ in range(8)]
        lamrep = sb("lamrep", [128, 256], F32)
        lamt = sb("lamt", [128, 8], F32)
        wsub = sb("wsub", [128, 1], F32)
        negb = sb("negb", [8, 1], F32)
        wfst = sb("wfst", [128, KC, 8], F32)
        wfb = sb("wfb", [128, KC, 8], BF16)
        bLAM = Buf()
        bWF = Buf()
        posi = sb("posi", [128, NB], I32)
        posf = sb("posf", [128, NB], F32)
        Yt = sb("Yt", [128, NB, 16], F32)
        Yi = sb("Yi", [128, NB, 16], I32)
        SPT = [ps("SP%d" % i, [128, 2, 512], F32) for i in range(3)]
        PT = [SPT[0][:, 0, :], SPT[0][:, 1, :]]
        bPT = [Buf(), Buf()]
        NST = 4
        NACC = 2
        ST = [SPT[1][:, 0, :], SPT[1][:, 1, :], SPT[2][:, 0, :], SPT[2][:, 1, :]]
        bSTb = [Buf() for _ in range(NST)]
        ACC = [ps("ACC%d" % i, [128, 512], F32) for i in range(NACC)]
        bACC = [Buf() for _ in range(NACC)]
        bXS = [Buf() for _ in range(NB)]

        LA_KIND = {'fox': int(os.environ.get('F_LA', '2')), 'diff': int(os.environ.get('F_LA', '2')), 'dil': int(os.environ.get('F_LA', '2'))}
        cnt = {"wst": 0, "stat": 0, "pt": 0, "st": 0, "p": 0, "f": 0, "acc": 0, "x": 0, "tb": 0}

        def nxt(key, mod):
            v = cnt[key] % mod
            cnt[key] += 1
            return v

        def program():
            for k_ in cnt:
                cnt[k_] = 0
            S.op("dve", lambda v: v.memset(ones128[:], 1.0), writes=[bCONST])
            S.op("dve", lambda v: v.memset(identf[:], 1.0), writes=[bCONST])
            S.op("pool", lambda g: g.affine_select(out=identf[:], in_=identf[:], pattern=[[1, 128]], compare_op=ALU.is_equal,
                                                   fill=0.0, base=0, channel_multiplier=-1), reads=[bCONST], writes=[bCONST])
            S.op("dve", lambda v: v.tensor_copy(out=identb[:], in_=identf[:]), reads=[bCONST], writes=[bCONST])
            S.op("dve", lambda v: v.memset(vaug[:], 1.0), writes=bV)
            S.op("pool", lambda g: g.memset(T4[64:128, 0, :], 0.0), writes=bT4[0])
            S.op("pool", lambda g: g.memset(T4[0:64, 1, :], 0.0), writes=bT4[1])
            S.dma(lambda q: q.dma_start(out=cm[:], in_=cm_in[:, :]), writes=[bCM])
            S.dma(lambda q: q.dma_start(out=cst[:], in_=cst_in[:, :]), writes=[bROPE])
            S.dma(lambda q: q.dma_start(out=posi[:], in_=pos_in[:, :]), writes=[bROPE])
            S.dma(lambda q: q.dma_start(out=condf[:], in_=ccol_in[:, :]), writes=[bCOND])
            S.op("act", lambda a: a.activation(out=condf[:], in_=condf[:], func=AF.Silu), reads=[bCOND], writes=[bCOND])
            S.op("dve", lambda v: v.tensor_copy(out=condb[:, :, 0], in_=condf[:]), reads=[bCOND], writes=[bCOND])
            S.op("dve", lambda v: v.tensor_copy(out=condb[:, :, 1], in_=condf[:]), reads=[bCOND], writes=[bCOND])
            S.op("dve", lambda v: v.tensor_copy(out=posf[:], in_=posi[:]), reads=[bROPE], writes=[bROPE])
            S.op("dve", lambda v: v.tensor_tensor(out=Yt[:], in0=posf[:].unsqueeze(2).to_broadcast([128, NB, 16]),
                                                  in1=cst[:, 0:16].unsqueeze(1).to_broadcast([128, NB, 16]), op=ALU.mult),
                 reads=[bROPE], writes=[bROPE])
            S.op("dve", lambda v: v.tensor_tensor(out=Yt[:], in0=Yt[:], in1=cst[:, 16:32].unsqueeze(1).to_broadcast([128, NB, 16]),
                                                  op=ALU.add), reads=[bROPE], writes=[bROPE])
            S.op("dve", lambda v: v.tensor_copy(out=Yi[:], in_=Yt[:]), reads=[bROPE], writes=[bROPE])
            S.op("dve", lambda v: v.tensor_copy(out=SN[:], in_=Yi[:]), reads=[bROPE], writes=[bROPE])
            S.op("dve", lambda v: v.tensor_tensor(out=Yt[:], in0=Yt[:], in1=SN[:], op=ALU.subtract), reads=[bROPE], writes=[bROPE])
            S.op("act", lambda a: a.activation(out=Yt[:], in_=Yt[:], func=AF.Sin, scale=2.0 * math.pi * (1.0 - 1e-6)),
                 reads=[bROPE], writes=[bROPE])
            S.op("dve", lambda v: v.tensor_copy(out=CS[:, :, 0:8], in_=Yt[:, :, 8:16]), reads=[bROPE], writes=[bROPE])
            S.op("dve", lambda v: v.tensor_copy(out=CS[:, :, 8:16], in_=Yt[:, :, 8:16]), reads=[bROPE], writes=[bROPE])
            S.op("dve", lambda v: v.tensor_scalar(out=SN[:, :, 0:8], in0=Yt[:, :, 0:8], scalar1=-1.0, scalar2=None, op0=ALU.mult),
                 reads=[bROPE], writes=[bROPE])
            S.op("dve", lambda v: v.tensor_copy(out=SN[:, :, 8:16], in_=Yt[:, :, 0:8]), reads=[bROPE], writes=[bROPE])

            def emit_mod_load(li, j, wslot, engs=("pool",)):
                emit_wload(lambda qtr: adaw_in[li, j, qtr], wslot, engs)

            def emit_mod_mm(li, j, wslot):
                if j == 0:
                    S.dma(lambda q: q.dma_start(out=colp[:], in_=colp_in[li]), writes=[bCOLP])
                b = nxt("pt", 2)
                for m in range(4):
                    for kc in range(KC):
                        S.op("pe", lambda pe: pe.matmul(PT[b][:, 2 * m:2 * m + 2], lhsT=wbf[:, wslot, kc, m * 128:(m + 1) * 128],
                                                        rhs=condb[:, kc, :], start=(kc == 0), stop=(kc == 7)),
                             reads=[bWB[wslot][kc // 2], bCOND], writes=[bPT[b]])
                S.op("dve", lambda v: v.tensor_tensor(out=modc[:, 4 * j:4 * j + 4],
                                                      in0=PT[b][:, 0:8].rearrange("p (i two) -> p i two", two=2)[:, :, 0],
                                                      in1=colp[:, 4 * j:4 * j + 4], op=ALU.add),
                     reads=[bPT[b], bCOLP], writes=[bMOD])
                if j == 5:
                    par = li % 2
                    S.op("dve", lambda v: v.scalar_tensor_tensor(out=ASc[:, par, 0:8], in0=modc[:, 8:16], scalar=1.0,
                                                                 in1=colp[:, 24:32], op0=ALU.add, op1=ALU.mult),
                         reads=[bMOD, bCOLP], writes=[bAS[par]])
                    S.op("dve", lambda v: v.tensor_copy(out=ASc[:, par, 8:16], in_=modc[:, 0:8]), reads=[bMOD], writes=[bAS[par]])
                    S.op("dve", lambda v: v.tensor_tensor(out=modc[:, 24:32], in0=modc[:, 16:24], in1=colp[:, 32:40], op=ALU.mult),
                         reads=[bMOD, bCOLP], writes=[bMOD])
                    b2 = nxt("pt", 2)
                    S.op("pe", lambda pe: pe.transpose(out=PT[b2][0:8, 0:128], in_=modc[:, 24:32], identity=identf[:]),
                         reads=[bMOD, bCONST], writes=[bPT[b2]])
                    S.op("dve", lambda v: v.tensor_copy(out=grow[0:8, :], in_=PT[b2][0:8, 0:128]), reads=[bPT[b2]], writes=[bGROW])
                    S.dma(lambda q: q.dma_start(out=gsc[li], in_=grow[0:8, :]), reads=[bGROW], writes=[bGSC[li]])

            def emit_g_bcast(li):
                S.dma(lambda q: q.dma_start(out=Gbc[:, :], in_=gsc[li:li + 1].rearrange("o a b -> o (a b)").to_broadcast([128, D])),
                      reads=[bGSC[li]], writes=[bG])

            def emit_rstd(s, src, dst, count):
                S.op("act", lambda a: a.activation(out=stat[:, s, dst:dst + 1], in_=stat[:, s, src:src + 1], func=AF.Ln,
                                                   scale=1.0 / count, bias=EPS), reads=[bST_[s]], writes=[bST_[s]])
                S.op("act", lambda a: a.activation(out=stat[:, s, dst:dst + 1], in_=stat[:, s, dst:dst + 1], func=AF.Exp,
                                                   scale=-0.5), reads=[bST_[s]], writes=[bST_[s]])

            def emit_h_block(li, blk, xsrc, bsrc, xdst, bdst):
                par = li % 2
                s = nxt("stat", 8)
                S.op("act", lambda a: a.activation(out=junk, in_=xsrc, func=AF.Square, accum_out=stat[:, s, 0:1]),
                     reads=[bsrc], writes=[bJ, bJ2, bST_[s]])
                emit_rstd(s, 0, 1, float(D))
                S.op("dve", lambda v: v.tensor_scalar(out=xdst, in0=xsrc, scalar1=stat[:, s, 1:2], scalar2=None, op0=ALU.mult),
                     reads=[bsrc, bST_[s]], writes=[bdst])
                for half in range(2):
                    b = nxt("pt", 2)
                    for k4 in range(4):
                        kc = half * 4 + k4
                        S.op("pe", lambda pe: pe.transpose(out=PT[b][:, k4 * 128:(k4 + 1) * 128], in_=xdst[:, kc * 128:(kc + 1) * 128],
                                                           identity=identf[:]), reads=[bdst, bCONST], writes=[bPT[b]])
                    for k4 in range(4):
                        kc = half * 4 + k4
                        if kc % 2 == 0:
                            S.op("act", lambda a: a.activation(out=hT[:, kc, blk * 128:(blk + 1) * 128], in_=PT[b][:, k4 * 128:(k4 + 1) * 128],
                                                               func=AF.Identity, scale=ASc[:, par, kc:kc + 1], bias=ASc[:, par, 8 + kc:9 + kc]),
                                 reads=[bPT[b], bAS[par]], writes=[bH[blk]])
                        else:
                            S.op("dve", lambda v: v.tensor_scalar(out=hT[:, kc, blk * 128:(blk + 1) * 128], in0=PT[b][:, k4 * 128:(k4 + 1) * 128],
                                                                  scalar1=ASc[:, par, kc:kc + 1], scalar2=ASc[:, par, 8 + kc:9 + kc],
                                                                  op0=ALU.mult, op1=ALU.add),
                                 reads=[bPT[b], bAS[par]], writes=[bH[blk]])

            def emit_h_block_bf(li, blk, xsl):
                par = li % 2
                s = nxt("stat", 8)
                xsrc = xw[:, xsl, :]
                S.op("act", lambda a: a.activation(out=junk, in_=xsrc, func=AF.Square, accum_out=stat[:, s, 0:1]),
                     reads=[bXW[xsl]], writes=[bJ, bJ2, bST_[s]])
                emit_rstd(s, 0, 1, float(D))
                xnb = P[:, 2 + 2 * xsl:4 + 2 * xsl, :].rearrange("p a d -> p (a d)")
                bxn = [bP[2 + 2 * xsl], bP[3 + 2 * xsl]]
                S.op("dve", lambda v: v.tensor_scalar(out=xnb, in0=xsrc, scalar1=stat[:, s, 1:2], scalar2=None, op0=ALU.mult),
                     reads=[bXW[xsl], bST_[s]], writes=bxn)
                b = nxt("pt", 2)
                ptb = PT[b].bitcast(BF16)
                for kc in range(KC):
                    S.op("pe", lambda pe: pe.transpose(out=ptb[:, kc * 128:(kc + 1) * 128], in_=xnb[:, kc * 128:(kc + 1) * 128],
                                                       identity=identb[:]), reads=bxn + [bCONST], writes=[bPT[b]])
                for kc in range(KC):
                    if kc % 2 == 0:
                        S.op("act", lambda a: a.activation(out=hT[:, kc, blk * 128:(blk + 1) * 128], in_=ptb[:, kc * 128:(kc + 1) * 128],
                                                           func=AF.Identity, scale=ASc[:, par, kc:kc + 1], bias=ASc[:, par, 8 + kc:9 + kc]),
                             reads=[bPT[b], bAS[par]], writes=[bH[blk]])
                    else:
                        S.op("dve", lambda v: v.tensor_scalar(out=hT[:, kc, blk * 128:(blk + 1) * 128], in0=ptb[:, kc * 128:(kc + 1) * 128],
                                                              scalar1=ASc[:, par, kc:kc + 1], scalar2=ASc[:, par, 8 + kc:9 + kc],
                                                              op0=ALU.mult, op1=ALU.add),
                             reads=[bPT[b], bAS[par]], writes=[bH[blk]])

            def emit_wload(src_fn, wslot, engs=("pool",)):
                for qtr in range(4):
                    sl = nxt("wst", 2)
                    S.dma(lambda q: q.dma_start(out=wst[:, sl], in_=src_fn(qtr)), writes=[bWS[sl]])
                    eng = engs[qtr % len(engs)]
                    if eng == "act":
                        S.op("act", lambda a: a.activation(out=wbf[:, wslot, 2 * qtr:2 * qtr + 2, :], in_=wst[:, sl], func=AF.Copy),
                             reads=[bWS[sl]], writes=[bWB[wslot][qtr]])
                    else:
                        S.op(eng, lambda g: g.tensor_copy(out=wbf[:, wslot, 2 * qtr:2 * qtr + 2, :], in_=wst[:, sl]),
                             reads=[bWS[sl]], writes=[bWB[wslot][qtr]])

            def emit_attention(groups, LA):
                SLOT = [(PT[0], PT[1]), (ST[0], ST[1]), (ST[2], ST[3])]
                bSLOT = [(bPT[0], bPT[1]), (bSTb[0], bSTb[1]), (bSTb[2], bSTb[3])]
                flat = []
                for g in groups:
                    c = g["c"]
                    jl = 4 * c + 3
                    for j in range(jl + 1):
                        flat.append((g, j, jl))
                pairs = [flat[i:i + 2] for i in range(0, len(flat), 2)]
                pend = []

                def emit_pv_pair(items, pbufs):
                    for (g, j, jl, n0, p) in items:
                        S.op("pe", lambda pe: pe.matmul(ACC[g["accO"]][:, n0:512], lhsT=g["vfun"](j), rhs=P[:, p, n0:512],
                                                        start=(j == 0), stop=(j == jl)),
                             reads=pbufs + [bV[j]], writes=[bACC[g["accO"]]])
                        if g["accD"] is not None:
                            S.op("pe", lambda pe: pe.matmul(ACC[g["accD"]][:, n0:512], lhsT=ones128[:, :], rhs=P[:, p, n0:512],
                                                            start=(j == 0), stop=(j == jl)),
                                 reads=pbufs + [bCONST], writes=[bACC[g["accD"]]])
                        if j == jl:
                            g["fin"](g)

                for pr in pairs:
                    k = nxt("st", 3)
                    m = nxt("p", 3)
                    sbufs = list(bSLOT[k][:len(pr)])
                    pbufs = [bP[2 * m + t] for t in range(len(pr))]
                    items = []
                    for t, (g, j, jl) in enumerate(pr):
                        c = g["c"]
                        n0 = max(0, j - 4 * c) * 128
                        qi, ki, r0, r1 = g["qk"]
                        bank = SLOT[k][t]
                        S.op("pe", lambda pe: pe.matmul(bank[:, n0:512], lhsT=T4[r0:r1, ki, j * 128:(j + 1) * 128],
                                                        rhs=T4[r0:r1, qi, c * 512 + n0:(c + 1) * 512], start=True, stop=True),
                             reads=[bT4[ki][j // 4], bT4[qi][c]], writes=[bSLOT[k][t]])
                        items.append((g, j, jl, n0, 2 * m + t))
                    if len(items) == 2 and items[0][3] == items[1][3]:
                        n0 = items[0][3]
                        g = items[0][0]
                        S.op("act", lambda a: a.activation(out=P[:, 2 * m:2 * m + 2, n0:512], in_=SPT[k][:, :, n0:512], func=AF.Exp,
                                                           scale=g["scale"]),
                             reads=sbufs, writes=[bP[2 * m], bP[2 * m + 1]])
                    else:
                        for t, (g, j, jl, n0, p) in enumerate(items):
                            bank = SLOT[k][t]
                            S.op("act", lambda a: a.activation(out=P[:, p, n0:512], in_=bank[:, n0:512], func=AF.Exp, scale=g["scale"]),
                                 reads=sbufs, writes=[bP[p]], skip_self=(t > 0))
                    for t, (g, j, jl, n0, p) in enumerate(items):
                        c = g["c"]
                        if g["mask"] == "causal":
                            if j >= 4 * c:
                                S.op("pool", lambda gp: gp.affine_select(out=P[:, p, n0:n0 + 128], in_=P[:, p, n0:n0 + 128],
                                                                         pattern=[[1, 128]], compare_op=ALU.is_ge, fill=0.0,
                                                                         base=0, channel_multiplier=-1),
                                     reads=pbufs, writes=[bP[p]], skip_self=(t > 0))
                        else:
                            m0 = max(4 * c - j, 0)
                            S.op("dve", lambda v: v.tensor_tensor(out=P[:, p, n0:512], in0=P[:, p, n0:512],
                                                                  in1=cm[:, m0 * 128:m0 * 128 + (512 - n0)], op=ALU.mult),
                                 reads=pbufs + [bCM], writes=[bP[p]], skip_self=(t > 0))
                    pend.append((items, pbufs))
                    if len(pend) > LA:
                        emit_pv_pair(*pend.pop(0))
                while pend:
                    emit_pv_pair(*pend.pop(0))

            def emit_unit(li, u, kind, next_w, mid_hook=None):
                wslot = u % 2
                if kind == "diff" and u == 4:
                    S.op("pool", lambda g: g.memset(T4[64:128, 0, :], 0.0), writes=bT4[0])
                    S.op("pool", lambda g: g.memset(T4[0:64, 1, :], 0.0), writes=bT4[1])
                if next_w is not None:
                    next_w()
                if kind == "fox":
                    pr = u
                    fqv = FQ[:, :].rearrange("p (b h r) -> p b h r", b=NB, h=8)
                    fkv = FK[:, :].rearrange("p (b h r) -> p b h r", b=NB, h=8)
                    qk4 = qk[:, :, :].rearrange("p b (g e) -> p b g e", g=4)
                    S.op("pool", lambda gp: gp.tensor_copy(out=qk4[:, :, 0:2, 64:70], in_=fqv[:, :, 2 * pr:2 * pr + 2, :]),
                         reads=[bFQK], writes=bQK)
                    S.op("pool", lambda gp: gp.tensor_copy(out=qk4[:, :, 2:4, 64:70], in_=fkv[:, :, 2 * pr:2 * pr + 2, :]),
                         reads=[bFQK], writes=bQK)
                for blk in range(NB):
                    b = nxt("pt", 2)
                    for kc in range(KC):
                        S.op("pe", lambda pe: pe.matmul(PT[b][:, 0:384], lhsT=hT[:, kc, blk * 128:(blk + 1) * 128],
                                                        rhs=wbf[:, wslot, kc, 0:384], start=(kc == 0), stop=(kc == 7)),
                             reads=[bH[blk], bWB[wslot][kc // 2]], writes=[bPT[b]])
                    srcq = PT[b][:, 0:128].rearrange("p (h d) -> p h d", h=2)
                    srck = PT[b][:, 128:256].rearrange("p (h d) -> p h d", h=2)
                    srcv = PT[b][:, 256:384].rearrange("p (h d) -> p h d", h=2)
                    if kind == "fox":
                        qv = qk[:, blk, :].rearrange("p (g e) -> p g e", g=4)
                        S.op("act", lambda a: a.activation(out=qv[:, 0:2, 0:64], in_=srcq, func=AF.Copy, scale=0.125),
                             reads=[bPT[b]], writes=[bQK[blk]])
                        S.op("dve", lambda v: v.tensor_copy(out=qv[:, 2:4, 0:64], in_=srck), reads=[bPT[b]], writes=[bQK[blk]])
                    else:
                        qv = qk[:, blk, 0:256].rearrange("p (g d) -> p g d", g=4)
                        S.op("act", lambda a: a.activation(out=qv[:, 0:2, :], in_=srcq, func=AF.Copy),
                             reads=[bPT[b]], writes=[bQK[blk]])
                        S.op("dve", lambda v: v.tensor_copy(out=qv[:, 2:4, :], in_=srck), reads=[bPT[b]], writes=[bQK[blk]])
                        S.op("dve", lambda v: v.tensor_copy(out=Rf[:, blk], in_=PT[b][:, 0:256].rearrange("p (g d) -> p g d", g=4)[:, :, 0:16]),
                             reads=[bPT[b]], writes=[bRf[blk]])
                    if kind == "diff":
                        S.op("act", lambda a: a.activation(out=vd[:, blk, :], in_=PT[b][:, 256:384], func=AF.Copy),
                             reads=[bPT[b]], writes=[bV[blk]])
                    else:
                        va4 = vaug[:, blk, :].rearrange("p (g d) -> p g d", g=4)
                        S.op("dve", lambda v: v.tensor_copy(out=va4[:, 0, :], in_=srcv[:, 0, :]), reads=[bPT[b]], writes=[bV[blk]])
                        S.op("act", lambda a: a.activation(out=va4[:, 3, :], in_=srcv[:, 1, :], func=AF.Copy), reads=[bPT[b]], writes=[bV[blk]])
                for c in range(4):
                    b = nxt("pt", 2)
                    for kc in range(KC):
                        S.op("pe", lambda pe: pe.matmul(PT[b][:, :], lhsT=wbf[:, wslot, kc, 384:512], rhs=hT[:, kc, c * 512:(c + 1) * 512],
                                                        start=(kc == 0), stop=(kc == 7)),
                             reads=bH[4 * c:4 * c + 4] + [bWB[wslot][kc // 2]], writes=[bPT[b]])
                    S.op("act", lambda a: a.activation(out=sgT[:, c * 512:(c + 1) * 512], in_=PT[b][:, :], func=AF.Silu),
                         reads=[bPT[b]], writes=[bSG[c]])
                if kind != "fox":
                    R2 = ftmp[:, 0:2, :].rearrange("p a (b g d) -> p (a b) g d", g=4, d=16)
                    bR2 = bF[0]
                    bR2b = bF[1]
                    S.op("dve", lambda v: v.tensor_tensor(out=R2[:, :, :, 0:8], in0=Rf[:, :, :, 8:16],
                                                          in1=SN[:, :, 0:8].unsqueeze(2).to_broadcast([128, NB, 4, 8]), op=ALU.mult),
                         reads=bRf + [bROPE], writes=[bR2, bR2b])
                    S.op("dve", lambda v: v.tensor_tensor(out=R2[:, :, :, 8:16], in0=Rf[:, :, :, 0:8],
                                                          in1=SN[:, :, 8:16].unsqueeze(2).to_broadcast([128, NB, 4, 8]), op=ALU.mult),
                         reads=bRf + [bROPE], writes=[bR2, bR2b])
                    S.op("dve", lambda v: v.tensor_tensor(out=Rf[:, :, :, :], in0=Rf[:, :, :, :],
                                                          in1=CS[:, :, :].unsqueeze(2).to_broadcast([128, NB, 4, 16]), op=ALU.mult),
                         reads=bRf + [bROPE, bR2, bR2b], writes=bRf)
                    qkr = qk[:, :, 0:256].rearrange("p b (g d) -> p b g d", g=4)
                    S.op("dve", lambda v: v.tensor_tensor(out=qkr[:, :, :, 0:16], in0=Rf[:, :, :, :], in1=R2[:, :, :, :], op=ALU.add),
                         reads=bRf + [bR2, bR2b], writes=bQK)
                TB = [PT[0], PT[1], ST[0], ST[1], ST[2], ST[3]]
                bTB = [bPT[0], bPT[1], bSTb[0], bSTb[1], bSTb[2], bSTb[3]]
                if kind == "fox":
                    for i in range(4):
                        for c in range(4):
                            b = nxt("tb", 6)
                            ptb = TB[b][:, :].bitcast(BF16)
                            for b4 in range(4):
                                blk = c * 4 + b4
                                S.op("pe", lambda pe: pe.transpose(out=ptb[0:70, b4 * 128:(b4 + 1) * 128], in_=qk[:, blk, i * 70:(i + 1) * 70],
                                                                   identity=identb[:]), reads=[bQK[blk], bCONST], writes=[bTB[b]])
                            eng = "act" if (i + c) % 2 == 0 else "dve"
                            if eng == "act":
                                S.op("act", lambda a: a.activation(out=T4[0:70, i, c * 512:(c + 1) * 512], in_=ptb[0:70, 0:512], func=AF.Copy),
                                     reads=[bTB[b]], writes=[bT4[i][c]])
                            else:
                                S.op("dve", lambda v: v.tensor_copy(out=T4[0:70, i, c * 512:(c + 1) * 512], in_=ptb[0:70, 0:512]),
                                     reads=[bTB[b]], writes=[bT4[i][c]])
                else:
                    for i in (0, 2):
                        for c in range(4):
                            b = nxt("tb", 6)
                            ptb = TB[b][:, :].bitcast(BF16)
                            for b4 in range(4):
                                blk = c * 4 + b4
                                S.op("pe", lambda pe: pe.transpose(out=ptb[:, b4 * 128:(b4 + 1) * 128], in_=qk[:, blk, i * 64:i * 64 + 128],
                                                                   identity=identb[:]), reads=[bQK[blk], bCONST], writes=[bTB[b]])
                            if i == 0:
                                S.op("act", lambda a: a.activation(out=T4[0:64, 0, c * 512:(c + 1) * 512], in_=ptb[0:64, 0:512], func=AF.Copy),
                                     reads=[bTB[b]], writes=[bT4[0][c]])
                                S.op("dve", lambda v: v.tensor_copy(out=T4[64:128, 1, c * 512:(c + 1) * 512], in_=ptb[64:128, 0:512]),
                                     reads=[bTB[b]], writes=[bT4[1][c]])
                            elif c % 2 == 0:
                                S.op("act", lambda a: a.activation(out=T4[:, i, c * 512:(c + 1) * 512], in_=ptb[:, 0:512], func=AF.Copy),
                                     reads=[bTB[b]], writes=[bT4[i][c]])
                            else:
                                S.op("dve", lambda v: v.tensor_copy(out=T4[:, i, c * 512:(c + 1) * 512], in_=ptb[:, 0:512]),
                                     reads=[bTB[b]], writes=[bT4[i][c]])
                if mid_hook is not None:
                    mid_hook()
                groups = []
                if kind in ("fox", "dil"):
                    def fin_pair(g):
                        hh = g["hh"]
                        c = g["c"]
                        a = g["accO"]
                        r0, r1 = (0, 64) if hh == 0 else (64, 128)
                        d0, d1 = (64, 128) if hh == 0 else (0, 64)
                        f = nxt("f", 4)
                        S.op("act", lambda a_: a_.activation(out=ftmp[r0:r1, f, :], in_=ACC[a][d0:d1, :], func=AF.Ln), reads=[bACC[a]], writes=[bF[f]])
                        S.op("act", lambda a_: a_.activation(out=ftmp[r0:r1, f, :], in_=ftmp[r0:r1, f, :], func=AF.Exp, scale=-1.0), reads=[bF[f]], writes=[bF[f]])
                        S.op("pool", lambda gp: gp.tensor_tensor(out=ftmp[r0:r1, f, :], in0=ftmp[r0:r1, f, :],
                                                                 in1=sgT[r0:r1, c * 512:(c + 1) * 512], op=ALU.mult),
                             reads=[bF[f], bSG[c]], writes=[bF[f]])
                        S.op("dve", lambda v: v.tensor_tensor(out=attnT[r0:r1, u, c * 512:(c + 1) * 512], in0=ACC[a][r0:r1, :],
                                                              in1=ftmp[r0:r1, f, :], op=ALU.mult),
                             reads=[bACC[a], bF[f]], writes=[bAT[u][c]])

                    for c in range(4):
                        for hh in range(2):
                            if kind == "fox":
                                qkspec = (hh, 2 + hh, 0, 70)
                            else:
                                qkspec = (hh, 2, 0, 128)
                            groups.append(dict(c=c, hh=hh, qk=qkspec, scale=(1.0 if kind == "fox" else 0.125),
                                               mask=("causal" if kind == "fox" else "cm"),
                                               vfun=(lambda j, hh=hh: vaug[:, j, hh * 128:(hh + 1) * 128]),
                                               accO=nxt("acc", 2), accD=None, fin=fin_pair))
                else:
                    state = {}

                    def fin_diff(g):
                        m = g["hh"]
                        c = g["c"]
                        o, d = g["accO"], g["accD"]
                        f = nxt("f", 4)
                        S.op("act", lambda a_: a_.activation(out=ftmp[:, f, :], in_=ACC[d][:, :], func=AF.Ln), reads=[bACC[d]], writes=[bF[f]])
                        S.op("act", lambda a_: a_.activation(out=ftmp[:, f, :], in_=ftmp[:, f, :], func=AF.Exp, scale=-1.0), reads=[bF[f]], writes=[bF[f]])
                        S.op("dve", lambda v: v.tensor_tensor(out=ftmp[:, f, :], in0=ACC[o][:, :], in1=ftmp[:, f, :], op=ALU.mult),
                             reads=[bACC[o], bF[f]], writes=[bF[f]])
                        if m == 0:
                            state["f1"] = f
                            return
                        f1 = state["f1"]
                        S.op("dve", lambda v: v.scalar_tensor_tensor(out=ftmp[:, f1, :], in0=ftmp[:, f, :], scalar=lamt[:, 4:5],
                                                                     in1=ftmp[:, f1, :], op0=ALU.mult, op1=ALU.add),
                             reads=[bF[f], bF[f1], bLAM], writes=[bF[f1]])
                        p = 6
                        S.op("pool", lambda gp: gp.tensor_tensor(out=P[:, p, :], in0=ftmp[:, f1, :], in1=ftmp[:, f1, :], op=ALU.mult),
                             reads=[bF[f1]], writes=[bP[p]])
                        S.op("pe", lambda pe: pe.matmul(ACC[d][:, :], lhsT=ones128[:, :], rhs=P[:, p, :], start=True, stop=True),
                             reads=[bP[p], bCONST], writes=[bACC[d]])
                        S.op("act", lambda a: a.activation(out=ftmp[:, f, :], in_=ACC[d][:, :], func=AF.Ln, scale=1.0 / 128.0, bias=EPS),
                             reads=[bACC[d]], writes=[bF[f]])
                        S.op("act", lambda a: a.activation(out=ftmp[:, f, :], in_=ftmp[:, f, :], func=AF.Exp, scale=-0.5),
                             reads=[bF[f]], writes=[bF[f]])
                        S.op("dve", lambda v: v.tensor_tensor(out=ftmp[:, f1, :], in0=ftmp[:, f1, :], in1=ftmp[:, f, :], op=ALU.mult),
                             reads=[bF[f1], bF[f]], writes=[bF[f1]])
                        S.op("dve", lambda v: v.scalar_tensor_tensor(out=attnT[:, u, c * 512:(c + 1) * 512], in0=ftmp[:, f1, :],
                                                                     scalar=wsub[:, 0:1], in1=sgT[:, c * 512:(c + 1) * 512],
                                                                     op0=ALU.mult, op1=ALU.mult),
                             reads=[bF[f1], bSG[c], bLAM], writes=[bAT[u][c]])

                    for c in range(4):
                        for m in range(2):
                            groups.append(dict(c=c, hh=m, qk=(m, 2, 0, 128), scale=0.125, mask="causal",
                                               vfun=(lambda j: vd[:, j, :]), accO=0, accD=1, fin=fin_diff))
                emit_attention(groups, LA_KIND[kind])

            def emit_even_prep(li):
                S.dma(lambda q: q.dma_start(out=lamrep[:], in_=lamrep_in[li]), writes=[bLAM])
                S.dma(lambda q: q.dma_start(out=lamt[:, 6:8], in_=lamc_in[li]), writes=[bLAM])
                S.dma(lambda q: q.dma_start(out=wsub[:], in_=subln_in[li]), writes=[bLAM])
                S.dma(lambda q: q.dma_start(out=negb[:], in_=bfc_in[li]), writes=[bLAM])
                S.dma(lambda q: q.dma_start(out=wfst[:], in_=wf_in[li]), writes=[bWF])
                S.op("dve", lambda v: v.tensor_copy(out=wfb[:], in_=wfst[:]), reads=[bWF], writes=[bWF])
                lr = lamrep[:, :].rearrange("p (a d) -> p a d", a=4)
                S.op("dve", lambda v: v.tensor_tensor(out=lamrep[:, 0:64], in0=lr[:, 0, :], in1=lr[:, 1, :], op=ALU.mult),
                     reads=[bLAM], writes=[bLAM])
                S.op("dve", lambda v: v.tensor_tensor(out=lamrep[:, 128:192], in0=lr[:, 2, :], in1=lr[:, 3, :], op=ALU.mult),
                     reads=[bLAM], writes=[bLAM])
                S.op("dve", lambda v: v.reduce_sum(out=lamt[:, 0:1], in_=lamrep[:, 0:64], axis=AX.X), reads=[bLAM], writes=[bLAM])
                S.op("dve", lambda v: v.reduce_sum(out=lamt[:, 1:2], in_=lamrep[:, 128:192], axis=AX.X), reads=[bLAM], writes=[bLAM])
                S.op("act", lambda a: a.activation(out=lamt[:, 2:4], in_=lamt[:, 0:2], func=AF.Exp), reads=[bLAM], writes=[bLAM])
                S.op("dve", lambda v: v.tensor_tensor(out=lamt[:, 4:5], in0=lamt[:, 3:4], in1=lamt[:, 2:3], op=ALU.subtract),
                     reads=[bLAM], writes=[bLAM])
                S.op("dve", lambda v: v.tensor_tensor(out=lamt[:, 4:5], in0=lamt[:, 4:5], in1=lamt[:, 6:7], op=ALU.add),
                     reads=[bLAM], writes=[bLAM])
                S.op("dve", lambda v: v.tensor_tensor(out=wsub[:], in0=wsub[:], in1=lamt[:, 7:8], op=ALU.mult),
                     reads=[bLAM], writes=[bLAM])
                S.op("dve", lambda v: v.tensor_scalar(out=negb[:], in0=negb[:], scalar1=-1.0, scalar2=None, op0=ALU.mult),
                     reads=[bLAM], writes=[bLAM])
                fza = xo[0:8, :, :].rearrange("p a d -> p (a d)")
                fzb = xw[0:8, :, :].rearrange("p a d -> p (a d)")
                for c in range(4):
                    b = nxt("pt", 2)
                    for kc in range(KC):
                        S.op("pe", lambda pe: pe.matmul(PT[b][0:8, :], lhsT=wfb[:, kc, :], rhs=hT[:, kc, c * 512:(c + 1) * 512],
                                                        start=(kc == 0), stop=(kc == 7)),
                             reads=bH[4 * c:4 * c + 4] + [bWF], writes=[bPT[b]])
                    S.op("act", lambda a: a.activation(out=fza[:, c * 512:(c + 1) * 512], in_=PT[b][0:8, :], func=AF.Exp,
                                                       scale=-1.0, bias=negb[0:8, 0:1]),
                         reads=[bPT[b], bLAM], writes=bXO)
                S.op("act", lambda a: a.activation(out=fza, in_=fza, func=AF.Ln, bias=1.0), reads=bXO, writes=bXO)
                ones8 = P[0:8, 0:4, :].rearrange("p a d -> p (a d)")
                S.op("dve", lambda v: v.memset(ones8, 1.0), writes=bP[0:4])
                S.op("dve", lambda v: v.tensor_tensor_scan(out=fzb, data0=ones8, data1=fza, initial=0.0, op0=ALU.mult, op1=ALU.add),
                     reads=bXO + bP[0:4], writes=bXW)
                b = nxt("pt", 2)
                for blk in range(NB):
                    S.op("pe", lambda pe: pe.transpose(out=PT[b][:, blk * 8:(blk + 1) * 8], in_=fzb[:, blk * 128:(blk + 1) * 128],
                                                       identity=identf[0:8, 0:8]), reads=bXW + [bCONST], writes=[bPT[b]])
                S.op("dve", lambda v: v.tensor_copy(out=cs_tok[:], in_=PT[b][:, 0:128]), reads=[bPT[b]], writes=[bSPL])
                S.op("dve", lambda v: v.tensor_copy(out=hml[:, 0, :], in_=cs_tok[:]), reads=[bSPL], writes=[bSPL])
                S.op("dve", lambda v: v.tensor_tensor(out=r_tok[:], in0=cs_tok[:], in1=hml[:, 0, :], op=ALU.subtract),
                     reads=[bSPL], writes=[bSPL])
                S.op("dve", lambda v: v.tensor_copy(out=hml[:, 1, :], in_=r_tok[:]), reads=[bSPL], writes=[bSPL])
                S.op("dve", lambda v: v.tensor_tensor(out=r_tok[:], in0=r_tok[:], in1=hml[:, 1, :], op=ALU.subtract),
                     reads=[bSPL], writes=[bSPL])
                S.op("dve", lambda v: v.tensor_copy(out=hml[:, 2, :], in_=r_tok[:]), reads=[bSPL], writes=[bSPL])
                S.op("dve", lambda v: v.memset(FQ[:], 1.0), writes=[bFQK])
                S.op("dve", lambda v: v.memset(FK[:], 1.0), writes=[bFQK])
                fq3 = FQ[:, :].rearrange("p (n r) -> p n r", r=6)
                fk3 = FK[:, :].rearrange("p (n r) -> p n r", r=6)
                for r in range(3):
                    S.op("dve", lambda v: v.tensor_scalar(out=fq3[:, :, r], in0=hml[:, r, :], scalar1=-1.0, scalar2=None, op0=ALU.mult),
                         reads=[bSPL], writes=[bFQK])
                    S.op("dve", lambda v: v.tensor_copy(out=fk3[:, :, 3 + r], in_=hml[:, r, :]), reads=[bSPL], writes=[bFQK])


            def emit_out_block(li, blk, x_src, x_dst, bsrc_list, bdst_list, has_next):
                par = li % 2
                xsl = blk % 2
                YB = [ACC[0], ACC[1]] if blk % 2 == 0 else [ST[0], ST[1]]
                bYB = [bACC[0], bACC[1]] if blk % 2 == 0 else [bSTb[0], bSTb[1]]
                for half in range(2):
                    for kc in range(KC):
                        S.op("pe", lambda pe: pe.matmul(YB[half][:, :], lhsT=attnT[:, kc, blk * 128:(blk + 1) * 128],
                                                        rhs=wbf[:, half, kc, :], start=(kc == 0), stop=(kc == 7)),
                             reads=[bAT[kc][blk // 4], bWB[half][kc // 2]], writes=[bYB[half]])
                s = nxt("stat", 8)
                for half in range(2):
                    S.op("act", lambda a: a.activation(out=junk[:, half * 512:(half + 1) * 512], in_=YB[half][:, :], func=AF.Square,
                                                       accum_out=stat[:, s, half:half + 1]),
                         reads=[bYB[half]], writes=[bJ, bJ2, bST_[s]])
                S.op("dve", lambda v: v.tensor_tensor(out=stat[:, s, 2:3], in0=stat[:, s, 0:1], in1=stat[:, s, 1:2], op=ALU.add),
                     reads=[bST_[s]], writes=[bST_[s]])
                emit_rstd(s, 2, 3, float(D))
                for half in range(2):
                    S.op("dve", lambda v: v.scalar_tensor_tensor(out=xw[:, xsl, half * 512:(half + 1) * 512], in0=YB[half][:, :],
                                                                 scalar=stat[:, s, 3:4], in1=Gbc[:, half * 512:(half + 1) * 512],
                                                                 op0=ALU.mult, op1=ALU.mult),
                         reads=[bYB[half], bST_[s], bG], writes=[bXW[xsl]])
                S.op("pool", lambda gp: gp.tensor_tensor(out=xw[:, xsl, :], in0=xw[:, xsl, :], in1=xo[:, xsl, :], op=ALU.add),
                     reads=[bXW[xsl], bXO[xsl]], writes=[bXW[xsl]])
                S.dma(lambda q: q.dma_start(out=x_dst[blk * 128:(blk + 1) * 128, :], in_=xw[:, xsl, :]),
                      reads=[bXW[xsl]], writes=[bdst_list[blk]])

            bXIN = [Buf() for _ in range(NB)]
            bOUT = [Buf() for _ in range(NB)]
            for j in range(6):
                emit_mod_load(0, j, j % 2, ("dve", "act", "dve", "pool"))
                emit_mod_mm(0, j, j % 2)
            emit_g_bcast(0)
            for blk in range(NB):
                sl = blk % 2
                S.dma(lambda q: q.dma_start(out=xw[:, sl, :], in_=x_in[blk * 128:(blk + 1) * 128, :]), writes=[bXW[sl]])
                emit_h_block(0, blk, xw[:, sl, :], bXW[sl], xo[:, sl, :], bXO[sl])

            for li, kd in enumerate(kinds):
                first = li == 0
                last = li == n - 1
                x_src, bsrc = (x_in, bXIN) if first else (xs, bXS)
                x_dst, bdst = (out, bOUT) if last else (xs, bXS)
                if kd == "e":
                    emit_even_prep(li)
                    ukinds = ["fox"] * 4 + ["diff"] * 4
                else:
                    ukinds = ["dil"] * 8
                emit_wload(lambda qtr: wu_in[li, 0, qtr], 0)
                for u in range(8):
                    if u < 7:
                        nw = (lambda u=u: emit_wload(lambda qtr: wu_in[li, u + 1, qtr], (u + 1) % 2))
                    else:
                        nw = (lambda: emit_wload(lambda qtr: wo_in[li, 0, qtr], 0))
                    mid = post = None
                    if (not last) and u < 6:
                        mid = (lambda u=u: emit_mod_load(li + 1, u, u % 2))
                        post = (lambda u=u: emit_mod_mm(li + 1, u, u % 2))
                    emit_unit(li, u, ukinds[u], nw, mid)
                    if post is not None:
                        post()
                emit_wload(lambda qtr: wo_in[li, 1, qtr], 1)
                def ld_xo(blk):
                    S.dma(lambda q: q.dma_start(out=xo[:, blk % 2, :], in_=x_src[blk * 128:(blk + 1) * 128, :]),
                          reads=[bsrc[blk]], writes=[bXO[blk % 2]])
                ld_xo(0)
                ld_xo(1)
                emit_out_block(li, 0, x_src, x_dst, bsrc, bdst, False)
                for blk in range(NB):
                    if blk + 1 < NB:
                        emit_out_block(li, blk + 1, x_src, x_dst, bsrc, bdst, False)
                    if blk + 2 < NB:
                        ld_xo(blk + 2)
                    if not last:
                        emit_h_block_bf(li + 1, blk, blk % 2)
                if not last:
                    emit_g_bcast(li + 1)
        program()
        S.finish_plan()
        program()
        for i in range(len(S.dsem)):
            if S.dcnt[i]:
                nc.sync.wait_ge(S.dsem[i], S.dcnt[i])
    return nc


def _pieces(w512):
    return np.ascontiguousarray(w512.reshape(4, 2, 128, 512).transpose(0, 2, 1, 3))


def _prep_shared(inp):
    depth = 4
    f32 = np.float32
    adaw = np.zeros((depth, 6, 4, 128, 2, 512), f32)
    colp = np.zeros((depth, 128, 40), f32)
    for l in range(depth):
        for j in range(6):
            adaw[l, j] = _pieces(inp["ada_w"][l][:, j * 512:(j + 1) * 512])
        colp[l, :, 0:24] = inp["ada_b"][l].reshape(24, 128).T
        colp[l, :, 24:32] = inp["norm_pre"][l].reshape(8, 128).T
        colp[l, :, 32:40] = inp["norm_post"][l].reshape(8, 128).T
    wu = np.zeros((depth, 8, 4, 128, 2, 512), f32)
    wo = np.zeros((depth, 2, 4, 128, 2, 512), f32)
    wf = np.zeros((depth, 128, KC, 8), f32)
    bfc = np.zeros((depth, 8, 1), f32)
    lamrep = np.zeros((depth, 128, 256), f32)
    sublnc = np.zeros((depth, 128, 1), f32)
    lamc = np.zeros((depth, 128, 2), f32)
    for l in range(depth):
        if l % 2 == 0:
            i = l // 2
            W = inp["ev_w_in"][i]
            for u in range(8):
                if u < 4:
                    offs = [128 * u, 512 + 128 * u, 1024 + 128 * u, 1544 + 128 * u]
                else:
                    d = u - 4
                    offs = [2056 + 128 * d, 2568 + 128 * d, 3080 + 128 * d, 3592 + 128 * d]
                w512 = np.concatenate([W[:, o:o + 128] for o in offs], axis=1)
                wu[l, u] = _pieces(w512)
            wf[l] = W[:, 1536:1544].reshape(KC, 128, 8).transpose(1, 0, 2)
            bfc[l, :, 0] = inp["ev_b_forget"][i]
            lamrep[l] = np.broadcast_to(np.concatenate([inp["ev_lambda_q1"][i], inp["ev_lambda_k1"][i],
                                                        inp["ev_lambda_q2"][i], inp["ev_lambda_k2"][i]])[None, :], (128, 256))
            sublnc[l, :, 0] = inp["ev_subln"][i]
            li0 = lam_init_of(l)
            lamc[l, :, 0] = -li0
            lamc[l, :, 1] = 1.0 - li0
            Wo = inp["ev_w_out"][i]
        else:
            j = l // 2
            W = inp["od_w_in"][j]
            for u in range(8):
                offs = [128 * u, 1024 + 128 * u, 2048 + 128 * u, 3072 + 128 * u]
                w512 = np.concatenate([W[:, o:o + 128] for o in offs], axis=1)
                wu[l, u] = _pieces(w512)
            Wo = inp["od_w_out"][j]
        for half in range(2):
            wo[l, half] = _pieces(Wo[:, half * 512:(half + 1) * 512])
    inv_freq = (1.0 / (THETA ** (np.arange(0, 16, 2, dtype=np.float32) / np.float32(16)))).astype(np.float32)
    cst = np.zeros((128, 32), f32)
    cst[:, 0:8] = inv_freq / (2.0 * np.pi)
    cst[:, 8:16] = inv_freq / (2.0 * np.pi)
    cst[:, 24:32] = 0.25
    k = np.arange(128)[:, None]
    q = np.arange(2048)[None, :]
    dlt = q - k
    cmv = ((dlt >= 0) & (dlt <= 128)).astype(np.float32) + ((dlt >= 0) & (dlt % 4 == 0) & (dlt <= 512)).astype(np.float32) \
        + ((dlt >= 0) & (dlt % 16 == 0)).astype(np.float32)
    cmm = cmv.astype(ml_dtypes.bfloat16)
    return dict(adaw=adaw, colp=colp, wu=wu, wo=wo, wf=wf, bfc=bfc, lamrep=lamrep, sublnc=sublnc, lamc=lamc, cst=cst, cm=cmm)


_LAYER_KEYS = ["adaw", "colp", "wu", "wo", "wf", "bfc", "lamrep", "sublnc", "lamc"]
_PROG_CACHE = {}


def _get_prog(kinds):
    key = tuple(kinds)
    if key not in _PROG_CACHE:
        _PROG_CACHE[key] = build(list(kinds))
    return _PROG_CACHE[key]


LAUNCH_GROUPS = [[0, 1, 2, 3]]


def kernel(**inputs):
    inp = {k: np.asarray(v) for k, v in inputs.items()}
    sh = _prep_shared(inp)
    x = np.ascontiguousarray(inp["x"].astype(np.float32, copy=False))
    pos = inp["positions"].astype(np.int32)
    c = inp["c"].astype(np.float32)
    cur = [x[b] for b in range(N_CORES)]
    for grp in LAUNCH_GROUPS:
        kinds = ["e" if l % 2 == 0 else "o" for l in grp]
        nc = _get_prog(kinds)
        in_maps = []
        for b in range(N_CORES):
            m = {"x": np.ascontiguousarray(cur[b]),
                 "pos": np.ascontiguousarray(pos[b].reshape(NB, 128).T),
                 "ccol": np.ascontiguousarray(c[b].reshape(KC, 128).T),
                 "cst": sh["cst"], "cm": sh["cm"]}
            for kk in _LAYER_KEYS:
                m[kk] = np.ascontiguousarray(sh[kk][grp[0]:grp[-1] + 1])
            in_maps.append(m)
        res = run_bass_kernel_spmd(nc, in_maps, core_ids=list(range(N_CORES)))
        cur = [np.asarray(res.results[b]["out"]) for b in range(N_CORES)]
    return np.stack(cur, axis=0).astype(np.float32)
```

```python
import math
import os
from contextlib import ExitStack

import numpy as np
import ml_dtypes
import concourse.bass as bass
import concourse.mybir as mybir
from concourse.bass_utils import run_bass_kernel_spmd

F32 = mybir.dt.float32
BF16 = mybir.dt.bfloat16
I32 = mybir.dt.int32
AF = mybir.ActivationFunctionType
ALU = mybir.AluOpType
AX = mybir.AxisListType

S_LEN = 2048
D = 1024
NB = 16
KC = 8
EPS = 1e-6
N_CORES = 8
THETA = 500000.0


ALL_BUFS = []


class Buf:
    __slots__ = ("w", "r")

    def __init__(self):
        self.w = None
        self.r = {}
        ALL_BUFS.append(self)


class Sched:
    def __init__(self, nc, stack, n_dma_sems=16):
        self.nc = nc
        self.eng = {"pe": nc.tensor, "dve": nc.vector, "act": nc.scalar, "pool": nc.gpsimd, "sp": nc.sync}
        self.sem = {k: stack.enter_context(nc.semaphore("s_" + k)) for k in self.eng}
        self.dsem = [stack.enter_context(nc.semaphore("d%d" % i)) for i in range(n_dma_sems)]
        self.semobj = {}
        for k in self.eng:
            self.semobj[("e", k)] = self.sem[k]
        for i in range(n_dma_sems):
            self.semobj[("d", i)] = self.dsem[i]
        self.plan = True
        self.targets = {("e", k): set() for k in self.eng}
        self.rank = {}
        self.reset()

    def reset(self):
        self.cnt = {k: 0 for k in self.eng}
        self.seen = {k: {} for k in self.eng}
        self.dcnt = [0] * len(self.dsem)
        self.dnext = 0
        for b in ALL_BUFS:
            b.w = None
            b.r = {}

    def finish_plan(self):
        self.plan = False
        for k, tg in self.targets.items():
            self.rank[k] = {v: i + 1 for i, v in enumerate(sorted(tg))}
        self.reset()

    def _waits(self, e, reads, writes, skip_self=False):
        need = {}

        def add(k, v):
            if k == ("e", "pe") and e == "pe":
                return
            if skip_self and k == ("e", e):
                return
            if need.get(k, 0) < v:
                need[k] = v

        for b in reads:
            if b.w is not None:
                add(*b.w)
        for b in writes:
            if b.w is not None:
                add(*b.w)
            for k, v in b.r.items():
                add(k, v)
        h = self.eng[e]
        seen = self.seen[e]
        for k, v in need.items():
            if seen.get(k, 0) >= v:
                continue
            seen[k] = v
            if k[0] == "e":
                if self.plan:
                    self.targets[k].add(v)
                else:
                    h.wait_ge(self.semobj[k], self.rank[k][v])
            elif not self.plan:
                h.wait_ge(self.semobj[k], v)

    def _commit(self, t, reads, writes):
        k, v = t
        for b in reads:
            if b.r.get(k, 0) < v:
                b.r[k] = v
        for b in writes:
            b.w = t
            b.r = {}

    def op(self, e, fn, reads=(), writes=(), skip_self=False):
        self._waits(e, reads, writes, skip_self)
        self.cnt[e] += 1
        k = ("e", e)
        if not self.plan:
            ins = fn(self.eng[e])
            if self.cnt[e] in self.rank[k]:
                ins.then_inc(self.sem[e], 1)
        t = (k, self.cnt[e])
        self._commit(t, reads, writes)
        return t

    def dma(self, fn, reads=(), writes=(), q="sp"):
        i = self.dnext
        self.dnext = (self.dnext + 1) % len(self.dsem)
        h = self.eng[q]
        k = ("d", i)
        if self.dcnt[i] > 0 and self.seen[q].get(k, 0) < self.dcnt[i]:
            if not self.plan:
                h.wait_ge(self.dsem[i], self.dcnt[i])
            self.seen[q][k] = self.dcnt[i]
        self._waits(q, reads, writes)
        self.dcnt[i] += 16
        if not self.plan:
            ins = fn(h)
            ins.then_inc(self.dsem[i], 16)
        t = (k, self.dcnt[i])
        self._commit(t, reads, writes)
        return t


def lam_init_of(layer_idx):
    return 0.8 - 0.6 * math.exp(-0.3 * layer_idx)


def build(kinds):
    n = len(kinds)
    nc = bass.Bass("TRN2", target_bir_lowering=False)

    def dram(name, shape, dt, kind):
        return nc.dram_tensor(name, shape, dt, kind=kind).ap()

    x_in = dram("x", [S_LEN, D], F32, "ExternalInput")
    pos_in = dram("pos", [128, NB], I32, "ExternalInput")
    ccol_in = dram("ccol", [128, KC], F32, "ExternalInput")
    adaw_in = dram("adaw", [n, 6, 4, 128, 2, 512], F32, "ExternalInput")
    colp_in = dram("colp", [n, 128, 40], F32, "ExternalInput")
    wu_in = dram("wu", [n, 8, 4, 128, 2, 512], F32, "ExternalInput")
    wo_in = dram("wo", [n, 2, 4, 128, 2, 512], F32, "ExternalInput")
    wf_in = dram("wf", [n, 128, KC, 8], F32, "ExternalInput")
    bfc_in = dram("bfc", [n, 8, 1], F32, "ExternalInput")
    lamrep_in = dram("lamrep", [n, 128, 256], F32, "ExternalInput")
    subln_in = dram("sublnc", [n, 128, 1], F32, "ExternalInput")
    lamc_in = dram("lamc", [n, 128, 2], F32, "ExternalInput")
    cst_in = dram("cst", [128, 32], F32, "ExternalInput")
    cm_in = dram("cm", [128, 2048], BF16, "ExternalInput")
    out = dram("out", [S_LEN, D], F32, "ExternalOutput")
    xs = dram("xs", [S_LEN, D], F32, "Internal")
    gsc = dram("gsc", [n, 8, 128], F32, "Internal")

    with ExitStack() as st:
        S = Sched(nc, st)

        def sb(name, shape, dt):
            return st.enter_context(nc.sbuf_tensor("sb_" + name, shape, dt))

        def ps(name, shape, dt):
            return st.enter_context(nc.psum_tensor("ps_" + name, shape, dt))

        hT = sb("hT", [128, KC, S_LEN], BF16)
        bH = [Buf() for _ in range(NB)]
        attnT = sb("attnT", [128, 8, S_LEN], BF16)
        bAT = [[Buf() for _ in range(4)] for _ in range(8)]
        wbf = sb("wbf", [128, 2, KC, 512], BF16)
        bWB = [[Buf() for _ in range(4)] for _ in range(2)]
        wst = sb("wst", [128, 2, 2, 512], F32)
        bWS = [Buf(), Buf()]
        qk = sb("qk", [128, NB, 280], BF16)
        bQK = [Buf() for _ in range(NB)]
        Rf = sb("Rf", [128, NB, 4, 16], F32)
        bRf = [Buf() for _ in range(NB)]
        T4 = sb("T4", [128, 4, S_LEN], BF16)
        bT4 = [[Buf() for _ in range(4)] for _ in range(4)]
        vaug = sb("vaug", [128, NB, 256], BF16)
        vd = sb("vd", [128, NB, 128], BF16)
        bV = [Buf() for _ in range(NB)]
        sgT = sb("sgT", [128, S_LEN], BF16)
        bSG = [Buf() for _ in range(4)]
        NP = 7
        P = sb("P", [128, NP, 512], BF16)
        bP = [Buf() for _ in range(NP)]
        junk = P[:, 0:2, :].rearrange("p a d -> p (a d)")
        bJ = bP[0]
        bJ2 = bP[1]
        ftmp = sb("ftmp", [128, 4, 512], F32)
        bF = [Buf() for _ in range(4)]
        cm = sb("cm", [128, 2048], BF16)
        bCM = Buf()
        ones128 = sb("ones128", [128, 128], BF16)
        identb = sb("identb", [128, 128], BF16)
        negm = sb("negm", [128, 128], BF16)
        identf = sb("identf", [128, 128], F32)
        bCONST = Buf()
        cst = sb("cst", [128, 32], F32)
        CS = sb("CS", [128, NB, 16], F32)
        SN = sb("SN", [128, NB, 16], F32)
        bROPE = Buf()
        Gbc = sb("Gbc", [128, D], F32)
        bG = Buf()
        ASc = sb("ASc", [128, 2, 16], F32)
        bAS = [Buf(), Buf()]
        modc = sb("modc", [128, 32], F32)
        bMOD = Buf()
        colp = sb("colp", [128, 40], F32)
        bCOLP = Buf()
        grow = sb("grow", [8, 128], F32)
        bGROW = Buf()
        bGSC = [Buf() for _ in range(n)]
        condf = sb("condf", [128, KC], F32)
        condb = sb("condb", [128, KC, 2], BF16)
        bCOND = Buf()
        FQ = sb("FQ", [128, 768], BF16)
        FK = sb("FK", [128, 768], BF16)
        bFQK = Buf()
        cs_tok = sb("cs_tok", [128, 128], F32)
        r_tok = sb("r_tok", [128, 128], F32)
        hml = sb("hml", [128, 3, 128], BF16)
        bSPL = Buf()
        xo = sb("xo", [128, 2, D], F32)
        bXO = [Buf(), Buf()]
        xw = sb("xw", [128, 2, D], F32)
        bXW = [Buf(), Buf()]
        stat = sb("stat", [128, 8, 8], F32)
        bST_ = [Buf() for _ in range(8)]
        lamrep = sb("lamrep", [128, 256], F32)
        lamt = sb("lamt", [128, 8], F32)
        wsub = sb("wsub", [128, 1], F32)
        negb = sb("negb", [8, 1], F32)
        wfst = sb("wfst", [128, KC, 8], F32)
        wfb = sb("wfb", [128, KC, 8], BF16)
        bLAM = Buf()
        bWF = Buf()
        posi = sb("posi", [128, NB], I32)
        posf = sb("posf", [128, NB], F32)
        Yt = sb("Yt", [128, NB, 16], F32)
        Yi = sb("Yi", [128, NB, 16], I32)
        SPT = [ps("SP%d" % i, [128, 2, 512], F32) for i in range(3)]
        PT = [SPT[0][:, 0, :], SPT[0][:, 1, :]]
        bPT = [Buf(), Buf()]
        NST = 4
        NACC = 2
        ST = [SPT[1][:, 0, :], SPT[1][:, 1, :], SPT[2][:, 0, :], SPT[2][:, 1, :]]
        bSTb = [Buf() for _ in range(NST)]
        ACC = [ps("ACC%d" % i, [128, 512], F32) for i in range(NACC)]
        bACC = [Buf() for _ in range(NACC)]
        bXS = [Buf() for _ in range(NB)]

        LA_KIND = {'fox': int(os.environ.get('F_LA', '2')), 'diff': int(os.environ.get('F_LA', '2')), 'dil': int(os.environ.get('F_LA', '2'))}
        cnt = {"wst": 0, "stat": 0, "pt": 0, "st": 0, "p": 0, "f": 0, "acc": 0, "x": 0, "tb": 0}

        def nxt(key, mod):
            v = cnt[key] % mod
            cnt[key] += 1
            return v

        def program():
            for k_ in cnt:
                cnt[k_] = 0
            S.op("dve", lambda v: v.memset(ones128[:], 1.0), writes=[bCONST])
            S.op("dve", lambda v: v.memset(identf[:], 1.0), writes=[bCONST])
            S.op("pool", lambda g: g.affine_select(out=identf[:], in_=identf[:], pattern=[[1, 128]], compare_op=ALU.is_equal,
                                                   fill=0.0, base=0, channel_multiplier=-1), reads=[bCONST], writes=[bCONST])
            S.op("dve", lambda v: v.tensor_copy(out=identb[:], in_=identf[:]), reads=[bCONST], writes=[bCONST])
            S.op("dve", lambda v: v.memset(negm[:], -30000.0), writes=[bCONST])
            S.op("pool", lambda g: g.affine_select(out=negm[:], in_=negm[:], pattern=[[-1, 128]], compare_op=ALU.is_ge,
                                                   fill=0.0, base=-1, channel_multiplier=1), reads=[bCONST], writes=[bCONST])
            S.op("dve", lambda v: v.memset(vaug[:], 1.0), writes=bV)
            S.op("pool", lambda g: g.memset(T4[64:128, 0, :], 0.0), writes=bT4[0])
            S.op("pool", lambda g: g.memset(T4[0:64, 1, :], 0.0), writes=bT4[1])
            S.dma(lambda q: q.dma_start(out=cm[:], in_=cm_in[:, :]), writes=[bCM])
            S.dma(lambda q: q.dma_start(out=cst[:], in_=cst_in[:, :]), writes=[bROPE])
            S.dma(lambda q: q.dma_start(out=posi[:], in_=pos_in[:, :]), writes=[bROPE])
            S.dma(lambda q: q.dma_start(out=condf[:], in_=ccol_in[:, :]), writes=[bCOND])
            S.op("act", lambda a: a.activation(out=condf[:], in_=condf[:], func=AF.Silu), reads=[bCOND], writes=[bCOND])
            S.op("dve", lambda v: v.tensor_copy(out=condb[:, :, 0], in_=condf[:]), reads=[bCOND], writes=[bCOND])
            S.op("dve", lambda v: v.tensor_copy(out=condb[:, :, 1], in_=condf[:]), reads=[bCOND], writes=[bCOND])
            S.op("dve", lambda v: v.tensor_copy(out=posf[:], in_=posi[:]), reads=[bROPE], writes=[bROPE])
            S.op("dve", lambda v: v.tensor_tensor(out=Yt[:], in0=posf[:].unsqueeze(2).to_broadcast([128, NB, 16]),
                                                  in1=cst[:, 0:16].unsqueeze(1).to_broadcast([128, NB, 16]), op=ALU.mult),
                 reads=[bROPE], writes=[bROPE])
            S.op("dve", lambda v: v.tensor_tensor(out=Yt[:], in0=Yt[:], in1=cst[:, 16:32].unsqueeze(1).to_broadcast([128, NB, 16]),
                                                  op=ALU.add), reads=[bROPE], writes=[bROPE])
            S.op("dve", lambda v: v.tensor_copy(out=Yi[:], in_=Yt[:]), reads=[bROPE], writes=[bROPE])
            S.op("dve", lambda v: v.tensor_copy(out=SN[:], in_=Yi[:]), reads=[bROPE], writes=[bROPE])
            S.op("dve", lambda v: v.tensor_tensor(out=Yt[:], in0=Yt[:], in1=SN[:], op=ALU.subtract), reads=[bROPE], writes=[bROPE])
            S.op("act", lambda a: a.activation(out=Yt[:], in_=Yt[:], func=AF.Sin, scale=2.0 * math.pi * (1.0 - 1e-6)),
                 reads=[bROPE], writes=[bROPE])
            S.op("dve", lambda v: v.tensor_copy(out=CS[:, :, 0:8], in_=Yt[:, :, 8:16]), reads=[bROPE], writes=[bROPE])
            S.op("dve", lambda v: v.tensor_copy(out=CS[:, :, 8:16], in_=Yt[:, :, 8:16]), reads=[bROPE], writes=[bROPE])
            S.op("dve", lambda v: v.tensor_scalar(out=SN[:, :, 0:8], in0=Yt[:, :, 0:8], scalar1=-1.0, scalar2=None, op0=ALU.mult),
                 reads=[bROPE], writes=[bROPE])
            S.op("dve", lambda v: v.tensor_copy(out=SN[:, :, 8:16], in_=Yt[:, :, 0:8]), reads=[bROPE], writes=[bROPE])

            def emit_mod_load(li, j, wslot, engs=("pool",)):
                emit_wload(lambda qtr: adaw_in[li, j, qtr], wslot, engs)

            def emit_mod_mm(li, j, wslot):
                if j == 0:
                    S.dma(lambda q: q.dma_start(out=colp[:], in_=colp_in[li]), writes=[bCOLP])
                b = nxt("pt", 2)
                for m in range(4):
                    for kc in range(KC):
                        S.op("pe", lambda pe: pe.matmul(PT[b][:, 2 * m:2 * m + 2], lhsT=wbf[:, wslot, kc, m * 128:(m + 1) * 128],
                                                        rhs=condb[:, kc, :], start=(kc == 0), stop=(kc == 7)),
                             reads=[bWB[wslot][kc // 2], bCOND], writes=[bPT[b]])
                S.op("dve", lambda v: v.tensor_tensor(out=modc[:, 4 * j:4 * j + 4],
                                                      in0=PT[b][:, 0:8].rearrange("p (i two) -> p i two", two=2)[:, :, 0],
                                                      in1=colp[:, 4 * j:4 * j + 4], op=ALU.add),
                     reads=[bPT[b], bCOLP], writes=[bMOD])
                if j == 5:
                    par = li % 2
                    S.op("dve", lambda v: v.scalar_tensor_tensor(out=ASc[:, par, 0:8], in0=modc[:, 8:16], scalar=1.0,
                                                                 in1=colp[:, 24:32], op0=ALU.add, op1=ALU.mult),
                         reads=[bMOD, bCOLP], writes=[bAS[par]])
                    S.op("dve", lambda v: v.tensor_copy(out=ASc[:, par, 8:16], in_=modc[:, 0:8]), reads=[bMOD], writes=[bAS[par]])
                    S.op("dve", lambda v: v.tensor_tensor(out=modc[:, 24:32], in0=modc[:, 16:24], in1=colp[:, 32:40], op=ALU.mult),
                         reads=[bMOD, bCOLP], writes=[bMOD])
                    b2 = nxt("pt", 2)
                    S.op("pe", lambda pe: pe.transpose(out=PT[b2][0:8, 0:128], in_=modc[:, 24:32], identity=identf[:]),
                         reads=[bMOD, bCONST], writes=[bPT[b2]])
                    S.op("dve", lambda v: v.tensor_copy(out=grow[0:8, :], in_=PT[b2][0:8, 0:128]), reads=[bPT[b2]], writes=[bGROW])
                    S.dma(lambda q: q.dma_start(out=gsc[li], in_=grow[0:8, :]), reads=[bGROW], writes=[bGSC[li]])

            def emit_g_bcast(li):
                S.dma(lambda q: q.dma_start(out=Gbc[:, :], in_=gsc[li:li + 1].rearrange("o a b -> o (a b)").to_broadcast([128, D])),
                      reads=[bGSC[li]], writes=[bG])

            def emit_rstd(s, src, dst, count):
                S.op("act", lambda a: a.activation(out=stat[:, s, dst:dst + 1], in_=stat[:, s, src:src + 1], func=AF.Ln,
                                                   scale=1.0 / count, bias=EPS), reads=[bST_[s]], writes=[bST_[s]])
                S.op("act", lambda a: a.activation(out=stat[:, s, dst:dst + 1], in_=stat[:, s, dst:dst + 1], func=AF.Exp,
                                                   scale=-0.5), reads=[bST_[s]], writes=[bST_[s]])

            def emit_h_block(li, blk, xsrc, bsrc, xdst, bdst):
                par = li % 2
                s = nxt("stat", 8)
                S.op("act", lambda a: a.activation(out=junk, in_=xsrc, func=AF.Square, accum_out=stat[:, s, 0:1]),
                     reads=[bsrc], writes=[bJ, bJ2, bST_[s]])
                emit_rstd(s, 0, 1, float(D))
                S.op("dve", lambda v: v.tensor_scalar(out=xdst, in0=xsrc, scalar1=stat[:, s, 1:2], scalar2=None, op0=ALU.mult),
                     reads=[bsrc, bST_[s]], writes=[bdst])
                for half in range(2):
                    b = nxt("pt", 2)
                    for k4 in range(4):
                        kc = half * 4 + k4
                        S.op("pe", lambda pe: pe.transpose(out=PT[b][:, k4 * 128:(k4 + 1) * 128], in_=xdst[:, kc * 128:(kc + 1) * 128],
                                                           identity=identf[:]), reads=[bdst, bCONST], writes=[bPT[b]])
                    for k4 in range(4):
                        kc = half * 4 + k4
                        if kc % 2 == 0:
                            S.op("act", lambda a: a.activation(out=hT[:, kc, blk * 128:(blk + 1) * 128], in_=PT[b][:, k4 * 128:(k4 + 1) * 128],
                                                               func=AF.Identity, scale=ASc[:, par, kc:kc + 1], bias=ASc[:, par, 8 + kc:9 + kc]),
                                 reads=[bPT[b], bAS[par]], writes=[bH[blk]])
                        else:
                            S.op("dve", lambda v: v.tensor_scalar(out=hT[:, kc, blk * 128:(blk + 1) * 128], in0=PT[b][:, k4 * 128:(k4 + 1) * 128],
                                                                  scalar1=ASc[:, par, kc:kc + 1], scalar2=ASc[:, par, 8 + kc:9 + kc],
                                                                  op0=ALU.mult, op1=ALU.add),
                                 reads=[bPT[b], bAS[par]], writes=[bH[blk]])

            def emit_h_block_bf(li, blk, xsl):
                par = li % 2
                s = nxt("stat", 8)
                xsrc = xw[:, xsl, :]
                S.op("act", lambda a: a.activation(out=junk, in_=xsrc, func=AF.Square, accum_out=stat[:, s, 0:1]),
                     reads=[bXW[xsl]], writes=[bJ, bJ2, bST_[s]])
                emit_rstd(s, 0, 1, float(D))
                xnb = P[:, 2 + 2 * xsl:4 + 2 * xsl, :].rearrange("p a d -> p (a d)")
                bxn = [bP[2 + 2 * xsl], bP[3 + 2 * xsl]]
                S.op("dve", lambda v: v.tensor_scalar(out=xnb, in0=xsrc, scalar1=stat[:, s, 1:2], scalar2=None, op0=ALU.mult),
                     reads=[bXW[xsl], bST_[s]], writes=bxn)
                b = nxt("pt", 2)
                ptb = PT[b].bitcast(BF16)
                for kc in range(KC):
                    S.op("pe", lambda pe: pe.transpose(out=ptb[:, kc * 128:(kc + 1) * 128], in_=xnb[:, kc * 128:(kc + 1) * 128],
                                                       identity=identb[:]), reads=bxn + [bCONST], writes=[bPT[b]])
                for kc in range(KC):
                    if kc % 2 == 0:
                        S.op("act", lambda a: a.activation(out=hT[:, kc, blk * 128:(blk + 1) * 128], in_=ptb[:, kc * 128:(kc + 1) * 128],
                                                           func=AF.Identity, scale=ASc[:, par, kc:kc + 1], bias=ASc[:, par, 8 + kc:9 + kc]),
                             reads=[bPT[b], bAS[par]], writes=[bH[blk]])
                    else:
                        S.op("dve", lambda v: v.tensor_scalar(out=hT[:, kc, blk * 128:(blk + 1) * 128], in0=ptb[:, kc * 128:(kc + 1) * 128],
                                                              scalar1=ASc[:, par, kc:kc + 1], scalar2=ASc[:, par, 8 + kc:9 + kc],
                                                              op0=ALU.mult, op1=ALU.add),
                             reads=[bPT[b], bAS[par]], writes=[bH[blk]])

            def emit_wload(src_fn, wslot, engs=("pool",)):
                for qtr in range(4):
                    sl = nxt("wst", 2)
                    S.dma(lambda q: q.dma_start(out=wst[:, sl], in_=src_fn(qtr)), writes=[bWS[sl]])
                    eng = engs[qtr % len(engs)]
                    if eng == "act":
                        S.op("act", lambda a: a.activation(out=wbf[:, wslot, 2 * qtr:2 * qtr + 2, :], in_=wst[:, sl], func=AF.Copy),
                             reads=[bWS[sl]], writes=[bWB[wslot][qtr]])
                    else:
                        S.op(eng, lambda g: g.tensor_copy(out=wbf[:, wslot, 2 * qtr:2 * qtr + 2, :], in_=wst[:, sl]),
                             reads=[bWS[sl]], writes=[bWB[wslot][qtr]])

            def emit_attention(groups, LA):
                SLOT = [(PT[0], PT[1]), (ST[0], ST[1]), (ST[2], ST[3])]
                bSLOT = [(bPT[0], bPT[1]), (bSTb[0], bSTb[1]), (bSTb[2], bSTb[3])]
                flat = []
                for g in groups:
                    c = g["c"]
                    jl = 4 * c + 3
                    for j in range(jl + 1):
                        flat.append((g, j, jl))
                pairs = [flat[i:i + 2] for i in range(0, len(flat), 2)]
                pend = []

                def emit_pv_pair(items, pbufs):
                    for (g, j, jl, n0, p) in items:
                        S.op("pe", lambda pe: pe.matmul(ACC[g["accO"]][:, n0:512], lhsT=g["vfun"](j), rhs=P[:, p, n0:512],
                                                        start=(j == 0), stop=(j == jl)),
                             reads=pbufs + [bV[j]], writes=[bACC[g["accO"]]])
                        if g["accD"] is not None:
                            S.op("pe", lambda pe: pe.matmul(ACC[g["accD"]][:, n0:512], lhsT=ones128[:, :], rhs=P[:, p, n0:512],
                                                            start=(j == 0), stop=(j == jl)),
                                 reads=pbufs + [bCONST], writes=[bACC[g["accD"]]])
                        if j == jl:
                            g["fin"](g)

                for pr in pairs:
                    k = nxt("st", 3)
                    m = nxt("p", 3)
                    sbufs = list(bSLOT[k][:len(pr)])
                    pbufs = [bP[2 * m + t] for t in range(len(pr))]
                    items = []
                    for t, (g, j, jl) in enumerate(pr):
                        c = g["c"]
                        n0 = max(0, j - 4 * c) * 128
                        qi, ki, r0, r1 = g["qk"]
                        bank = SLOT[k][t]
                        diag = (g["mask"] == "causal") and (j >= 4 * c)
                        S.op("pe", lambda pe: pe.matmul(bank[:, n0:512], lhsT=T4[r0:r1, ki, j * 128:(j + 1) * 128],
                                                        rhs=T4[r0:r1, qi, c * 512 + n0:(c + 1) * 512], start=True, stop=(not diag)),
                             reads=[bT4[ki][j // 4], bT4[qi][c]], writes=[bSLOT[k][t]])
                        if diag:
                            S.op("pe", lambda pe: pe.matmul(bank[:, n0:n0 + 128], lhsT=identb[:, :], rhs=negm[:, :], start=False, stop=True),
                                 reads=[bCONST], writes=[bSLOT[k][t]])
                        items.append((g, j, jl, n0, 2 * m + t))
                    if len(items) == 2 and items[0][3] == items[1][3]:
                        n0 = items[0][3]
                        g = items[0][0]
                        S.op("act", lambda a: a.activation(out=P[:, 2 * m:2 * m + 2, n0:512], in_=SPT[k][:, :, n0:512], func=AF.Exp,
                                                           scale=g["scale"]),
                             reads=sbufs, writes=[bP[2 * m], bP[2 * m + 1]])
                    else:
                        for t, (g, j, jl, n0, p) in enumerate(items):
                            bank = SLOT[k][t]
                            S.op("act", lambda a: a.activation(out=P[:, p, n0:512], in_=bank[:, n0:512], func=AF.Exp, scale=g["scale"]),
                                 reads=sbufs, writes=[bP[p]], skip_self=(t > 0))
                    for t, (g, j, jl, n0, p) in enumerate(items):
                        c = g["c"]
                        if g["mask"] != "causal":
                            m0 = max(4 * c - j, 0)
                            S.op("dve", lambda v: v.tensor_tensor(out=P[:, p, n0:512], in0=P[:, p, n0:512],
                                                                  in1=cm[:, m0 * 128:m0 * 128 + (512 - n0)], op=ALU.mult),
                                 reads=pbufs + [bCM], writes=[bP[p]], skip_self=(t > 0))
                    pend.append((items, pbufs))
                    if len(pend) > LA:
                        emit_pv_pair(*pend.pop(0))
                while pend:
                    emit_pv_pair(*pend.pop(0))

            def emit_unit(li, u, kind, next_w, mid_hook=None):
                wslot = u % 2
                if kind == "diff" and u == 4:
                    S.op("pool", lambda g: g.memset(T4[64:128, 0, :], 0.0), writes=bT4[0])
                    S.op("pool", lambda g: g.memset(T4[0:64, 1, :], 0.0), writes=bT4[1])
                if next_w is not None:
                    next_w()
                if kind == "fox":
                    pr = u
                    fqv = FQ[:, :].rearrange("p (b h r) -> p b h r", b=NB, h=8)
                    fkv = FK[:, :].rearrange("p (b h r) -> p b h r", b=NB, h=8)
                    qk4 = qk[:, :, :].rearrange("p b (g e) -> p b g e", g=4)
                    S.op("pool", lambda gp: gp.tensor_copy(out=qk4[:, :, 0:2, 64:70], in_=fqv[:, :, 2 * pr:2 * pr + 2, :]),
                         reads=[bFQK], writes=bQK)
                    S.op("pool", lambda gp: gp.tensor_copy(out=qk4[:, :, 2:4, 64:70], in_=fkv[:, :, 2 * pr:2 * pr + 2, :]),
                         reads=[bFQK], writes=bQK)
                for blk in range(NB):
                    b = nxt("pt", 2)
                    for kc in range(KC):
                        S.op("pe", lambda pe: pe.matmul(PT[b][:, 0:384], lhsT=hT[:, kc, blk * 128:(blk + 1) * 128],
                                                        rhs=wbf[:, wslot, kc, 0:384], start=(kc == 0), stop=(kc == 7)),
                             reads=[bH[blk], bWB[wslot][kc // 2]], writes=[bPT[b]])
                    srcq = PT[b][:, 0:128].rearrange("p (h d) -> p h d", h=2)
                    srck = PT[b][:, 128:256].rearrange("p (h d) -> p h d", h=2)
                    srcv = PT[b][:, 256:384].rearrange("p (h d) -> p h d", h=2)
                    if kind == "fox":
                        qv = qk[:, blk, :].rearrange("p (g e) -> p g e", g=4)
                        S.op("act", lambda a: a.activation(out=qv[:, 0:2, 0:64], in_=srcq, func=AF.Copy, scale=0.125),
                             reads=[bPT[b]], writes=[bQK[blk]])
                        S.op("dve", lambda v: v.tensor_copy(out=qv[:, 2:4, 0:64], in_=srck), reads=[bPT[b]], writes=[bQK[blk]])
                    else:
                        qv = qk[:, blk, 0:256].rearrange("p (g d) -> p g d", g=4)
                        S.op("act", lambda a: a.activation(out=qv[:, 0:2, :], in_=srcq, func=AF.Copy),
                             reads=[bPT[b]], writes=[bQK[blk]])
                        S.op("dve", lambda v: v.tensor_copy(out=qv[:, 2:4, :], in_=srck), reads=[bPT[b]], writes=[bQK[blk]])
                        S.op("dve", lambda v: v.tensor_copy(out=Rf[:, blk], in_=PT[b][:, 0:256].rearrange("p (g d) -> p g d", g=4)[:, :, 0:16]),
                             reads=[bPT[b]], writes=[bRf[blk]])
                    if kind == "diff":
                        S.op("act", lambda a: a.activation(out=vd[:, blk, :], in_=PT[b][:, 256:384], func=AF.Copy),
                             reads=[bPT[b]], writes=[bV[blk]])
                    else:
                        va4 = vaug[:, blk, :].rearrange("p (g d) -> p g d", g=4)
                        S.op("dve", lambda v: v.tensor_copy(out=va4[:, 0, :], in_=srcv[:, 0, :]), reads=[bPT[b]], writes=[bV[blk]])
                        S.op("act", lambda a: a.activation(out=va4[:, 3, :], in_=srcv[:, 1, :], func=AF.Copy), reads=[bPT[b]], writes=[bV[blk]])
                for c in range(4):
                    b = nxt("pt", 2)
                    for kc in range(KC):
                        S.op("pe", lambda pe: pe.matmul(PT[b][:, :], lhsT=wbf[:, wslot, kc, 384:512], rhs=hT[:, kc, c * 512:(c + 1) * 512],
                                                        start=(kc == 0), stop=(kc == 7)),
                             reads=bH[4 * c:4 * c + 4] + [bWB[wslot][kc // 2]], writes=[bPT[b]])
                    S.op("act", lambda a: a.activation(out=sgT[:, c * 512:(c + 1) * 512], in_=PT[b][:, :], func=AF.Silu),
                         reads=[bPT[b]], writes=[bSG[c]])
                if kind != "fox":
                    R2 = ftmp[:, 0:2, :].rearrange("p a (b g d) -> p (a b) g d", g=4, d=16)
                    bR2 = bF[0]
                    bR2b = bF[1]
                    S.op("dve", lambda v: v.tensor_tensor(out=R2[:, :, :, 0:8], in0=Rf[:, :, :, 8:16],
                                                          in1=SN[:, :, 0:8].unsqueeze(2).to_broadcast([128, NB, 4, 8]), op=ALU.mult),
                         reads=bRf + [bROPE], writes=[bR2, bR2b])
                    S.op("dve", lambda v: v.tensor_tensor(out=R2[:, :, :, 8:16], in0=Rf[:, :, :, 0:8],
                                                          in1=SN[:, :, 8:16].unsqueeze(2).to_broadcast([128, NB, 4, 8]), op=ALU.mult),
                         reads=bRf + [bROPE], writes=[bR2, bR2b])
                    S.op("dve", lambda v: v.tensor_tensor(out=Rf[:, :, :, :], in0=Rf[:, :, :, :],
                                                          in1=CS[:, :, :].unsqueeze(2).to_broadcast([128, NB, 4, 16]), op=ALU.mult),
                         reads=bRf + [bROPE, bR2, bR2b], writes=bRf)
                    qkr = qk[:, :, 0:256].rearrange("p b (g d) -> p b g d", g=4)
                    S.op("dve", lambda v: v.tensor_tensor(out=qkr[:, :, :, 0:16], in0=Rf[:, :, :, :], in1=R2[:, :, :, :], op=ALU.add),
                         reads=bRf + [bR2, bR2b], writes=bQK)
                TB = [PT[0], PT[1], ST[0], ST[1], ST[2], ST[3]]
                bTB = [bPT[0], bPT[1], bSTb[0], bSTb[1], bSTb[2], bSTb[3]]
                if kind == "fox":
                    for i in range(4):
                        for c in range(4):
                            b = nxt("tb", 6)
                            ptb = TB[b][:, :].bitcast(BF16)
                            for b4 in range(4):
                                blk = c * 4 + b4
                                S.op("pe", lambda pe: pe.transpose(out=ptb[0:70, b4 * 128:(b4 + 1) * 128], in_=qk[:, blk, i * 70:(i + 1) * 70],
                                                                   identity=identb[:]), reads=[bQK[blk], bCONST], writes=[bTB[b]])
                            eng = "act" if (i + c) % 2 == 0 else "dve"
                            if eng == "act":
                                S.op("act", lambda a: a.activation(out=T4[0:70, i, c * 512:(c + 1) * 512], in_=ptb[0:70, 0:512], func=AF.Copy),
                                     reads=[bTB[b]], writes=[bT4[i][c]])
                            else:
                                S.op("dve", lambda v: v.tensor_copy(out=T4[0:70, i, c * 512:(c + 1) * 512], in_=ptb[0:70, 0:512]),
                                     reads=[bTB[b]], writes=[bT4[i][c]])
                else:
                    for i in (0, 2):
                        for c in range(4):
                            b = nxt("tb", 6)
                            ptb = TB[b][:, :].bitcast(BF16)
                            for b4 in range(4):
                                blk = c * 4 + b4
                                S.op("pe", lambda pe: pe.transpose(out=ptb[:, b4 * 128:(b4 + 1) * 128], in_=qk[:, blk, i * 64:i * 64 + 128],
                                                                   identity=identb[:]), reads=[bQK[blk], bCONST], writes=[bTB[b]])
                            if i == 0:
                                S.op("act", lambda a: a.activation(out=T4[0:64, 0, c * 512:(c + 1) * 512], in_=ptb[0:64, 0:512], func=AF.Copy),
                                     reads=[bTB[b]], writes=[bT4[0][c]])
                                S.op("dve", lambda v: v.tensor_copy(out=T4[64:128, 1, c * 512:(c + 1) * 512], in_=ptb[64:128, 0:512]),
                                     reads=[bTB[b]], writes=[bT4[1][c]])
                            elif c % 2 == 0:
                                S.op("act", lambda a: a.activation(out=T4[:, i, c * 512:(c + 1) * 512], in_=ptb[:, 0:512], func=AF.Copy),
                                     reads=[bTB[b]], writes=[bT4[i][c]])
                            else:
                                S.op("dve", lambda v: v.tensor_copy(out=T4[:, i, c * 512:(c + 1) * 512], in_=ptb[:, 0:512]),
                                     reads=[bTB[b]], writes=[bT4[i][c]])
                if mid_hook is not None:
                    mid_hook()
                groups = []
                if kind in ("fox", "dil"):
                    def fin_pair(g):
                        hh = g["hh"]
                        c = g["c"]
                        a = g["accO"]
                        r0, r1 = (0, 64) if hh == 0 else (64, 128)
                        d0, d1 = (64, 128) if hh == 0 else (0, 64)
                        f = nxt("f", 4)
                        S.op("act", lambda a_: a_.activation(out=ftmp[r0:r1, f, :], in_=ACC[a][d0:d1, :], func=AF.Ln), reads=[bACC[a]], writes=[bF[f]])
                        S.op("act", lambda a_: a_.activation(out=ftmp[r0:r1, f, :], in_=ftmp[r0:r1, f, :], func=AF.Exp, scale=-1.0), reads=[bF[f]], writes=[bF[f]])
                        S.op("pool", lambda gp: gp.tensor_tensor(out=ftmp[r0:r1, f, :], in0=ftmp[r0:r1, f, :],
                                                                 in1=sgT[r0:r1, c * 512:(c + 1) * 512], op=ALU.mult),
                             reads=[bF[f], bSG[c]], writes=[bF[f]])
                        S.op("dve", lambda v: v.tensor_tensor(out=attnT[r0:r1, u, c * 512:(c + 1) * 512], in0=ACC[a][r0:r1, :],
                                                              in1=ftmp[r0:r1, f, :], op=ALU.mult),
                             reads=[bACC[a], bF[f]], writes=[bAT[u][c]])

                    for c in range(4):
                        for hh in range(2):
                            if kind == "fox":
                                qkspec = (hh, 2 + hh, 0, 70)
                            else:
                                qkspec = (hh, 2, 0, 128)
                            groups.append(dict(c=c, hh=hh, qk=qkspec, scale=(1.0 if kind == "fox" else 0.125),
                                               mask=("causal" if kind == "fox" else "cm"),
                                               vfun=(lambda j, hh=hh: vaug[:, j, hh * 128:(hh + 1) * 128]),
                                               accO=nxt("acc", 2), accD=None, fin=fin_pair))
                else:
                    state = {}

                    def fin_diff(g):
                        m = g["hh"]
                        c = g["c"]
                        o, d = g["accO"], g["accD"]
                        f = nxt("f", 4)
                        S.op("act", lambda a_: a_.activation(out=ftmp[:, f, :], in_=ACC[d][:, :], func=AF.Ln), reads=[bACC[d]], writes=[bF[f]])
                        S.op("act", lambda a_: a_.activation(out=ftmp[:, f, :], in_=ftmp[:, f, :], func=AF.Exp, scale=-1.0), reads=[bF[f]], writes=[bF[f]])
                        S.op("dve", lambda v: v.tensor_tensor(out=ftmp[:, f, :], in0=ACC[o][:, :], in1=ftmp[:, f, :], op=ALU.mult),
                             reads=[bACC[o], bF[f]], writes=[bF[f]])
                        if m == 0:
                            state["f1"] = f
                            return
                        f1 = state["f1"]
                        S.op("dve", lambda v: v.scalar_tensor_tensor(out=ftmp[:, f1, :], in0=ftmp[:, f, :], scalar=lamt[:, 4:5],
                                                                     in1=ftmp[:, f1, :], op0=ALU.mult, op1=ALU.add),
                             reads=[bF[f], bF[f1], bLAM], writes=[bF[f1]])
                        p = 6
                        S.op("pool", lambda gp: gp.tensor_tensor(out=P[:, p, :], in0=ftmp[:, f1, :], in1=ftmp[:, f1, :], op=ALU.mult),
                             reads=[bF[f1]], writes=[bP[p]])
                        S.op("pe", lambda pe: pe.matmul(ACC[d][:, :], lhsT=ones128[:, :], rhs=P[:, p, :], start=True, stop=True),
                             reads=[bP[p], bCONST], writes=[bACC[d]])
                        S.op("act", lambda a: a.activation(out=ftmp[:, f, :], in_=ACC[d][:, :], func=AF.Ln, scale=1.0 / 128.0, bias=EPS),
                             reads=[bACC[d]], writes=[bF[f]])
                        S.op("act", lambda a: a.activation(out=ftmp[:, f, :], in_=ftmp[:, f, :], func=AF.Exp, scale=-0.5),
                             reads=[bF[f]], writes=[bF[f]])
                        S.op("dve", lambda v: v.tensor_tensor(out=ftmp[:, f1, :], in0=ftmp[:, f1, :], in1=ftmp[:, f, :], op=ALU.mult),
                             reads=[bF[f1], bF[f]], writes=[bF[f1]])
                        S.op("dve", lambda v: v.scalar_tensor_tensor(out=attnT[:, u, c * 512:(c + 1) * 512], in0=ftmp[:, f1, :],
                                                                     scalar=wsub[:, 0:1], in1=sgT[:, c * 512:(c + 1) * 512],
                                                                     op0=ALU.mult, op1=ALU.mult),
                             reads=[bF[f1], bSG[c], bLAM], writes=[bAT[u][c]])

                    for c in range(4):
                        for m in range(2):
                            groups.append(dict(c=c, hh=m, qk=(m, 2, 0, 128), scale=0.125, mask="causal",
                                               vfun=(lambda j: vd[:, j, :]), accO=0, accD=1, fin=fin_diff))
                emit_attention(groups, LA_KIND[kind])

            def emit_even_prep(li):
                S.dma(lambda q: q.dma_start(out=lamrep[:], in_=lamrep_in[li]), writes=[bLAM])
                S.dma(lambda q: q.dma_start(out=lamt[:, 6:8], in_=lamc_in[li]), writes=[bLAM])
                S.dma(lambda q: q.dma_start(out=wsub[:], in_=subln_in[li]), writes=[bLAM])
                S.dma(lambda q: q.dma_start(out=negb[:], in_=bfc_in[li]), writes=[bLAM])
                S.dma(lambda q: q.dma_start(out=wfst[:], in_=wf_in[li]), writes=[bWF])
                S.op("dve", lambda v: v.tensor_copy(out=wfb[:], in_=wfst[:]), reads=[bWF], writes=[bWF])
                lr = lamrep[:, :].rearrange("p (a d) -> p a d", a=4)
                S.op("dve", lambda v: v.tensor_tensor(out=lamrep[:, 0:64], in0=lr[:, 0, :], in1=lr[:, 1, :], op=ALU.mult),
                     reads=[bLAM], writes=[bLAM])
                S.op("dve", lambda v: v.tensor_tensor(out=lamrep[:, 128:192], in0=lr[:, 2, :], in1=lr[:, 3, :], op=ALU.mult),
                     reads=[bLAM], writes=[bLAM])
                S.op("dve", lambda v: v.reduce_sum(out=lamt[:, 0:1], in_=lamrep[:, 0:64], axis=AX.X), reads=[bLAM], writes=[bLAM])
                S.op("dve", lambda v: v.reduce_sum(out=lamt[:, 1:2], in_=lamrep[:, 128:192], axis=AX.X), reads=[bLAM], writes=[bLAM])
                S.op("act", lambda a: a.activation(out=lamt[:, 2:4], in_=lamt[:, 0:2], func=AF.Exp), reads=[bLAM], writes=[bLAM])
                S.op("dve", lambda v: v.tensor_tensor(out=lamt[:, 4:5], in0=lamt[:, 3:4], in1=lamt[:, 2:3], op=ALU.subtract),
                     reads=[bLAM], writes=[bLAM])
                S.op("dve", lambda v: v.tensor_tensor(out=lamt[:, 4:5], in0=lamt[:, 4:5], in1=lamt[:, 6:7], op=ALU.add),
                     reads=[bLAM], writes=[bLAM])
                S.op("dve", lambda v: v.tensor_tensor(out=wsub[:], in0=wsub[:], in1=lamt[:, 7:8], op=ALU.mult),
                     reads=[bLAM], writes=[bLAM])
                S.op("dve", lambda v: v.tensor_scalar(out=negb[:], in0=negb[:], scalar1=-1.0, scalar2=None, op0=ALU.mult),
                     reads=[bLAM], writes=[bLAM])
                fza = xo[0:8, :, :].rearrange("p a d -> p (a d)")
                fzb = xw[0:8, :, :].rearrange("p a d -> p (a d)")
                for c in range(4):
                    b = nxt("pt", 2)
                    for kc in range(KC):
                        S.op("pe", lambda pe: pe.matmul(PT[b][0:8, :], lhsT=wfb[:, kc, :], rhs=hT[:, kc, c * 512:(c + 1) * 512],
                                                        start=(kc == 0), stop=(kc == 7)),
                             reads=bH[4 * c:4 * c + 4] + [bWF], writes=[bPT[b]])
                    S.op("act", lambda a: a.activation(out=fza[:, c * 512:(c + 1) * 512], in_=PT[b][0:8, :], func=AF.Exp,
                                                       scale=-1.0, bias=negb[0:8, 0:1]),
                         reads=[bPT[b], bLAM], writes=bXO)
                S.op("act", lambda a: a.activation(out=fza, in_=fza, func=AF.Ln, bias=1.0), reads=bXO, writes=bXO)
                ones8 = P[0:8, 0:4, :].rearrange("p a d -> p (a d)")
                S.op("dve", lambda v: v.memset(ones8, 1.0), writes=bP[0:4])
                S.op("dve", lambda v: v.tensor_tensor_scan(out=fzb, data0=ones8, data1=fza, initial=0.0, op0=ALU.mult, op1=ALU.add),
                     reads=bXO + bP[0:4], writes=bXW)
                b = nxt("pt", 2)
                for blk in range(NB):
                    S.op("pe", lambda pe: pe.transpose(out=PT[b][:, blk * 8:(blk + 1) * 8], in_=fzb[:, blk * 128:(blk + 1) * 128],
                                                       identity=identf[0:8, 0:8]), reads=bXW + [bCONST], writes=[bPT[b]])
                S.op("dve", lambda v: v.tensor_copy(out=cs_tok[:], in_=PT[b][:, 0:128]), reads=[bPT[b]], writes=[bSPL])
                S.op("dve", lambda v: v.tensor_copy(out=hml[:, 0, :], in_=cs_tok[:]), reads=[bSPL], writes=[bSPL])
                S.op("dve", lambda v: v.tensor_tensor(out=r_tok[:], in0=cs_tok[:], in1=hml[:, 0, :], op=ALU.subtract),
                     reads=[bSPL], writes=[bSPL])
                S.op("dve", lambda v: v.tensor_copy(out=hml[:, 1, :], in_=r_tok[:]), reads=[bSPL], writes=[bSPL])
                S.op("dve", lambda v: v.tensor_tensor(out=r_tok[:], in0=r_tok[:], in1=hml[:, 1, :], op=ALU.subtract),
                     reads=[bSPL], writes=[bSPL])
                S.op("dve", lambda v: v.tensor_copy(out=hml[:, 2, :], in_=r_tok[:]), reads=[bSPL], writes=[bSPL])
                S.op("dve", lambda v: v.memset(FQ[:], 1.0), writes=[bFQK])
                S.op("dve", lambda v: v.memset(FK[:], 1.0), writes=[bFQK])
                fq3 = FQ[:, :].rearrange("p (n r) -> p n r", r=6)
                fk3 = FK[:, :].rearrange("p (n r) -> p n r", r=6)
                for r in range(3):
                    S.op("dve", lambda v: v.tensor_scalar(out=fq3[:, :, r], in0=hml[:, r, :], scalar1=-1.0, scalar2=None, op0=ALU.mult),
                         reads=[bSPL], writes=[bFQK])
                    S.op("dve", lambda v: v.tensor_copy(out=fk3[:, :, 3 + r], in_=hml[:, r, :]), reads=[bSPL], writes=[bFQK])


            def emit_out_block(li, blk, x_src, x_dst, bsrc_list, bdst_list, has_next):
                par = li % 2
                xsl = blk % 2
                YB = [ACC[0], ACC[1]] if blk % 2 == 0 else [ST[0], ST[1]]
                bYB = [bACC[0], bACC[1]] if blk % 2 == 0 else [bSTb[0], bSTb[1]]
                for half in range(2):
                    for kc in range(KC):
                        S.op("pe", lambda pe: pe.matmul(YB[half][:, :], lhsT=attnT[:, kc, blk * 128:(blk + 1) * 128],
                                                        rhs=wbf[:, half, kc, :], start=(kc == 0), stop=(kc == 7)),
                             reads=[bAT[kc][blk // 4], bWB[half][kc // 2]], writes=[bYB[half]])
                s = nxt("stat", 8)
                for half in range(2):
                    S.op("act", lambda a: a.activation(out=junk[:, half * 512:(half + 1) * 512], in_=YB[half][:, :], func=AF.Square,
                                                       accum_out=stat[:, s, half:half + 1]),
                         reads=[bYB[half]], writes=[bJ, bJ2, bST_[s]])
                S.op("dve", lambda v: v.tensor_tensor(out=stat[:, s, 2:3], in0=stat[:, s, 0:1], in1=stat[:, s, 1:2], op=ALU.add),
                     reads=[bST_[s]], writes=[bST_[s]])
                emit_rstd(s, 2, 3, float(D))
                for half in range(2):
                    S.op("dve", lambda v: v.scalar_tensor_tensor(out=xw[:, xsl, half * 512:(half + 1) * 512], in0=YB[half][:, :],
                                                                 scalar=stat[:, s, 3:4], in1=Gbc[:, half * 512:(half + 1) * 512],
                                                                 op0=ALU.mult, op1=ALU.mult),
                         reads=[bYB[half], bST_[s], bG], writes=[bXW[xsl]])
                S.op("pool", lambda gp: gp.tensor_tensor(out=xw[:, xsl, :], in0=xw[:, xsl, :], in1=xo[:, xsl, :], op=ALU.add),
                     reads=[bXW[xsl], bXO[xsl]], writes=[bXW[xsl]])
                S.dma(lambda q: q.dma_start(out=x_dst[blk * 128:(blk + 1) * 128, :], in_=xw[:, xsl, :]),
                      reads=[bXW[xsl]], writes=[bdst_list[blk]])

            bXIN = [Buf() for _ in range(NB)]
            bOUT = [Buf() for _ in range(NB)]
            for j in range(6):
                emit_mod_load(0, j, j % 2, ("dve", "act", "dve", "pool"))
                emit_mod_mm(0, j, j % 2)
            emit_g_bcast(0)
            for blk in range(NB):
                sl = blk % 2
                S.dma(lambda q: q.dma_start(out=xw[:, sl, :], in_=x_in[blk * 128:(blk + 1) * 128, :]), writes=[bXW[sl]])
                emit_h_block(0, blk, xw[:, sl, :], bXW[sl], xo[:, sl, :], bXO[sl])

            for li, kd in enumerate(kinds):
                first = li == 0
                last = li == n - 1
                x_src, bsrc = (x_in, bXIN) if first else (xs, bXS)
                x_dst, bdst = (out, bOUT) if last else (xs, bXS)
                if kd == "e":
                    emit_even_prep(li)
                    ukinds = ["fox"] * 4 + ["diff"] * 4
                else:
                    ukinds = ["dil"] * 8
                emit_wload(lambda qtr: wu_in[li, 0, qtr], 0)
                for u in range(8):
                    if u < 7:
                        nw = (lambda u=u: emit_wload(lambda qtr: wu_in[li, u + 1, qtr], (u + 1) % 2))
                    else:
                        nw = (lambda: emit_wload(lambda qtr: wo_in[li, 0, qtr], 0))
                    mid = post = None
                    if (not last) and u < 6:
                        mid = (lambda u=u: emit_mod_load(li + 1, u, u % 2))
                        post = (lambda u=u: emit_mod_mm(li + 1, u, u % 2))
                    emit_unit(li, u, ukinds[u], nw, mid)
                    if post is not None:
                        post()
                emit_wload(lambda qtr: wo_in[li, 1, qtr], 1)
                def ld_xo(blk):
                    S.dma(lambda q: q.dma_start(out=xo[:, blk % 2, :], in_=x_src[blk * 128:(blk + 1) * 128, :]),
                          reads=[bsrc[blk]], writes=[bXO[blk % 2]])
                ld_xo(0)
                ld_xo(1)
                emit_out_block(li, 0, x_src, x_dst, bsrc, bdst, False)
                for blk in range(NB):
                    if blk + 1 < NB:
                        emit_out_block(li, blk + 1, x_src, x_dst, bsrc, bdst, False)
                    if blk + 2 < NB:
                        ld_xo(blk + 2)
                    if not last:
                        emit_h_block_bf(li + 1, blk, blk % 2)
                if not last:
                    emit_g_bcast(li + 1)
        program()
        S.finish_plan()
        program()
        for i in range(len(S.dsem)):
            if S.dcnt[i]:
                nc.sync.wait_ge(S.dsem[i], S.dcnt[i])
    return nc


def _pieces(w512):
    return np.ascontiguousarray(w512.reshape(4, 2, 128, 512).transpose(0, 2, 1, 3))


def _prep_shared(inp):
    depth = 4
    f32 = np.float32
    adaw = np.zeros((depth, 6, 4, 128, 2, 512), f32)
    colp = np.zeros((depth, 128, 40), f32)
    for l in range(depth):
        for j in range(6):
            adaw[l, j] = _pieces(inp["ada_w"][l][:, j * 512:(j + 1) * 512])
        colp[l, :, 0:24] = inp["ada_b"][l].reshape(24, 128).T
        colp[l, :, 24:32] = inp["norm_pre"][l].reshape(8, 128).T
        colp[l, :, 32:40] = inp["norm_post"][l].reshape(8, 128).T
    wu = np.zeros((depth, 8, 4, 128, 2, 512), f32)
    wo = np.zeros((depth, 2, 4, 128, 2, 512), f32)
    wf = np.zeros((depth, 128, KC, 8), f32)
    bfc = np.zeros((depth, 8, 1), f32)
    lamrep = np.zeros((depth, 128, 256), f32)
    sublnc = np.zeros((depth, 128, 1), f32)
    lamc = np.zeros((depth, 128, 2), f32)
    for l in range(depth):
        if l % 2 == 0:
            i = l // 2
            W = inp["ev_w_in"][i]
            for u in range(8):
                if u < 4:
                    offs = [128 * u, 512 + 128 * u, 1024 + 128 * u, 1544 + 128 * u]
                else:
                    d = u - 4
                    offs = [2056 + 128 * d, 2568 + 128 * d, 3080 + 128 * d, 3592 + 128 * d]
                w512 = np.concatenate([W[:, o:o + 128] for o in offs], axis=1)
                wu[l, u] = _pieces(w512)
            wf[l] = W[:, 1536:1544].reshape(KC, 128, 8).transpose(1, 0, 2)
            bfc[l, :, 0] = inp["ev_b_forget"][i]
            lamrep[l] = np.broadcast_to(np.concatenate([inp["ev_lambda_q1"][i], inp["ev_lambda_k1"][i],
                                                        inp["ev_lambda_q2"][i], inp["ev_lambda_k2"][i]])[None, :], (128, 256))
            sublnc[l, :, 0] = inp["ev_subln"][i]
            li0 = lam_init_of(l)
            lamc[l, :, 0] = -li0
            lamc[l, :, 1] = 1.0 - li0
            Wo = inp["ev_w_out"][i]
        else:
            j = l // 2
            W = inp["od_w_in"][j]
            for u in range(8):
                offs = [128 * u, 1024 + 128 * u, 2048 + 128 * u, 3072 + 128 * u]
                w512 = np.concatenate([W[:, o:o + 128] for o in offs], axis=1)
                wu[l, u] = _pieces(w512)
            Wo = inp["od_w_out"][j]
        for half in range(2):
            wo[l, half] = _pieces(Wo[:, half * 512:(half + 1) * 512])
    inv_freq = (1.0 / (THETA ** (np.arange(0, 16, 2, dtype=np.float32) / np.float32(16)))).astype(np.float32)
    cst = np.zeros((128, 32), f32)
    cst[:, 0:8] = inv_freq / (2.0 * np.pi)
    cst[:, 8:16] = inv_freq / (2.0 * np.pi)
    cst[:, 24:32] = 0.25
    k = np.arange(128)[:, None]
    q = np.arange(2048)[None, :]
    dlt = q - k
    cmv = ((dlt >= 0) & (dlt <= 128)).astype(np.float32) + ((dlt >= 0) & (dlt % 4 == 0) & (dlt <= 512)).astype(np.float32) \
        + ((dlt >= 0) & (dlt % 16 == 0)).astype(np.float32)
    cmm = cmv.astype(ml_dtypes.bfloat16)
    return dict(adaw=adaw, colp=colp, wu=wu, wo=wo, wf=wf, bfc=bfc, lamrep=lamrep, sublnc=sublnc, lamc=lamc, cst=cst, cm=cmm)


_LAYER_KEYS = ["adaw", "colp", "wu", "wo", "wf", "bfc", "lamrep", "sublnc", "lamc"]
_PROG_CACHE = {}


def _get_prog(kinds):
    key = tuple(kinds)
    if key not in _PROG_CACHE:
        _PROG_CACHE[key] = build(list(kinds))
    return _PROG_CACHE[key]


LAUNCH_GROUPS = [[0, 1, 2, 3]]


def kernel(**inputs):
    inp = {k: np.asarray(v) for k, v in inputs.items()}
    sh = _prep_shared(inp)
    x = np.ascontiguousarray(inp["x"].astype(np.float32, copy=False))
    pos = inp["positions"].astype(np.int32)
    c = inp["c"].astype(np.float32)
    cur = [x[b] for b in range(N_CORES)]
    for grp in LAUNCH_GROUPS:
        kinds = ["e" if l % 2 == 0 else "o" for l in grp]
        nc = _get_prog(kinds)
        in_maps = []
        for b in range(N_CORES):
            m = {"x": np.ascontiguousarray(cur[b]),
                 "pos": np.ascontiguousarray(pos[b].reshape(NB, 128).T),
                 "ccol": np.ascontiguousarray(c[b].reshape(KC, 128).T),
                 "cst": sh["cst"], "cm": sh["cm"]}
            for kk in _LAYER_KEYS:
                m[kk] = np.ascontiguousarray(sh[kk][grp[0]:grp[-1] + 1])
            in_maps.append(m)
        res = run_bass_kernel_spmd(nc, in_maps, core_ids=list(range(N_CORES)))
        cur = [np.asarray(res.results[b]["out"]) for b in range(N_CORES)]
    return np.stack(cur, axis=0).astype(np.float32)
```
